# Optimizing a Trainium2 kernel written in Bass

```python
import jax, jax.numpy as jnp
from jax import lax
import numpy as np

D_MODEL = 2048
BATCH = 4
SEQ = 4096
DEPTH = 2

D_MIX = D_MODEL
N_MIXERS = 4
GROUP_W = D_MIX // N_MIXERS
HEAD_DIM = 128
N_HEADS = GROUP_W // HEAD_DIM
POOL_WINDOWS = (2, 4, 8, 16)
SGU_CHUNK = 128
MOBA_BLOCK = 256
MOBA_TOPK = 3
MOBA_QCHUNK = 64
CONV_WIDTH = 3
D_FF = 11 * D_MODEL // 4
N_IN_COLS = 9 * GROUP_W
FFN_RES = 0.5
EPS = 1e-6

kernel_name = 'hybrid_pool_sgu_moba_conv_macaron'


def rms(x, g):
    xf = x.astype(jnp.float32)
    y = xf * lax.rsqrt(jnp.mean(xf * xf, axis=-1, keepdims=True) + EPS)
    return (y * g.astype(jnp.float32)).astype(x.dtype)


def swiglu(h, w13, w2):
    a, b = jnp.split(h @ w13, 2, axis=-1)
    return (jax.nn.silu(a) * b) @ w2


def pool_mixer(xa, w, scale):
    B, S, _ = xa.shape
    xg_all = xa.reshape(B, S, N_HEADS, HEAD_DIM)
    pos = jnp.arange(S)
    outs = []
    for g, win in enumerate(POOL_WINDOWS):
        xg = xg_all[:, :, g].astype(jnp.float32)
        cs = jnp.cumsum(xg, axis=1)
        lag = jnp.pad(cs[:, :S - win], ((0, 0), (win, 0), (0, 0)))
        cnt = jnp.minimum(pos + 1, win).astype(jnp.float32)[None, :, None]
        outs.append((cs - lag) / cnt - xg)
    d = jnp.stack(outs, axis=2).astype(xa.dtype)
    y = jnp.einsum('bsgc,gcd->bsgd', d, w).reshape(B, S, GROUP_W)
    return y * scale


def sgu_mixer(u, v, w_s, b_s, g_v):
    B, S, _ = u.shape
    u = jax.nn.gelu(u)
    v = jax.nn.gelu(v)
    vh = rms(v.reshape(B, S, N_HEADS, HEAD_DIM), g_v.reshape(N_HEADS, HEAD_DIM))
    vc = vh.reshape(B, S // SGU_CHUNK, SGU_CHUNK, N_HEADS, HEAD_DIM)
    causal = jnp.tril(jnp.ones((SGU_CHUNK, SGU_CHUNK), dtype=w_s.dtype))
    mixed = jnp.einsum('hts,bnshc->bnthc', w_s * causal[None], vc) + b_s.T[None, None, :, :, None]
    return u * mixed.reshape(B, S, GROUP_W)


def moba_mixer(q, k, v, gq, gk):
    B, S, _ = q.shape
    H, D, BLK, QC = N_HEADS, HEAD_DIM, MOBA_BLOCK, MOBA_QCHUNK

    def heads(t):
        return t.reshape(B, S, H, D).transpose(0, 2, 1, 3)

    qh = heads(rms(q.reshape(B, S, H, D), gq).reshape(B, S, GROUP_W)) * jnp.asarray(D ** -0.5, q.dtype)
    kh = heads(rms(k.reshape(B, S, H, D), gk).reshape(B, S, GROUP_W))
    vh = heads(v)
    nb = -(-S // BLK)
    pad = nb * BLK - S
    kb = jnp.pad(kh, ((0, 0), (0, 0), (0, pad), (0, 0))).reshape(B, H, nb, BLK, D)
    vb = jnp.pad(vh, ((0, 0), (0, 0), (0, pad), (0, 0))).reshape(B, H, nb, BLK, D)
    kmean = jnp.mean(kb.astype(jnp.float32), axis=3)
    topk = min(MOBA_TOPK, nb)
    bi = jnp.arange(B)[:, None, None, None]
    hi = jnp.arange(H)[None, :, None, None]
    blk_ids = jnp.arange(nb)
    kpos_local = jnp.arange(BLK)
    qpos_local = jnp.arange(QC)

    def chunk(n):
        q0 = n * QC
        own = q0 // BLK
        qc = lax.dynamic_slice_in_dim(qh, q0, QC, axis=2)
        s_blk = jnp.einsum('bhqd,bhnd->bhqn', qc.astype(jnp.float32), kmean)
        s_blk = jnp.where(blk_ids < own, s_blk, -jnp.inf)
        _, sel = lax.top_k(s_blk, topk)
        sel_ok = sel < own
        ks = kb[bi, hi, sel]
        vs = vb[bi, hi, sel]
        l_sel = jnp.einsum('bhqd,bhqkpd->bhqkp', qc, ks).astype(jnp.float32)
        l_sel = jnp.where(sel_ok[..., None], l_sel, -jnp.inf).reshape(B, H, QC, topk * BLK)
        ko = lax.dynamic_index_in_dim(kb, own, axis=2, keepdims=False)
        vo = lax.dynamic_index_in_dim(vb, own, axis=2, keepdims=False)
        l_own = jnp.einsum('bhqd,bhpd->bhqp', qc, ko).astype(jnp.float32)
        causal = (own * BLK + kpos_local)[None, :] <= (q0 + qpos_local)[:, None]
        l_own = jnp.where(causal, l_own, -jnp.inf)
        p = jax.nn.softmax(jnp.concatenate([l_sel, l_own], axis=-1), axis=-1).astype(vh.dtype)
        p_sel = p[..., :topk * BLK].reshape(B, H, QC, topk, BLK)
        p_own = p[..., topk * BLK:]
        return (jnp.einsum('bhqkp,bhqkpd->bhqd', p_sel, vs)
                + jnp.einsum('bhqp,bhpd->bhqd', p_own, vo))

    o = lax.map(chunk, jnp.arange(S // QC))
    return o.transpose(1, 0, 3, 2, 4).reshape(B, S, GROUP_W)


def conv_mixer(gb, gc, h, w):
    z = gc * h
    y = lax.conv_general_dilated(z, w[:, None, :], window_strides=(1,),
                                 padding=((CONV_WIDTH - 1, 0),),
                                 dimension_numbers=('NWC', 'WIO', 'NWC'),
                                 feature_group_count=GROUP_W)
    return gb * y


def setup_inputs(seed: int = 0) -> dict:
    key = jax.random.key(seed)
    ks = jax.random.split(key, 20)

    def nrm(k, shape, scale):
        return jax.random.normal(k, shape, jnp.float32) * scale

    L = DEPTH
    return {
        'x': nrm(ks[0], (BATCH, SEQ, D_MODEL), 1.0),
        'c': nrm(ks[1], (BATCH, D_MODEL), 1.0),
        'ada_w': nrm(ks[2], (L, D_MODEL, 9 * D_MODEL), 0.5 * D_MODEL ** -0.5),
        'ada_b': nrm(ks[3], (L, 9 * D_MODEL), 0.02),
        'norm_g': 1.0 + nrm(ks[4], (L, 3, D_MODEL), 0.1),
        'ffn1_w13': nrm(ks[5], (L, D_MODEL, 2 * D_FF), D_MODEL ** -0.5),
        'ffn1_w2': nrm(ks[6], (L, D_FF, D_MODEL), D_FF ** -0.5),
        'w_in': nrm(ks[7], (L, D_MODEL, N_IN_COLS), D_MODEL ** -0.5),
        'pool_w': nrm(ks[8], (L, N_HEADS, HEAD_DIM, HEAD_DIM), HEAD_DIM ** -0.5),
        'pool_scale': 1.0 + nrm(ks[9], (L, GROUP_W), 0.1),
        'sgu_w': nrm(ks[10], (L, N_HEADS, SGU_CHUNK, SGU_CHUNK), SGU_CHUNK ** -0.5),
        'sgu_b': 1.0 + nrm(ks[11], (L, N_HEADS, SGU_CHUNK), 0.1),
        'sgu_norm_g': 1.0 + nrm(ks[12], (L, GROUP_W), 0.1),
        'q_norm_g': 1.0 + nrm(ks[13], (L, HEAD_DIM), 0.1),
        'k_norm_g': 1.0 + nrm(ks[14], (L, HEAD_DIM), 0.1),
        'conv_w': nrm(ks[15], (L, CONV_WIDTH, GROUP_W), CONV_WIDTH ** -0.5),
        'out_norm_g': 1.0 + nrm(ks[16], (L, D_MIX), 0.1),
        'w_out': nrm(ks[17], (L, D_MIX, D_MODEL), D_MIX ** -0.5),
        'ffn2_w13': nrm(ks[18], (L, D_MODEL, 2 * D_FF), D_MODEL ** -0.5),
        'ffn2_w2': nrm(ks[19], (L, D_FF, D_MODEL), D_FF ** -0.5),
    }


def reference(x, c, ada_w, ada_b, norm_g, ffn1_w13, ffn1_w2, w_in, pool_w, pool_scale,
              sgu_w, sgu_b, sgu_norm_g, q_norm_g, k_norm_g, conv_w, out_norm_g, w_out,
              ffn2_w13, ffn2_w2):
    B, S, _ = x.shape
    for l in range(DEPTH):
        mod = (jax.nn.silu(c) @ ada_w[l] + ada_b[l]).reshape(B, 3, 3, 1, D_MODEL)

        def mod_norm(h, i):
            return rms(h, norm_g[l, i]) * (1.0 + mod[:, i, 1]) + mod[:, i, 0]

        x = x + FFN_RES * mod[:, 0, 2] * swiglu(mod_norm(x, 0), ffn1_w13[l], ffn1_w2[l])

        p = jnp.split(mod_norm(x, 1) @ w_in[l], 9, axis=-1)
        ya = pool_mixer(p[0], pool_w[l], pool_scale[l])
        yb = sgu_mixer(p[1], p[2], sgu_w[l], sgu_b[l], sgu_norm_g[l])
        yc = moba_mixer(p[3], p[4], p[5], q_norm_g[l], k_norm_g[l])
        yd = conv_mixer(p[6], p[7], p[8], conv_w[l])
        ycat = jnp.stack([ya, yb, yc, yd], axis=2)
        ycat = rms(ycat, out_norm_g[l].reshape(N_MIXERS, GROUP_W)).reshape(B, S, D_MIX)
        x = x + mod[:, 1, 2] * (ycat @ w_out[l])

        x = x + FFN_RES * mod[:, 2, 2] * swiglu(mod_norm(x, 2), ffn2_w13[l], ffn2_w2[l])
    return x
```

```python
import contextlib
import numpy as np
import concourse.bass as bass
import concourse.mybir as mybir
from concourse.bass_utils import run_bass_kernel_spmd

F32 = mybir.dt.float32
BF16 = mybir.dt.bfloat16
ALU = mybir.AluOpType
AF = mybir.ActivationFunctionType
AX = mybir.AxisListType

L = 2
D = 2048
NDC = 16
DFF = 5632
NFC = 44
TL = 2048
TF = 1024
TM = 256
NTM = TL // TM
EPS = 1e-6
BIG = 1.0e30
WSLOT = 5632
NW = 3
ENGS = ('pe', 'act', 'dve', 'pool', 'sp')
DBG_YCAT = False

PP_C = 0
PP_NORMG = PP_C + 16
PP_POOLS = PP_NORMG + L * 3 * 16
PP_QG = PP_POOLS + L * 4
PP_KG = PP_QG + L
PP_CONVW = PP_KG + L
PP_OUTG = PP_CONVW + L * 12
PP_HALF = PP_OUTG + L * 16
PP_ADAB = PP_HALF + 1
PP_TABS = PP_ADAB + L * 144
PP_INVC = PP_TABS + 8 * 3 * 16
NPP = PP_INVC + 64
BC_SGUB = 0
BC_SGUG = BC_SGUB + L * 4 * 128
NBC = BC_SGUG + L * 512


class Sem:
    def __init__(self, h):
        self.h = h
        self.count = 0


class Tok:
    __slots__ = ('eng', 'idx', 'sem', 'val')

    def __init__(self, eng=None, idx=None, sem=None, val=None):
        self.eng, self.idx, self.sem, self.val = eng, idx, sem, val

    def key(self):
        return self.eng if self.sem is None else id(self.sem)

    def order(self):
        return self.idx if self.sem is None else self.val


def _merge(d, t):
    k = t.key()
    o = d.get(k)
    if o is None or o.order() < t.order():
        d[k] = t


class Res:
    def __init__(self):
        self.w = {}
        self.r = {}


class Prog:
    def __init__(self, dry):
        self.dry = dry
        self.q = {e: [] for e in ENGS}
        self.pending_dma = {}

    def op(self, eng, fn, r=(), w=(), pw=(), fw=(), dsem=None, dinc=16, extra=()):
        if self.dry:
            return None
        deps = {}
        for t in extra:
            if t is not None:
                _merge(deps, t)
        for x in r:
            for t in x.w.values():
                _merge(deps, t)
        for x in list(w) + list(pw):
            for t in x.w.values():
                _merge(deps, t)
            for t in x.r.values():
                _merge(deps, t)
        if dsem is not None:
            dsem.count += dinc
            tok = Tok(sem=dsem, val=dsem.count)
            self.pending_dma[id(dsem)] = tok
        else:
            tok = Tok(eng=eng, idx=len(self.q[eng]))
        for x in r:
            _merge(x.r, tok)
        for x in w:
            x.w = {tok.key(): tok}
            x.r = {}
        for x in pw:
            x.w = {}
            x.r = {}
        for x in fw:
            x.w = {tok.key(): tok}
            x.r = {}
        self.q[eng].append((fn, list(deps.values()), tok, dinc))
        return tok


def _mk_program(stage):
    nc = bass.Bass("TRN2", target_bir_lowering=False)
    dr = {}

    def din(name, shape):
        dr[name] = nc.dram_tensor(name, list(shape), F32, kind="ExternalInput").ap()
        return dr[name]

    xT_in = din("xT", [NDC, 128, TL])
    pp_d = din("pp", [128, NPP])
    bc_d = din("bc", [128, NBC])
    cm_d = din("cmat", [128, 128 * 3 + 2 * 256 + 16 * 128])
    poolw_d = din("poolw", [128, L * 4 * 128])
    sguw_d = din("sguwT", [128, L * 4 * 128])
    ada_d = din("adaw", [L, 72, 128, 2 * 2048])
    w13_d = din("w13", [L * 2, NFC, 128, 4096])
    w2_d = din("w2", [L * 2, 16, 128, DFF])
    winf_d = din("winf", [L, 18, 128, 2 * 2048])
    wint_d = din("wint", [L, 2, 2, 128, 8 * 512])
    wout_d = din("wout", [L, 8, 128, 2 * 2048])
    yT_out = nc.dram_tensor("yT", [NDC, 128, TL], F32, kind="ExternalOutput").ap()
    xs = nc.dram_tensor("xs", [NDC, 128, TL], F32).ap()
    gk_i = nc.dram_tensor("gk_i", [512, 2048], BF16).ap()
    gk_o = nc.dram_tensor("gk_o", [1024, 2048], BF16).ap()
    gv_i = nc.dram_tensor("gv_i", [512, 2048], BF16).ap()
    gv_o = nc.dram_tensor("gv_o", [1024, 2048], BF16).ap()
    gh_i = nc.dram_tensor("gh_i", [128, 2048], F32).ap()
    gh_o = nc.dram_tensor("gh_o", [256, 2048], F32).ap()
    wsc = nc.dram_tensor("wsc", [L, 22, 128, 4096], BF16).ap()

    es = contextlib.ExitStack()
    with es:
        def sb(name, shape, dt):
            return es.enter_context(nc.sbuf_tensor("sb_" + name, list(shape), dt))

        def newsem(name):
            return Sem(es.enter_context(nc.semaphore(name)))

        pp = sb("pp", [128, NPP], F32)
        bc = sb("bc", [128, NBC], F32)
        modpp = sb("modpp", [128, L * 144], F32)
        der = sb("der", [128, L * 3 * 16 * 3], F32)
        misc = sb("misc", [128, 64], F32)
        ident_f = sb("ident_f", [128, 128], F32)
        trilT = sb("trilT", [128, 128], F32)
        ident_b = sb("ident_b", [128, 128], BF16)
        causal_b = sb("causal_b", [128, 2, 256], BF16)
        onehot_b = sb("onehot_b", [128, 16, 128], BF16)
        ones_b = sb("ones_b", [128, 4, 128], BF16)
        poolw_b = sb("poolw_b", [128, L * 4, 128], BF16)
        sguw_b = sb("sguw_b", [128, L * 4, 128], BF16)
        siluc_b = sb("siluc_b", [128, 16], BF16)
        wbuf = sb("wbuf", [128, NW, WSLOT], BF16)
        psum = [es.enter_context(nc.psum_tensor(f"ps{k}", [128, 512], F32)) for k in range(8)]

        PH = int(nc.sbuf_bytes_remaining) // 4 - 16
        arena = sb("arena", [128, PH], F32)
        cstage = arena[:, 0:2560]
        sguw_f = arena[:, 2560:2560 + L * 4 * 128].rearrange("p (a b) -> p a b", a=L * 4)
        sem_eng = {e: newsem(f"s_{e}") for e in ENGS}
        wsem = [newsem(f"w{k}") for k in range(NW)]
        dsems = {}

        def dsem(name):
            if name not in dsems:
                dsems[name] = newsem("d_" + name)
            return dsems[name]

        def run(dry, wseq):
            P = Prog(dry)
            R = {}

            def res(name):
                if name not in R:
                    R[name] = Res()
                return R[name]

            bankres = [Res() for _ in range(8)]
            bank_ptr = [0]

            reserved = set()

            def take_banks(n):
                out = []
                while len(out) < n:
                    if bank_ptr[0] not in reserved:
                        out.append(bank_ptr[0])
                    bank_ptr[0] = (bank_ptr[0] + 1) % 8
                return out

            xres = {}

            def xr(buf, dc, slab):
                k = (buf, dc, slab)
                if k not in xres:
                    xres[k] = Res()
                return xres[k]

            wrec = []
            wstate = {'next_issue': 0, 'next_get': 0}
            wslot_res = [Res() for _ in range(NW)]

            def w_issue_upto(n):
                while wstate['next_issue'] < min(n, len(wseq)):
                    i = wstate['next_issue']
                    src, nel, rkey = wseq[i]
                    k = i % NW
                    P.op('pool', lambda e, src=src, k=k, nel=nel: e.dma_start(out=wbuf[:, k, 0:nel], in_=src),
                         r=[res(rkey)] if rkey else [], w=[wslot_res[k]], dsem=wsem[k])
                    wstate['next_issue'] += 1

            def wget(src, nel, rkey=None):
                i = wstate['next_get']
                wstate['next_get'] += 1
                wrec.append((src, nel, rkey))
                wstate['last_k'] = i % NW
                if dry:
                    return wbuf[:, 0, 0:nel], wslot_res[0]
                w_issue_upto(i + NW)
                k = i % NW
                return wbuf[:, k, 0:nel], wslot_res[k]

            def af(off, n):
                return arena[:, off:off + n]

            def ab(off, n):
                return arena[:, off:off + n].bitcast(BF16)

            r_pp, r_bc, r_c = res('pp'), res('bc'), res('consts')
            P.op('sp', lambda e: e.dma_start(out=pp[:], in_=pp_d), w=[r_pp], dsem=dsem('pp'))
            P.op('sp', lambda e: e.dma_start(out=bc[:], in_=bc_d), w=[r_bc], dsem=dsem('bc'))
            r_cst = res('cstage')
            P.op('sp', lambda e: e.dma_start(out=ident_f[:], in_=cm_d[:, 0:128]), w=[res('ident_f')], dsem=dsem('c1'))
            P.op('sp', lambda e: e.dma_start(out=trilT[:], in_=cm_d[:, 128:256]), w=[res('trilT')], dsem=dsem('c2'))
            P.op('sp', lambda e: e.dma_start(out=cstage[:, 0:512], in_=cm_d[:, 384:896]), w=[r_cst], dsem=dsem('c3'))
            P.op('sp', lambda e: e.dma_start(out=cstage[0:16, 512:2560], in_=cm_d[0:16, 896:2944]), w=[res('cst2')], dsem=dsem('c4'))
            P.op('sp', lambda e: e.dma_start(out=arena[:, 2560:2560 + L * 4 * 128], in_=sguw_d), w=[res('sguw_f')], dsem=dsem('c5'))
            P.op('pool', lambda e: e.dma_start(out=poolw_b[:].rearrange("p a b -> p (a b)"), in_=poolw_d), w=[res('poolw_b')], dsem=dsem('c6'))
            P.op('dve', lambda e: e.tensor_copy(out=ident_b[:], in_=ident_f[:]), r=[res('ident_f')], w=[res('ident_b')])
            P.op('dve', lambda e: e.tensor_copy(out=causal_b[:].rearrange("p a b -> p (a b)"), in_=cstage[:, 0:512]), r=[r_cst], w=[res('causal_b')])
            P.op('dve', lambda e: e.memset(onehot_b[:].rearrange("p a b -> p (a b)"), 0.0), w=[res('onehot_b')])
            P.op('dve', lambda e: e.tensor_copy(out=onehot_b[0:16].rearrange("p a b -> p (a b)"), in_=cstage[0:16, 512:2560]), r=[res('cst2'), res('onehot_b')], w=[res('onehot_b')])
            for k, v in enumerate((1.0 / 2048, 1.0 / 128, 1.0 / 512, 1.0)):
                P.op('dve', lambda e, k=k, v=v: e.memset(ones_b[:, k, :], v), w=[res(f'ones{k}')])
            P.op('dve', lambda e: e.tensor_tensor(out=sguw_b[:], in0=sguw_f, in1=trilT[:].unsqueeze(1).to_broadcast([128, L * 4, 128]), op=ALU.mult),
                 r=[res('sguw_f'), res('trilT')], w=[res('sguw_b')])
            P.op('act', lambda e: e.activation(out=siluc_b[:], in_=pp[:, PP_C:PP_C + 16], func=AF.Silu), r=[r_pp], w=[res('siluc')])
            P.op('dve', lambda e: e.tensor_scalar(out=misc[:, 0:L], in0=pp[:, PP_QG:PP_QG + L], scalar1=float(128 ** -0.5), scalar2=None, op0=ALU.mult),
                 r=[r_pp], w=[res('qgs')])

            def ones(k):
                return ones_b[:, k, :], res(f'ones{k}')

            P.op('dve', lambda e: e.memset(misc[:, 8:9], EPS), w=[res('eps')])

            def rsqrt(out_ap, in_ap, in_res, out_res, prescale=1.0):
                P.op('act', lambda e: e.activation(out=out_ap, in_=in_ap, func=AF.Sqrt, bias=misc[0:out_ap.shape[0], 8:9], scale=prescale),
                     r=list(in_res) + [res('eps')], w=[out_res])
                P.op('dve', lambda e: e.reciprocal(out=out_ap, in_=out_ap), r=[out_res], w=[out_res])

            def ada_group(l, grp):
                wt, wr = wget(ada_d[l, grp], 4096)
                wv = wt.rearrange("p (c k n) -> p c k n", c=2, k=16)
                bk = take_banks(1)[0]
                for c2 in range(2):
                    for dc in range(16):
                        P.op('pe', lambda e, dc=dc, wv=wv, c2=c2, bk=bk: e.matmul(psum[bk][:, c2:c2 + 1], lhsT=wv[:, c2, dc, :], rhs=siluc_b[:, dc:dc + 1], start=(dc == 0), stop=(dc == 15)),
                             r=[wr, res('siluc')], pw=[bankres[bk]] if (c2 == 0 and dc == 0) else [], fw=[bankres[bk]] if (c2 == 1 and dc == 15) else [])
                o = l * 144 + grp * 2
                P.op('dve', lambda e, o=o, bk=bk: e.tensor_tensor(out=modpp[:, o:o + 2], in0=psum[bk][:, 0:2], in1=pp[:, PP_ADAB + o:PP_ADAB + o + 2], op=ALU.add),
                     r=[bankres[bk], r_pp], w=[res(f'mod{l}_{grp}')])

            def ada_derive(l, i):
                mo = l * 144
                modres = [res(f'mod{l}_{g}') for g in range(i * 24, (i + 1) * 24)]
                o = (l * 3 + i) * 48
                sh = mo + (i * 3 + 0) * 16
                sc = mo + (i * 3 + 1) * 16
                gt = mo + (i * 3 + 2) * 16
                ng = PP_NORMG + (l * 3 + i) * 16
                P.op('dve', lambda e: e.scalar_tensor_tensor(out=der[:, o:o + 16], in0=modpp[:, sc:sc + 16], scalar=1.0, in1=pp[:, ng:ng + 16], op0=ALU.add, op1=ALU.mult),
                     r=modres + [r_pp], w=[res(f'derA{l}{i}')])
                P.op('dve', lambda e: e.tensor_copy(out=der[:, o + 16:o + 32], in_=modpp[:, sh:sh + 16]),
                     r=modres, w=[res(f'derB{l}{i}')])
                gm = 1.0 if i == 1 else 0.5
                P.op('dve', lambda e: e.tensor_scalar(out=der[:, o + 32:o + 48], in0=modpp[:, gt:gt + 16], scalar1=gm, scalar2=None, op0=ALU.mult),
                     r=modres, w=[res(f'derG{l}{i}')])

            def derA(l, i, dc):
                o = (l * 3 + i) * 48
                return der[:, o + dc:o + dc + 1]

            def derB(l, i, dc):
                o = (l * 3 + i) * 48 + 16
                return der[:, o + dc:o + dc + 1]

            def derG(l, i, dc):
                o = (l * 3 + i) * 48 + 32
                return der[:, o + dc:o + dc + 1]

            def der_res(l, i):
                return [res(f'derA{l}{i}'), res(f'derB{l}{i}'), res(f'derG{l}{i}')]

            def ffn(l, i, widx, src, srcname, dst, dstname, bg=()):
                bg = list(bg)
                XN = 0
                HH = 8192
                XC = HH + 22528
                RS = XC + 3072
                SQ = RS + 1024
                ST = SQ + 1024
                TMP = ST + 1024
                assert TMP + 1024 <= PH, (TMP + 1024, PH)
                xn = ab(XN, 8192).rearrange("p (c t) -> p c t", c=16)
                hh = ab(HH, 22528).rearrange("p (c t) -> p c t", c=NFC)
                xc = [af(XC + k * 1024, 1024) for k in range(3)]
                xc_res = [res(f'xc{k}') for k in range(3)]
                xc_sem = [dsem(f'xc{k}') for k in range(3)]
                rstd = af(RS, 1024)
                sq = [ab(SQ + k * 512, 512) for k in range(2)]
                stt = [af(ST + k * 512, 512) for k in range(2)]
                tmp = [af(TMP + k * 512, 512) for k in range(2)]
                xci = [0]
                dres = der_res(l, i)
                o1, o1r = ones(0)

                def xload(dc, t0):
                    k = xci[0] % 3
                    xci[0] += 1
                    slabs = [xr(srcname, dc, (t0 // TM) + s) for s in range(TF // TM)]
                    P.op('sp', lambda e, k=k, dc=dc, t0=t0: e.dma_start(out=xc[k], in_=src[dc, :, t0:t0 + TF]),
                         r=slabs, w=[xc_res[k]], dsem=xc_sem[k])
                    return k

                def mn_p1_load(dc, t0):
                    k = xload(dc, t0)
                    s_ = dc % 2
                    P.op('act', lambda e, k=k, s_=s_: e.activation(out=sq[s_], in_=xc[k], func=AF.Square),
                         r=[xc_res[k]], w=[res(f'sq{s_}')])

                def mn_p1_mm(dc, sb_):
                    s_ = dc % 2
                    for hf in range(2):
                        P.op('pe', lambda e, s_=s_, hf=hf, dc=dc, b=sb_[hf]: e.matmul(psum[b][:, :], lhsT=o1, rhs=sq[s_][:, hf * 512:(hf + 1) * 512], start=(dc == 0), stop=(dc == 15)),
                             r=[res(f'sq{s_}'), o1r], pw=[bankres[sb_[hf]]] if dc == 0 else [], fw=[bankres[sb_[hf]]] if dc == 15 else [])

                def mn_rsqrt(sb_):
                    for hf in range(2):
                        rsqrt(rstd[:, hf * 512:(hf + 1) * 512], psum[sb_[hf]][:, :], [bankres[sb_[hf]]], res(f'rstd{hf}'))

                def mn_pass2_step(dc, t0):
                    k = xload(dc, t0)
                    for hf in range(2):
                        P.op('dve', lambda e, k=k, hf=hf, dc=dc: e.scalar_tensor_tensor(out=tmp[hf], in0=xc[k][:, hf * 512:(hf + 1) * 512], scalar=derA(l, i, dc), in1=rstd[:, hf * 512:(hf + 1) * 512], op0=ALU.mult, op1=ALU.mult),
                             r=[xc_res[k], res(f'rstd{hf}')] + dres, w=[res(f'tmp{hf}')])
                        P.op('act', lambda e, hf=hf, dc=dc: e.activation(out=xn[:, dc, hf * 512:(hf + 1) * 512], in_=tmp[hf], func=AF.Identity, bias=derB(l, i, dc), scale=1.0),
                             r=[res(f'tmp{hf}')] + dres, w=[res(f'xn{dc}')] if hf == 1 else [], pw=[res(f'xn{dc}')] if hf == 0 else [])

                NT = TL // TF
                for tt in range(NT):
                    t0 = tt * TF
                    if tt == 0:
                        sb_ = take_banks(2)
                        for dc in range(16):
                            mn_p1_load(dc, t0)
                            mn_p1_mm(dc, sb_)
                        mn_rsqrt(sb_)
                        for dc in range(16):
                            mn_pass2_step(dc, t0)
                    for fc in range(NFC):
                        wt, wr = wget(w13_d[widx, fc], 4096)
                        wv = wt.rearrange("p (a k n) -> p a k n", a=2, k=16)
                        bk = take_banks(4)
                        for dc in range(16):
                            for a_ in range(2):
                                for hf in range(2):
                                    b = bk[a_ * 2 + hf]
                                    P.op('pe', lambda e, a_=a_, hf=hf, dc=dc, b=b, wv=wv: e.matmul(psum[b][:, :], lhsT=wv[:, a_, dc, :], rhs=xn[:, dc, hf * 512:(hf + 1) * 512], start=(dc == 0), stop=(dc == 15)),
                                         r=[wr, res(f'xn{dc}')], pw=[bankres[b]] if dc == 0 else [], fw=[bankres[b]] if dc == 15 else [])
                        for hf in range(2):
                            P.op('act', lambda e, hf=hf, b=bk[hf]: e.activation(out=stt[hf], in_=psum[b][:, :], func=AF.Silu),
                                 r=[bankres[bk[hf]]], w=[res(f'stt{hf}')])
                            P.op('dve', lambda e, hf=hf, b=bk[2 + hf], fc=fc: e.tensor_tensor(out=hh[:, fc, hf * 512:(hf + 1) * 512], in0=psum[b][:, :], in1=stt[hf], op=ALU.mult),
                                 r=[bankres[bk[2 + hf]], res(f'stt{hf}')], w=[res(f'hh{fc}_{hf}')])
                        if bg:
                            bg.pop(0)()
                    nxt = tt + 1 < NT
                    if nxt:
                        sbn = take_banks(2)
                        reserved.update(sbn)
                    for dt in range(16):
                        wt, wr = wget(w2_d[widx, dt], DFF)
                        wv = wt.rearrange("p (f n) -> p f n", f=NFC)
                        bk = take_banks(2)
                        k = xload(dt, t0)
                        for fc in range(NFC):
                            for hf in range(2):
                                b = bk[hf]
                                P.op('pe', lambda e, hf=hf, fc=fc, b=b, wv=wv: e.matmul(psum[b][:, :], lhsT=wv[:, fc, :], rhs=hh[:, fc, hf * 512:(hf + 1) * 512], start=(fc == 0), stop=(fc == NFC - 1)),
                                     r=[wr, res(f'hh{fc}_{hf}')], pw=[bankres[b]] if fc == 0 else [], fw=[bankres[b]] if fc == NFC - 1 else [])
                        if nxt and 1 <= dt <= 8:
                            mn_p1_mm(2 * (dt - 1), sbn)
                            mn_p1_mm(2 * (dt - 1) + 1, sbn)
                            if dt == 8:
                                mn_rsqrt(sbn)
                                reserved.difference_update(sbn)
                        for hf in range(2):
                            P.op('dve', lambda e, hf=hf, b=bk[hf], k=k, dt=dt: e.scalar_tensor_tensor(out=xc[k][:, hf * 512:(hf + 1) * 512], in0=psum[b][:, :], scalar=derG(l, i, dt), in1=xc[k][:, hf * 512:(hf + 1) * 512], op0=ALU.mult, op1=ALU.add),
                                 r=[bankres[bk[hf]], xc_res[k]] + dres, fw=[xc_res[k]] if hf == 1 else [])
                        slabs = [xr(dstname, dt, (t0 // TM) + s) for s in range(TF // TM)]
                        P.op('sp', lambda e, k=k, dt=dt, t0=t0: e.dma_start(out=dst[dt, :, t0:t0 + TF], in_=xc[k]),
                             r=[xc_res[k]], w=slabs, dsem=xc_sem[k])
                        if nxt:
                            if dt < 8:
                                mn_p1_load(2 * dt, t0 + TF)
                                mn_p1_load(2 * dt + 1, t0 + TF)
                            else:
                                mn_pass2_step(2 * (dt - 8), t0 + TF)
                                mn_pass2_step(2 * (dt - 8) + 1, t0 + TF)
                while bg:
                    bg.pop(0)()

            def barrier():
                if dry:
                    return
                toks = []
                for e in ENGS:
                    for ent in reversed(P.q[e]):
                        if ent[0] is not None:
                            if ent[2].sem is None:
                                toks.append(ent[2])
                            break
                toks += list(P.pending_dma.values())
                for e in ENGS:
                    P.op(e, None, extra=toks)

            def mixer(l, src, srcname, dst, dstname):
                KL, KP, VL, VP = 0, 4096, 8192, 12288
                XT = 16384
                XN = XT + 4096
                YC = XN + 2048
                RS = YC + 4096
                TP = RS + 1024
                T2 = TP + 768
                O_TQ, O_TRS = T2 + 0, T2 + 128
                O_SM, O_M8, O_SEL, O_SS4, O_JUNK, O_T16, O_TG, O_HST, O_HPV = T2 + 384, T2 + 448, T2 + 480, T2 + 544, T2 + 552, T2 + 680, T2 + 696, T2 + 712, T2 + 840
                O_P0B = T2 + 968
                O_ZB = O_P0B + 1088
                O_G = O_ZB + 1040
                O_QT = O_G + 2176
                O_PT = O_QT + 512
                O_QF = O_PT + 512
                O_MSK = O_QF + 1024
                O_DD = O_MSK + 512
                KM = O_DD + 512
                HL = KM + 64
                END = HL + 128 + 1024 + 256
                assert END <= PH, (END, PH)
                Kl = ab(KL, 4096).rearrange("p (h t) -> p h t", h=4)
                Kp = ab(KP, 4096).rearrange("p (h t) -> p h t", h=4)
                Vl = ab(VL, 4096).rearrange("p (c n) -> p c n", c=16)
                Vp = ab(VP, 4096).rearrange("p (c n) -> p c n", c=16)
                xt = af(XT, 4096).rearrange("p (c t) -> p c t", c=16)
                xn = ab(XN, 2048).rearrange("p (c t) -> p c t", c=16)
                sqy = ab(YC, 2048).rearrange("p (c t) -> p c t", c=16)
                yc = af(YC, 4096).rearrange("p (c t) -> p c t", c=16)
                rs4 = af(RS, 1024).rearrange("p (c t) -> p c t", c=4)
                kmT = af(KM, 64).rearrange("p (h n) -> p h n", h=4)
                p0h = af(HL, 64).rearrange("p (g t) -> p g t", g=4)
                zh = af(HL + 64, 64).rearrange("p (g t) -> p g t", g=4)
                t_rstd = af(TP, 256)
                t_tmp = [af(TP + 256 + k * 256, 256) for k in range(2)]
                tq = ab(O_TQ, 128)
                trs = af(O_TRS, 256)
                sm = af(O_SM, 64).rearrange("p (h n) -> p h n", h=4)
                m8 = af(O_M8, 32).rearrange("p (h n) -> p h n", h=4)
                sel = af(O_SEL, 64).rearrange("p (h n) -> p h n", h=4)
                ss4 = af(O_SS4, 4)
                junk = af(O_JUNK, 128)
                t16 = af(O_T16, 16)
                tg = af(O_TG, 16)
                hst = af(O_HST, 128).rearrange("p (a g t) -> p a g t", a=2, g=4)
                hprev = af(O_HPV, 128)
                p0b = af(O_P0B, 1088).rearrange("p (g t) -> p g t", g=4)
                zb = af(O_ZB, 1040).rearrange("p (g t) -> p g t", g=4)
                G = [af(O_G + k * 544, 544) for k in range(4)]
                rG = [res(f'G{k}') for k in range(4)]
                qT = ab(O_QT, 512).rearrange("p (h t) -> p h t", h=4)
                pT = [ab(O_PT + k * 256, 256).rearrange("p (a t) -> p a t", a=2) for k in range(2)] + [ab(HL + 128 + 1024, 256).rearrange("p (a t) -> p a t", a=2)]
                qf = af(O_QF, 1024).rearrange("p (h t) -> p h t", h=4)
                mskT = ab(O_MSK, 512).rearrange("p (h t) -> p h t", h=4)
                for h in range(4):
                    P.op('dve', lambda e, h=h: e.memset(mskT[:, h, :], 0.0), w=[res(f'mskT{h}')])
                dres = der_res(l, 1)
                o2048, o2048r = ones(0)
                o128, o128r = ones(1)
                o512, o512r = ones(2)
                o1, o1r = ones(3)
                r_xt = res('m_xt')
                rxn = [res(f'm_xn{dc}') for dc in range(16)]
                ryc = [res(f'yc{c}') for c in range(16)]
                xt_sem = dsem('m_xt')

                def load_xt(j):
                    slabs = [xr(srcname, dc, j) for dc in range(16)]
                    P.op('sp', lambda e, j=j: e.dma_start(out=xt, in_=src[:, :, j * TM:(j + 1) * TM].rearrange("c p t -> p c t")),
                         r=slabs, w=[r_xt], dsem=xt_sem)

                def modnorm_A():
                    bk = take_banks(1)[0]
                    P.op('act', lambda e: e.activation(out=sqy, in_=xt, func=AF.Square), r=[r_xt], w=ryc)
                    for dc in range(16):
                        P.op('pe', lambda e, dc=dc: e.matmul(psum[bk][:, 0:256], lhsT=o2048, rhs=sqy[:, dc, :], start=(dc == 0), stop=(dc == 15)),
                             r=ryc + [o2048r], pw=[bankres[bk]] if dc == 0 else [], fw=[bankres[bk]] if dc == 15 else [])
                    rsqrt(t_rstd, psum[bk][:, 0:256], [bankres[bk]], res('m_rstd'))
                    for q4 in range(4):
                        P.op('dve', lambda e, q4=q4: e.tensor_tensor(out=xt[:, q4 * 4:(q4 + 1) * 4, :], in0=xt[:, q4 * 4:(q4 + 1) * 4, :], in1=t_rstd.unsqueeze(1).to_broadcast([128, 4, 256]), op=ALU.mult),
                             r=[r_xt, res('m_rstd')], w=[res(f'm_xs{q4}')])

                def modnorm_B():
                    for dc in range(16):
                        q4 = dc // 4
                        if True:
                            P.op('act', lambda e, dc=dc: e.activation(out=xn[:, dc, :], in_=xt[:, dc, :], func=AF.Identity, bias=derB(l, 1, dc), scale=derA(l, 1, dc)),
                                 r=[res(f'm_xs{q4}'), r_xt] + dres, w=[rxn[dc]])
                        else:
                            P.op('dve', lambda e, dc=dc: e.tensor_scalar(out=xn[:, dc, :], in0=xt[:, dc, :], scalar1=derA(l, 1, dc), scalar2=derB(l, 1, dc), op0=ALU.mult, op1=ALU.add),
                                 r=[res(f'm_xs{q4}'), r_xt] + dres, w=[rxn[dc]])

                def proj_fm(chunks, N=256, c0=0, wg=None):
                    wg = wg or wget
                    out = []
                    for gi in range(0, len(chunks), 2):
                        cc = chunks[gi]
                        assert cc % 2 == 0 and chunks[gi + 1] == cc + 1
                        wt, wr = wg(winf_d[l, cc // 2], 4096)
                        wv = wt.rearrange("p (c k n) -> p c k n", c=2, k=16)
                        bk = take_banks(1)[0]
                        for c2 in range(2):
                            for dc in range(16):
                                P.op('pe', lambda e, c2=c2, dc=dc, wv=wv, bk=bk: e.matmul(psum[bk][:, c2 * 256:c2 * 256 + N], lhsT=wv[:, c2, dc, :], rhs=xn[:, dc, c0:c0 + N], start=(dc == 0), stop=(dc == 15)),
                                     r=[wr, rxn[dc]], pw=[bankres[bk]] if (c2 == 0 and dc == 0) else [], fw=[bankres[bk]] if (c2 == 1 and dc == 15) else [])
                            out.append((bk, c2 * 256))
                    return out

                def proj_tm(which, wg=None):
                    wg = wg or wget
                    bks = take_banks(2)
                    for half in range(2):
                        wt, wr = wg(wint_d[l, which, half], 4096)
                        wv = wt.rearrange("p (k n) -> p k n", k=8)
                        for s_ in range(2):
                            for k8 in range(8):
                                dc = half * 8 + k8
                                P.op('pe', lambda e, s_=s_, k8=k8, dc=dc, wv=wv, b=bks[s_]: e.matmul(psum[b][:, :], lhsT=xn[:, dc, s_ * 128:(s_ + 1) * 128], rhs=wv[:, k8, :], start=(dc == 0), stop=(dc == 15)),
                                     r=[wr, rxn[dc]], pw=[bankres[bks[s_]]] if dc == 0 else [], fw=[bankres[bks[s_]]] if dc == 15 else [])
                    return bks

                def headnorm(bk, off, gain_ap, gain_res, dst_b, dst_res, dst_f, dst_f_res):
                    sb_ = take_banks(1)[0]
                    P.op('act', lambda e: e.activation(out=tq, in_=psum[bk][:, off:off + 256], func=AF.Square),
                         r=[bankres[bk]], w=[res('hn_sq')])
                    P.op('pe', lambda e: e.matmul(psum[sb_][:, 0:256], lhsT=o128, rhs=tq, start=True, stop=True),
                         r=[res('hn_sq'), o128r], w=[bankres[sb_]])
                    rsqrt(trs, psum[sb_][:, 0:256], [bankres[sb_]], res('hn_rs'))
                    P.op('dve', lambda e: e.scalar_tensor_tensor(out=dst_f, in0=psum[bk][:, off:off + 256], scalar=gain_ap, in1=trs, op0=ALU.mult, op1=ALU.mult),
                         r=[bankres[bk], res('hn_rs')] + gain_res, w=[dst_f_res])
                    P.op('act', lambda e: e.activation(out=dst_b, in_=dst_f, func=AF.Copy), r=[dst_f_res], w=[dst_res])

                rqf = [res(f'qf{h}') for h in range(4)]

                def headnorm4(outs, gain_ap, gain_res, dst_b, dst_res):
                    tqs = [pT[h // 2][:, h % 2, 0:256] for h in range(4)]
                    tqr = [res(f'pT{h // 2}') for h in range(4)]
                    sbs = []
                    for h in range(4):
                        bk, off = outs[h]
                        P.op('act', lambda e, h=h, bk=bk, off=off: e.activation(out=tqs[h], in_=psum[bk][:, off:off + 256], func=AF.Square),
                             r=[bankres[bk]], w=[res(f'tq{h}')], pw=[tqr[h]] if h % 2 == 0 else [])
                    for h in range(4):
                        sb_ = take_banks(1)[0]
                        sbs.append(sb_)
                        P.op('pe', lambda e, h=h, sb_=sb_: e.matmul(psum[sb_][:, 0:256], lhsT=o128, rhs=tqs[h], start=True, stop=True),
                             r=[res(f'tq{h}'), o128r], w=[bankres[sb_]])
                    for h in range(4):
                        rsqrt(qf[:, h, :], psum[sbs[h]][:, 0:256], [bankres[sbs[h]]], rqf[h])
                    for h in range(4):
                        bk, off = outs[h]
                        P.op('dve', lambda e, h=h, bk=bk, off=off: e.scalar_tensor_tensor(out=qf[:, h, :], in0=psum[bk][:, off:off + 256], scalar=gain_ap, in1=qf[:, h, :], op0=ALU.mult, op1=ALU.mult),
                             r=[bankres[bk], rqf[h]] + gain_res, w=[rqf[h]])
                    for h in range(4):
                        P.op('act', lambda e, h=h: e.activation(out=dst_b[h], in_=qf[:, h, :], func=AF.Copy), r=[rqf[h]], w=[dst_res[h]])

                vst = [G[2][:, 0:512], G[3][:, 0:512]]
                gv_iv = gv_i.rearrange("r c -> (r c)").rearrange("(t n) -> t n", n=512)
                load_xt(0)
                modnorm_A()
                for j in range(NTM):
                    modnorm_B()
                    load_xt(j + 1 if j + 1 < NTM else 0)
                    outs = proj_fm([16, 17, 18, 19])
                    modnorm_A()
                    headnorm4(outs, pp[:, PP_KG + l:PP_KG + l + 1], [r_pp], [Kl[:, h, j * TM:(j + 1) * TM] for h in range(4)], [res(f'm_Kl_{h}_{j}') for h in range(4)])
                    P.op('sp', lambda e, j=j: e.dma_start(out=gk_i[:, j * TM:(j + 1) * TM].rearrange("(h p) t -> p h t", p=128), in_=Kl[:, :, j * TM:(j + 1) * TM]),
                         r=[res(f'm_Kl_{h}_{j}') for h in range(4)], w=[res(f'gin_k{j}')], dsem=dsem('m_kst'))
                    bks = proj_tm(1)
                    for s_ in range(2):
                        P.op('act', lambda e, s_=s_, j=j, b=bks[s_]: e.activation(out=Vl[:, j * 2 + s_, :], in_=psum[b][:, :], func=AF.Copy),
                             r=[bankres[bks[s_]]], w=[res(f'm_Vl_{j}_{s_}')])
                    P.op('sp', lambda e, j=j: e.dma_start(out=gv_iv[j * TM:(j + 1) * TM, :].rearrange("(s p) n -> p s n", p=128), in_=Vl[:, 2 * j:2 * j + 2, :]),
                         r=[res(f'm_Vl_{j}_0'), res(f'm_Vl_{j}_1')], w=[res(f'gin_v{j}')], dsem=dsem('m_vst'))
                    if j == NTM - 1:
                        o_p0 = proj_fm([0, 1, 2, 3], N=16, c0=240)
                        o_gc = proj_fm([28, 29, 30, 31], N=16, c0=240)
                        o_h = proj_fm([32, 33, 34, 35], N=16, c0=240)
                        for g in range(4):
                            bk, off = o_p0[g]
                            P.op('act', lambda e, g=g, bk=bk, off=off: e.activation(out=hst[:, 0, g, :], in_=psum[bk][:, off:off + 16], func=AF.Copy),
                                 r=[bankres[bk]], w=[res('hst')])
                            bk, off = o_gc[g]
                            P.op('act', lambda e, bk=bk, off=off: e.activation(out=tg, in_=psum[bk][:, off:off + 16], func=AF.Copy),
                                 r=[bankres[bk]], w=[res('tg')])
                            bk2, off2 = o_h[g]
                            P.op('dve', lambda e, g=g, bk2=bk2, off2=off2: e.tensor_tensor(out=hst[:, 1, g, :], in0=psum[bk2][:, off2:off2 + 16], in1=tg, op=ALU.mult),
                                 r=[bankres[bk2], res('tg'), res('hst')], w=[res('hst')])
                        P.op('sp', lambda e: e.dma_start(out=gh_i[0:8, :].rearrange("r c -> (r c)").rearrange("(p x) -> p x", p=128), in_=af(O_HST, 128)),
                             r=[res('hst')], w=[res('gin_h')], dsem=dsem('m_hst'))
                for h in range(4):
                    P.op('dve', lambda e, h=h: e.tensor_reduce(out=kmT[:, h, 8:16], in_=Kl[:, h, :].rearrange("p (n t) -> p n t", t=256), axis=AX.X, op=ALU.add),
                         r=[res(f'm_Kl_{h}_{j}') for j in range(NTM)], w=[res(f'kmL{h}')])
                RG = [[0, 1], [2, 3], [4, 5], [6, 7]]

                def gather(name, i_ap, o_ap, rlist):
                    P.op('pool', lambda e: e.collective_compute("AllGather", ALU.bypass, replica_groups=RG, ins=[i_ap.opt()], outs=[o_ap.opt()]),
                         r=rlist, w=[res('go_' + name)], dsem=dsem('cc_' + name), dinc=1)

                gather('k', gk_i, gk_o, [res(f'gin_k{j}') for j in range(NTM)])
                gather('v', gv_i, gv_o, [res(f'gin_v{j}') for j in range(NTM)])
                gather('h', gh_i, gh_o, [res('gin_h')])
                P.op('pool', lambda e: e.dma_start(out=Kp, in_=gk_o[0:512, :].rearrange("(h p) t -> p h t", p=128)),
                     r=[res('go_k')], w=[res('m_Kp0'), res('m_Kp1')], dsem=dsem('m_kp'))
                gvv = gv_o[0:512, :].rearrange("r c -> (r c)").rearrange("(t n) -> t n", n=512)
                P.op('pool', lambda e: e.dma_start(out=Vp, in_=gvv.rearrange("(c p) n -> p c n", p=128)),
                     r=[res('go_v')], w=[res('m_Vp0'), res('m_Vp1')], dsem=dsem('m_vp'))
                P.op('sp', lambda e: e.dma_start(out=hprev, in_=gh_o[0:8, :].rearrange("r c -> (r c)").rearrange("(p x) -> p x", p=128)),
                     r=[res('go_h')], w=[res('hprev')], dsem=dsem('m_hp'))
                P.op('dve', lambda e: e.tensor_scalar(out=af(HL, 128), in0=hprev, scalar1=pp[:, PP_HALF:PP_HALF + 1], scalar2=None, op0=ALU.mult),
                     r=[res('hprev'), r_pp], w=[res('p0h'), res('zh')])
                for h in range(4):
                    P.op('dve', lambda e, h=h: e.tensor_reduce(out=kmT[:, h, 0:8], in_=Kp[:, h, :].rearrange("p (n t) -> p n t", t=256), axis=AX.X, op=ALU.add),
                         r=[res(f'm_Kp{h // 2}')], w=[res(f'kmP{h}')])
                P.op('dve', lambda e: e.tensor_scalar(out=af(KM, 64), in0=af(KM, 64), scalar1=1.0 / 256, scalar2=None, op0=ALU.mult),
                     r=[res(f'kmP{h}') for h in range(4)] + [res(f'kmL{h}') for h in range(4)], w=[res('kmT')])

                def gelu_ip(buf, n, bres):
                    t_ = G[1][:, 0:n]
                    P.op('dve', lambda e: e.tensor_tensor(out=t_, in0=buf, in1=buf, op=ALU.mult), r=[bres], w=[rG[1]])
                    P.op('dve', lambda e: e.tensor_scalar(out=t_, in0=t_, scalar1=0.044715, scalar2=1.0, op0=ALU.mult, op1=ALU.add), r=[rG[1]], w=[rG[1]])
                    P.op('dve', lambda e: e.tensor_tensor(out=t_, in0=t_, in1=buf, op=ALU.mult), r=[rG[1], bres], w=[rG[1]])
                    P.op('act', lambda e: e.activation(out=t_, in_=t_, func=AF.Sigmoid, scale=1.5957691216057308), r=[rG[1]], w=[rG[1]])
                    P.op('dve', lambda e: e.tensor_tensor(out=buf, in0=t_, in1=buf, op=ALU.mult), r=[rG[1], bres], w=[bres])

                rp0 = [res(f'p0b{g}') for g in range(4)]
                rzb = [res(f'zb{g}') for g in range(4)]
                rdd = [res(f'dd{g}') for g in range(4)]
                ddb = ab(O_DD, 512).rearrange("p (g t) -> p g t", g=4)
                vr = [G[2][:, 0:512], G[3][:, 0:512]]
                vh = G[0][:, 0:512].bitcast(BF16).rearrange("p (s n) -> p s n", s=2)
                acc = G[1][:, 0:256]
                xcb = [af(HL + 128 + k * 256, 256) for k in range(4)]
                rxcb = [res(f'xcb{k}') for k in range(4)]
                xcb_sem = [dsem(f'xcb{k}') for k in range(4)]
                for j in range(NTM):
                    modnorm_B()
                    if j + 1 < NTM:
                        load_xt(j + 1)
                    wi = [0]

                    def wget2(src_, nel, j=j):
                        idx = wi[0]
                        wi[0] += 1
                        if j == 0:
                            wt_, wr_ = wget(src_, nel)
                            k_ = wstate['last_k']
                            P.op('sp', lambda e, idx=idx, wt_=wt_: e.dma_start(out=wsc[l, idx], in_=wt_),
                                 r=[wr_], w=[res(f'wsc{l}_{idx}')], dsem=dsem(f'wst{k_}'))
                            return wt_, wr_
                        return wget(wsc[l, idx], nel, rkey=f'wsc{l}_{idx}')

                    tb = PP_TABS + j * 48
                    rankb = pp[:, tb:tb + 16].unsqueeze(1).to_broadcast([128, 4, 16])
                    notown = pp[:, tb + 16:tb + 32].unsqueeze(1).to_broadcast([128, 4, 16])
                    finalb = pp[:, tb + 32:tb + 48].unsqueeze(1).to_broadcast([128, 4, 16])
                    o_q = proj_fm([12, 13, 14, 15], wg=wget2)
                    headnorm4(o_q, misc[:, l:l + 1], [res('qgs')], [qT[:, h, :] for h in range(4)], [res(f'qT{h}') for h in range(4)])
                    mbk = take_banks(2)
                    sel_toks = []
                    sbks = take_banks(2)
                    for s_ in range(2):
                        sbk = sbks[s_]
                        for h in range(4):
                            P.op('pe', lambda e, h=h, s_=s_, sbk=sbk: e.matmul(psum[sbk][:, h * 16:(h + 1) * 16], lhsT=qf[:, h, s_ * 128:(s_ + 1) * 128], rhs=kmT[:, h, :], start=True, stop=True),
                                 r=[rqf[h], res('kmT')], pw=[bankres[sbk]] if h == 0 else [], fw=[bankres[sbk]] if h == 3 else [])
                    o_p0 = proj_fm([0, 1, 2, 3], wg=wget2)
                    P.op('pool', lambda e: e.tensor_copy(out=p0b[:, :, 0:16], in_=p0h), r=[res('p0h')], w=[res('p0halo')])
                    for g in range(4):
                        bk, off = o_p0[g]
                        P.op('act', lambda e, g=g, bk=bk, off=off: e.activation(out=p0b[:, g, 16:272], in_=psum[bk][:, off:off + 256], func=AF.Copy),
                             r=[bankres[bk]], w=[rp0[g]])
                    P.op('pool', lambda e: e.tensor_copy(out=p0h, in_=p0b[:, :, 256:272]), r=rp0, w=[res('p0h')])
                    o_u = proj_fm([4, 5, 6, 7], wg=wget2)
                    for h in range(4):
                        bk, off = o_u[h]
                        P.op('act', lambda e, h=h, bk=bk, off=off: e.activation(out=yc[:, 4 + h, :], in_=psum[bk][:, off:off + 256], func=AF.Copy),
                             r=[bankres[bk]], w=[ryc[4 + h]])
                    bks = proj_tm(0, wg=wget2)
                    for s_ in range(2):
                        P.op('act', lambda e, s_=s_, b=bks[s_]: e.activation(out=vr[s_], in_=psum[b][:, :], func=AF.Copy),
                             r=[bankres[bks[s_]]], w=[rG[2 + s_]])
                    for s_ in range(2):
                        sbk = sbks[s_]
                        P.op('dve', lambda e, sbk=sbk, rankb=rankb: e.tensor_tensor(out=sm, in0=psum[sbk][:, 0:64].rearrange("p (h n) -> p h n", h=4), in1=rankb, op=ALU.add),
                             r=[bankres[sbk], r_pp], w=[res('sm')])
                        for h in range(4):
                            P.op('dve', lambda e, h=h: e.max(out=m8[:, h, :], in_=sm[:, h, :]), r=[res('sm')], w=[res('m8')])
                        for h in range(4):
                            P.op('dve', lambda e, h=h: e.tensor_scalar(out=sel[:, h, :], in0=sm[:, h, :], scalar1=m8[:, h, 2:3], scalar2=None, op0=ALU.is_ge),
                                 r=[res('sm'), res('m8')], w=[res('sel')])
                        P.op('dve', lambda e: e.tensor_scalar(out=sel, in0=sel, scalar1=-1.0, scalar2=BIG, op0=ALU.add, op1=ALU.mult), r=[res('sel')], w=[res('sel')])
                        P.op('dve', lambda e, notown=notown: e.tensor_tensor(out=sel, in0=sel, in1=notown, op=ALU.mult), r=[res('sel'), r_pp], w=[res('sel')])
                        P.op('dve', lambda e, finalb=finalb: e.tensor_tensor(out=sel, in0=sel, in1=finalb, op=ALU.add), r=[res('sel'), r_pp], w=[res('sel')])
                        for h in range(4):
                            b = mbk[h // 2]
                            P.op('pe', lambda e, h=h, s_=s_, b=b: e.transpose(psum[b][0:16, (h % 2) * 256 + s_ * 128:(h % 2) * 256 + (s_ + 1) * 128], sel[:, h, :], ident_f[:]),
                                 r=[res('sel'), res('ident_f')], pw=[bankres[b]] if (s_ == 0 and h % 2 == 0) else [], fw=[bankres[b]] if (s_ == 1 and h % 2 == 1) else [])
                    for h in range(4):
                        b = mbk[h // 2]
                        P.op('act', lambda e, h=h, b=b: e.activation(out=mskT[0:16, h, :], in_=psum[b][0:16, (h % 2) * 256:(h % 2 + 1) * 256], func=AF.Copy),
                             r=[bankres[b]], w=[res(f'mskT{h}')])
                    o_gc = proj_fm([28, 29, 30, 31], wg=wget2)
                    P.op('pool', lambda e: e.tensor_copy(out=zb[:, :, 0:2], in_=zh[:, :, 14:16]), r=[res('zh')], w=[res('zhalo')])
                    for g in range(4):
                        bk, off = o_gc[g]
                        P.op('act', lambda e, g=g, bk=bk, off=off: e.activation(out=zb[:, g, 2:258], in_=psum[bk][:, off:off + 256], func=AF.Copy), r=[bankres[bk]], w=[rzb[g]])
                    o_hh = proj_fm([32, 33, 34, 35], wg=wget2)
                    for g in range(4):
                        bk2, off2 = o_hh[g]
                        P.op('dve', lambda e, g=g, bk2=bk2, off2=off2: e.tensor_tensor(out=zb[:, g, 2:258], in0=psum[bk2][:, off2:off2 + 256], in1=zb[:, g, 2:258], op=ALU.mult),
                             r=[bankres[bk2], rzb[g]], w=[rzb[g]])
                    P.op('pool', lambda e: e.tensor_copy(out=zh[:, :, 14:16], in_=zb[:, :, 256:258]), r=rzb, w=[res('zh')])
                    o_gb = proj_fm([24, 25, 26, 27], wg=wget2)
                    for g in range(4):
                        bk3, off3 = o_gb[g]
                        P.op('act', lambda e, g=g, bk3=bk3, off3=off3: e.activation(out=yc[:, 12 + g, :], in_=psum[bk3][:, off3:off3 + 256], func=AF.Copy), r=[bankres[bk3]], w=[ryc[12 + g]])

                    def pool_unit(g):
                        wn = 2 << g
                        srcb = p0b[:, g, :]
                        cur_in, cur_res = srcb, [rp0[g], res('p0halo')]
                        wkb = G[0].rearrange("p (a t) -> p a t", a=2)
                        for st_ in range(g + 1):
                            sh = 1 << st_
                            o_ = wkb[:, st_ % 2, :]
                            P.op('pool', lambda e, o_=o_, cur_in=cur_in, sh=sh: e.tensor_tensor(out=o_[:, sh:272], in0=cur_in[:, sh:272], in1=cur_in[:, 0:272 - sh], op=ALU.add),
                                 r=cur_res, w=[rG[0]])
                            cur_in, cur_res = o_, [rG[0]]
                        dd = ddb[:, g, :]
                        P.op('dve', lambda e, cur_in=cur_in, srcb=srcb, wn=wn, dd=dd: e.scalar_tensor_tensor(out=dd, in0=cur_in[:, 16:272], scalar=1.0 / wn, in1=srcb[:, 16:272], op0=ALU.mult, op1=ALU.subtract),
                             r=cur_res + [rp0[g]], w=[rdd[g]])
                        if j == 0:
                            P.op('dve', lambda e, cur_in=cur_in, g=g: e.tensor_tensor(out=t16, in0=cur_in[:, 16:32], in1=pp[:, PP_INVC + g * 16:PP_INVC + (g + 1) * 16], op=ALU.mult),
                                 r=cur_res + [r_pp], w=[res('t16')])
                            P.op('dve', lambda e, srcb=srcb, dd=dd: e.tensor_tensor(out=dd[:, 0:16], in0=t16, in1=srcb[:, 16:32], op=ALU.subtract),
                                 r=[res('t16'), rp0[g], rdd[g]], w=[rdd[g]])

                    def sgu_v_unit(s_):
                        gelu_ip(vr[s_], 512, rG[2 + s_])
                        for h in range(4):
                            P.op('act', lambda e, h=h, s_=s_: e.activation(out=junk, in_=vr[s_][:, h * 128:(h + 1) * 128], func=AF.Square, accum_out=ss4[:, h:h + 1]),
                                 r=[rG[2 + s_]], w=[res('ss4'), res('junk')])
                        rsqrt(ss4, ss4, [res('ss4')], res('ss4'), prescale=1.0 / 128)
                        for h in range(4):
                            P.op('dve', lambda e, h=h, s_=s_: e.scalar_tensor_tensor(out=vh[:, s_, h * 128:(h + 1) * 128], in0=vr[s_][:, h * 128:(h + 1) * 128], scalar=ss4[:, h:h + 1],
                                                                                     in1=bc[:, BC_SGUG + l * 512 + h * 128:BC_SGUG + l * 512 + (h + 1) * 128], op0=ALU.mult, op1=ALU.mult),
                                 r=[rG[2 + s_], res('ss4'), r_bc], w=[res(f'vh{s_}')], pw=[rG[0]] if h == 0 else [])

                    def conv_unit(g):
                        zr = [rzb[g], res('zhalo')]
                        cw = PP_CONVW + l * 12
                        P.op('dve', lambda e, g=g, cw=cw: e.tensor_scalar(out=acc, in0=zb[:, g, 0:256], scalar1=pp[:, cw + g:cw + g + 1], scalar2=None, op0=ALU.mult),
                             r=zr + [r_pp], w=[rG[1]])
                        for jj in (1, 2):
                            P.op('dve', lambda e, g=g, cw=cw, jj=jj: e.scalar_tensor_tensor(out=acc, in0=zb[:, g, jj:jj + 256], scalar=pp[:, cw + jj * 4 + g:cw + jj * 4 + g + 1], in1=acc, op0=ALU.mult, op1=ALU.add),
                                 r=zr + [r_pp, rG[1]], w=[rG[1]])
                        P.op('dve', lambda e, g=g: e.tensor_tensor(out=yc[:, 12 + g, :], in0=yc[:, 12 + g, :], in1=acc, op=ALU.mult),
                             r=[ryc[12 + g], rG[1]], w=[ryc[12 + g]])

                    units = [lambda g=g: pool_unit(g) for g in range(4)] + [lambda s_=s_: sgu_v_unit(s_) for s_ in range(2)] \
                        + [lambda h=h: gelu_ip(yc[:, 4 + h, :], 256, ryc[4 + h]) for h in range(4)] + [lambda g=g: conv_unit(g) for g in range(4)]
                    upos = [0]

                    def run_units(n):
                        for _ in range(n):
                            if upos[0] < len(units):
                                units[upos[0]]()
                                upos[0] += 1

                    npair = 8 + (j + 1)

                    def emit_S(h, pr, gi):
                        sbk = 4 + (gi % 3)
                        for i2 in range(2):
                            kt = pr * 2 + i2
                            if kt < 16:
                                Ksrc, Kres, blk, own = Kp[:, h, kt * 128:(kt + 1) * 128], [res(f'm_Kp{h // 2}')], kt // 2, False
                            else:
                                kl = kt - 16
                                Ksrc, Kres, blk, own = Kl[:, h, kl * 128:(kl + 1) * 128], [res(f'm_Kl_{h}_{kl // 2}')], 8 + kl // 2, (kl // 2 == j)
                            oc = i2 * 256
                            P.op('pe', lambda e, Ksrc=Ksrc, oc=oc, h=h, sbk=sbk: e.matmul(psum[sbk][:, oc:oc + 256], lhsT=Ksrc, rhs=qT[:, h, :], start=True, stop=False),
                                 r=Kres + [res(f'qT{h}')], pw=[bankres[sbk]] if i2 == 0 else [])
                            P.op('pe', lambda e, blk=blk, oc=oc, h=h, sbk=sbk, own=own: e.matmul(psum[sbk][:, oc:oc + 256], lhsT=onehot_b[:, blk, :], rhs=mskT[:, h, :], start=False, stop=(not own)),
                                 r=[res('onehot_b'), res(f'mskT{h}')], fw=[bankres[sbk]] if (i2 == 1 and not own) else [])
                            if own:
                                P.op('pe', lambda e, oc=oc, sbk=sbk, i2=i2: e.matmul(psum[sbk][:, oc:oc + 256], lhsT=ident_b[:], rhs=causal_b[:, i2, :], start=False, stop=True),
                                     r=[res('ident_b'), res('causal_b')], fw=[bankres[sbk]] if i2 == 1 else [])
                        pk = gi % 3
                        P.op('act', lambda e, sbk=sbk, pk=pk: e.activation(out=pT[pk].rearrange("p a t -> p (a t)"), in_=psum[sbk][:, :], func=AF.Exp),
                             r=[bankres[sbk]], w=[res(f'pT{pk}')])

                    def emit_PV(h, pr, gi):
                        ob, db = (0, 1) if h % 2 == 0 else (2, 3)
                        pk = gi % 3
                        for i2 in range(2):
                            kt = pr * 2 + i2
                            if kt < 16:
                                Vsrc, Vres = Vp[:, kt, h * 128:(h + 1) * 128], [res(f'm_Vp{kt // 8}')]
                            else:
                                kl = kt - 16
                                Vsrc, Vres = Vl[:, kl, h * 128:(h + 1) * 128], [res(f'm_Vl_{kl // 2}_{kl % 2}')]
                            first = (pr == 0 and i2 == 0)
                            last = (pr == npair - 1 and i2 == 1)
                            P.op('pe', lambda e, Vsrc=Vsrc, pk=pk, i2=i2, ob=ob, first=first, last=last: e.matmul(psum[ob][:, 0:256], lhsT=Vsrc, rhs=pT[pk][:, i2, :], start=first, stop=last),
                                 r=Vres + [res(f'pT{pk}')], pw=[bankres[ob]] if first else [], fw=[bankres[ob]] if last else [])
                            P.op('pe', lambda e, pk=pk, i2=i2, db=db, first=first, last=last: e.matmul(psum[db][:, 0:256], lhsT=o1, rhs=pT[pk][:, i2, :], start=first, stop=last),
                                 r=[o1r, res(f'pT{pk}')], pw=[bankres[db]] if first else [], fw=[bankres[db]] if last else [])

                    seq = [(h, pr) for h in range(4) for pr in range(npair)]
                    emit_S(seq[0][0], seq[0][1], 0)
                    emit_S(seq[1][0], seq[1][1], 1)
                    for gi, (h, pr) in enumerate(seq):
                        if gi + 2 < len(seq):
                            emit_S(seq[gi + 2][0], seq[gi + 2][1], gi + 2)
                        emit_PV(h, pr, gi)
                        if pr == npair - 1:
                            ob, db = (0, 1) if h % 2 == 0 else (2, 3)
                            run_units((4, 2, 4, 4)[h])
                            rD = trs
                            P.op('dve', lambda e, db=db, rD=rD: e.reciprocal(out=rD, in_=psum[db][:, 0:256]), r=[bankres[db]], w=[res('hn_rs')])
                            P.op('dve', lambda e, ob=ob, rD=rD, h=h: e.tensor_tensor(out=yc[:, 8 + h, :], in0=psum[ob][:, 0:256], in1=rD, op=ALU.mult),
                                 r=[bankres[ob], res('hn_rs')], w=[ryc[8 + h]])
                    run_units(len(units))

                    for g in range(4):
                        bk = take_banks(1)[0]
                        P.op('pe', lambda e, g=g, bk=bk: e.matmul(psum[bk][:, 0:256], lhsT=poolw_b[:, l * 4 + g, :], rhs=ddb[:, g, :], start=True, stop=True),
                             r=[rdd[g], res('poolw_b')], w=[bankres[bk]])
                        P.op('act', lambda e, g=g, bk=bk: e.activation(out=yc[:, g, :], in_=psum[bk][:, 0:256], func=AF.Copy, scale=pp[:, PP_POOLS + l * 4 + g:PP_POOLS + l * 4 + g + 1]),
                             r=[bankres[bk], r_pp], w=[ryc[g]])
                    mb_ = take_banks(2)
                    for h in range(4):
                        b = mb_[h // 2]
                        for s_ in range(2):
                            P.op('pe', lambda e, h=h, s_=s_, b=b: e.matmul(psum[b][:, (h % 2) * 256 + s_ * 128:(h % 2) * 256 + (s_ + 1) * 128], lhsT=vh[:, s_, h * 128:(h + 1) * 128], rhs=sguw_b[:, l * 4 + h, :], start=True, stop=True),
                                 r=[res(f'vh{s_}'), rG[0], res('sguw_b')], pw=[bankres[b]] if (h % 2 == 0 and s_ == 0) else [], fw=[bankres[b]] if (h % 2 == 1 and s_ == 1) else [])
                    for h in range(4):
                        b = mb_[h // 2]
                        tmpm = G[1][:, (h % 2) * 256:(h % 2) * 256 + 256]
                        sgb = bc[:, BC_SGUB + (l * 4 + h) * 128:BC_SGUB + (l * 4 + h + 1) * 128]
                        P.op('dve', lambda e, h=h, b=b, tmpm=tmpm, sgb=sgb: e.tensor_tensor(out=tmpm.rearrange("p (s t) -> p s t", s=2), in0=psum[b][:, (h % 2) * 256:(h % 2 + 1) * 256].rearrange("p (s t) -> p s t", s=2),
                                                                                          in1=sgb.unsqueeze(1).to_broadcast([128, 2, 128]), op=ALU.add),
                             r=[bankres[b], r_bc], w=[rG[1]])
                        P.op('dve', lambda e, h=h, tmpm=tmpm: e.tensor_tensor(out=yc[:, 4 + h, :], in0=yc[:, 4 + h, :], in1=tmpm, op=ALU.mult),
                             r=[rG[1], ryc[4 + h]], w=[ryc[4 + h]])

                    if DBG_YCAT:
                        P.op('dve', lambda e: e.tensor_copy(out=xt, in_=yc), r=ryc + [r_xt], w=[r_xt])
                        slabs = [xr(dstname, dc, j) for dc in range(16)]
                        P.op('sp', lambda e, j=j: e.dma_start(out=dst[:, :, j * TM:(j + 1) * TM].rearrange("c p t -> p c t"), in_=xt),
                             r=[r_xt], w=slabs + [r_xt], dsem=xt_sem)
                        continue
                    gb_ = take_banks(2)
                    for m in range(4):
                        b = gb_[m // 2]
                        sqg = G[m][:, 0:512].bitcast(BF16).rearrange("p (c t) -> p c t", c=4)
                        P.op('act', lambda e, m=m, sqg=sqg: e.activation(out=sqg, in_=yc[:, m * 4:(m + 1) * 4, :], func=AF.Square), r=ryc[m * 4:(m + 1) * 4], w=[rG[m]])
                        for c4 in range(4):
                            P.op('pe', lambda e, m=m, c4=c4, b=b, sqg=sqg: e.matmul(psum[b][:, (m % 2) * 256:(m % 2 + 1) * 256], lhsT=o512, rhs=sqg[:, c4, :], start=(c4 == 0), stop=(c4 == 3)),
                                 r=[rG[m], o512r], pw=[bankres[b]] if (m % 2 == 0 and c4 == 0) else [], fw=[bankres[b]] if (m % 2 == 1 and c4 == 3) else [])
                    for m2 in range(2):
                        rsqrt(rs4[:, m2 * 2:m2 * 2 + 2, :].rearrange("p a t -> p (a t)"), psum[gb_[m2]][:, :], [bankres[gb_[m2]]], res(f'rs4_{m2}'))
                    for dc in range(16):
                        eng = 'dve'
                        og = PP_OUTG + l * 16 + dc
                        P.op(eng, lambda e, dc=dc, og=og: e.scalar_tensor_tensor(out=xn[:, dc, :], in0=yc[:, dc, :], scalar=pp[:, og:og + 1], in1=rs4[:, dc // 4, :], op0=ALU.mult, op1=ALU.mult),
                             r=[ryc[dc], res(f'rs4_{dc // 8}'), r_pp], w=[rxn[dc]])
                    if j + 1 < NTM and not DBG_YCAT:
                        modnorm_A()
                    def xcb_load(dt, j=j):
                        kb = dt % 4
                        P.op('sp', lambda e, dt=dt, kb=kb, j=j: e.dma_start(out=xcb[kb], in_=src[dt, :, j * TM:(j + 1) * TM]),
                             r=[xr(srcname, dt, j)], w=[rxcb[kb]], dsem=xcb_sem[kb])

                    for dt in range(4):
                        xcb_load(dt)
                    for grp in range(8):
                        wt, wr = wget2(wout_d[l, grp], 4096)
                        wv = wt.rearrange("p (c k n) -> p c k n", c=2, k=16)
                        bk = take_banks(1)[0]
                        for c2 in range(2):
                            for dc in range(16):
                                P.op('pe', lambda e, c2=c2, dc=dc, wv=wv, bk=bk: e.matmul(psum[bk][:, c2 * 256:(c2 + 1) * 256], lhsT=wv[:, c2, dc, :], rhs=xn[:, dc, :], start=(dc == 0), stop=(dc == 15)),
                                     r=[wr, rxn[dc]], pw=[bankres[bk]] if (c2 == 0 and dc == 0) else [], fw=[bankres[bk]] if (c2 == 1 and dc == 15) else [])
                        for c2 in range(2):
                            dt = grp * 2 + c2
                            kb = dt % 4
                            P.op('dve', lambda e, c2=c2, dt=dt, bk=bk, kb=kb: e.scalar_tensor_tensor(out=xcb[kb], in0=psum[bk][:, c2 * 256:(c2 + 1) * 256], scalar=derG(l, 1, dt), in1=xcb[kb], op0=ALU.mult, op1=ALU.add),
                                 r=[bankres[bk], rxcb[kb]] + dres, fw=[rxcb[kb]])
                            P.op('sp', lambda e, dt=dt, kb=kb, j=j: e.dma_start(out=dst[dt, :, j * TM:(j + 1) * TM], in_=xcb[kb]),
                                 r=[rxcb[kb]], w=[xr(dstname, dt, j)], dsem=xcb_sem[kb])
                            if dt + 4 < 16:
                                xcb_load(dt + 4)

            nsub = min(stage, 3 * L)
            barrier()
            k = 0
            cur_src, cur_name = xT_in, 'in'
            for grp in range(24):
                ada_group(0, grp)
            ada_derive(0, 0)
            for l in range(L):
                for i in range(3):
                    if k >= nsub:
                        break
                    last = (k == nsub - 1) and not DBG_YCAT
                    dst, dname = (yT_out, 'out') if last else (xs, 'xs')
                    if i == 1:
                        barrier()
                        mixer(l, cur_src, cur_name, dst, dname)
                        barrier()
                    else:
                        bg = []
                        if l == 0 and i == 0:
                            bg = [lambda grp=grp: ada_group(0, grp) for grp in range(24, 72)] + [lambda ii=ii: ada_derive(0, ii) for ii in (1, 2)]
                        elif i == 2 and l + 1 < L:
                            bg = [lambda grp=grp, l=l: ada_group(l + 1, grp) for grp in range(72)] + [lambda ii=ii, l=l: ada_derive(l + 1, ii) for ii in range(3)]
                        ffn(l, i, l * 2 + (0 if i == 0 else 1), cur_src, cur_name, dst, dname, bg=bg)
                    cur_src, cur_name = dst, dname
                    k += 1
            if DBG_YCAT:
                barrier()
                for dc in range(16):
                    P.op('sp', lambda e, dc=dc: e.dma_start(out=yT_out[dc], in_=xs[dc]), r=[xr('xs', dc, sl) for sl in range(8)], w=[res(f'fin{dc}')], dsem=dsem('fin'))
            barrier()
            return P, wrec

        _, wseq = run(True, [])
        P, _ = run(False, wseq)

        targets = {e: set() for e in ENGS}
        for e in ENGS:
            for (fn, deps, tok, dinc) in P.q[e]:
                for d in deps:
                    if d.sem is None:
                        targets[d.eng].add(d.idx)
        cnt = {}
        for e in ENGS:
            c = 0
            arr = []
            for idx in range(len(P.q[e])):
                if idx in targets[e]:
                    c += 1
                arr.append(c)
            cnt[e] = arr
        for e in ENGS:
            for idx, (fn, deps, tok, dinc) in enumerate(P.q[e]):
                assert not (fn is None and idx in targets[e])

        engobj = {'pe': nc.tensor, 'act': nc.scalar, 'dve': nc.vector, 'pool': nc.gpsimd, 'sp': nc.sync}

        def replay(e, eng):
            seen = {}
            for idx, (fn, deps, tok, dinc) in enumerate(P.q[e]):
                for d in deps:
                    if d.sem is None:
                        if d.eng == e and d.idx >= idx:
                            continue
                        s, v = sem_eng[d.eng], cnt[d.eng][d.idx]
                    else:
                        s, v = d.sem, d.val
                    if seen.get(id(s), 0) >= v:
                        continue
                    seen[id(s)] = v
                    eng.wait_ge(s.h, v)
                if fn is None:
                    continue
                ins = fn(eng)
                if tok.sem is not None:
                    ins.then_inc(tok.sem.h, dinc)
                elif idx in targets[e]:
                    ins.then_inc(sem_eng[e].h, 1)

        with nc.Block() as block:
            @block.tensor
            def _(eng):
                replay('pe', eng)

            @block.scalar
            def _(eng):
                replay('act', eng)

            @block.vector
            def _(eng):
                replay('dve', eng)

            @block.gpsimd
            def _(eng):
                replay('pool', eng)

            @block.sync
            def _(eng):
                replay('sp', eng)
    return nc


def _host_prep(inputs):
    f = np.float32
    g = {k: np.asarray(v, dtype=f) for k, v in inputs.items()}
    sh = {}
    ada = g['ada_w'].reshape(L, 16, 128, 144, 128).transpose(0, 3, 2, 1, 4)
    sh['adaw'] = np.ascontiguousarray(ada.reshape(L, 72, 2, 128, 16, 128).transpose(0, 1, 3, 2, 4, 5)).reshape(L, 72, 128, 4096)
    w13 = []
    w2 = []
    for l in range(L):
        for nm13, nm2 in (('ffn1_w13', 'ffn1_w2'), ('ffn2_w13', 'ffn2_w2')):
            a = g[nm13][l].reshape(16, 128, 2, NFC, 128).transpose(3, 1, 2, 0, 4)
            w13.append(np.ascontiguousarray(a).reshape(NFC, 128, 4096))
            b = g[nm2][l].reshape(NFC, 128, 16, 128).transpose(2, 1, 0, 3)
            w2.append(np.ascontiguousarray(b).reshape(16, 128, DFF))
    sh['w13'] = np.stack(w13)
    sh['w2'] = np.stack(w2)
    wi = g['w_in'].reshape(L, 16, 128, 36, 128).transpose(0, 3, 2, 1, 4)
    sh['winf'] = np.ascontiguousarray(wi.reshape(L, 18, 2, 128, 16, 128).transpose(0, 1, 3, 2, 4, 5)).reshape(L, 18, 128, 4096)
    wt = np.stack([g['w_in'][:, :, 1024:1536], g['w_in'][:, :, 2560:3072]], axis=1)
    wt = wt.reshape(L, 2, 2, 8, 128, 512).transpose(0, 1, 2, 4, 3, 5)
    sh['wint'] = np.ascontiguousarray(wt).reshape(L, 2, 2, 128, 4096)
    wo = g['w_out'].reshape(L, 16, 128, 16, 128).transpose(0, 3, 2, 1, 4)
    sh['wout'] = np.ascontiguousarray(wo.reshape(L, 8, 2, 128, 16, 128).transpose(0, 1, 3, 2, 4, 5)).reshape(L, 8, 128, 4096)
    sh['poolw'] = np.ascontiguousarray(g['pool_w'].transpose(2, 0, 1, 3)).reshape(128, L * 4 * 128)
    sh['sguwT'] = np.ascontiguousarray(g['sgu_w'].transpose(3, 0, 1, 2)).reshape(128, L * 4 * 128)
    cm = np.zeros((128, 128 * 3 + 512 + 2048), f)
    cm[:, 0:128] = np.eye(128, dtype=f)
    ss, tt = np.meshgrid(np.arange(128), np.arange(128), indexing='ij')
    cm[:, 128:256] = (ss <= tt).astype(f)
    kk, qq = np.meshgrid(np.arange(128), np.arange(256), indexing='ij')
    for r in range(2):
        cm[:, 384 + r * 256:384 + (r + 1) * 256] = np.where(r * 128 + kk <= qq, 0.0, -BIG).astype(f)
    oh = np.zeros((16, 16, 128), f)
    for b_ in range(16):
        oh[b_, b_, :] = 1.0
    cm[0:16, 896:2944] = oh.reshape(16, 2048)
    sh['cmat'] = cm
    bcv = np.concatenate([g['sgu_b'].reshape(-1), g['sgu_norm_g'].reshape(-1)])
    sh['bc'] = np.ascontiguousarray(np.broadcast_to(bcv[None, :], (128, NBC)))

    def ppl(v):
        v = np.asarray(v, f)
        lead = v.shape[:-1]
        n = v.shape[-1] // 128
        return v.reshape(*lead, n, 128).transpose(len(lead) + 1, *range(len(lead)), len(lead)).reshape(128, -1)

    in_maps = []
    for c in range(8):
        b, half = c // 2, c % 2
        pp = np.zeros((128, NPP), f)
        pp[:, PP_C:PP_C + 16] = ppl(g['c'][b])
        pp[:, PP_NORMG:PP_NORMG + L * 48] = ppl(g['norm_g'])
        pp[:, PP_POOLS:PP_POOLS + L * 4] = ppl(g['pool_scale'])
        pp[:, PP_QG:PP_QG + L] = g['q_norm_g'].T
        pp[:, PP_KG:PP_KG + L] = g['k_norm_g'].T
        pp[:, PP_CONVW:PP_CONVW + L * 12] = ppl(g['conv_w'])
        pp[:, PP_OUTG:PP_OUTG + L * 16] = ppl(g['out_norm_g'])
        pp[:, PP_HALF] = float(half)
        pp[:, PP_ADAB:PP_ADAB + L * 144] = ppl(g['ada_b'])
        tabs = np.zeros((8, 3, 16), f)
        prevb = 0.0 if half == 1 else -BIG
        for o in range(8):
            tabs[o, 0, 0:8] = prevb
            tabs[o, 0, 8:8 + o] = 0.0
            tabs[o, 0, 8 + o:] = -BIG
            tabs[o, 1, :] = 1.0
            tabs[o, 1, 8 + o] = 0.0
            tabs[o, 2, 0:8] = prevb
            tabs[o, 2, 8:8 + o + 1] = 0.0
            tabs[o, 2, 8 + o + 1:] = -BIG
        pp[:, PP_TABS:PP_TABS + 384] = tabs.reshape(1, -1)
        inv = np.zeros((4, 16), f)
        for gi in range(4):
            wn = 2 << gi
            for t in range(16):
                inv[gi, t] = 1.0 / (wn if half == 1 else min(t + 1, wn))
        pp[:, PP_INVC:PP_INVC + 64] = inv.reshape(1, -1)
        xT = np.ascontiguousarray(g['x'][b, half * TL:(half + 1) * TL, :].T).reshape(NDC, 128, TL)
        m = dict(sh)
        m['pp'] = pp
        m['xT'] = xT
        in_maps.append(m)
    return in_maps


_NC_CACHE = {}


def _run(inputs, stage=6):
    in_maps = _host_prep(inputs)
    if stage not in _NC_CACHE:
        _NC_CACHE[stage] = _mk_program(stage)
    nc = _NC_CACHE[stage]
    res = run_bass_kernel_spmd(nc, in_maps, core_ids=list(range(8)))
    out = np.empty((4, 4096, D), np.float32)
    for c in range(8):
        b, half = c // 2, c % 2
        yT = np.asarray(res.results[c]["yT"]).reshape(D, TL)
        out[b, half * TL:(half + 1) * TL, :] = yT.T
    return out


def kernel(**inputs):
    return _run(inputs, stage=6)
```

```python
import contextlib
import numpy as np
import concourse.bass as bass
import concourse.mybir as mybir
from concourse.bass_utils import run_bass_kernel_spmd

F32 = mybir.dt.float32
BF16 = mybir.dt.bfloat16
ALU = mybir.AluOpType
AF = mybir.ActivationFunctionType
AX = mybir.AxisListType

L = 2
D = 2048
NDC = 16
DFF = 5632
NFC = 44
TL = 2048
TF = 1024
TM = 256
NTM = TL // TM
EPS = 1e-6
BIG = 1.0e30
WSLOT = 5632
NW = 3
ENGS = ('pe', 'act', 'dve', 'pool', 'sp')
DBG_YCAT = False

PP_C = 0
PP_NORMG = PP_C + 16
PP_POOLS = PP_NORMG + L * 3 * 16
PP_QG = PP_POOLS + L * 4
PP_KG = PP_QG + L
PP_CONVW = PP_KG + L
PP_OUTG = PP_CONVW + L * 12
PP_HALF = PP_OUTG + L * 16
PP_ADAB = PP_HALF + 1
PP_TABS = PP_ADAB + L * 144
PP_INVC = PP_TABS + 8 * 3 * 16
NPP = PP_INVC + 64
BC_SGUB = 0
BC_SGUG = BC_SGUB + L * 4 * 128
NBC = BC_SGUG + L * 512


class Sem:
    def __init__(self, h):
        self.h = h
        self.count = 0


class Tok:
    __slots__ = ('eng', 'idx', 'sem', 'val')

    def __init__(self, eng=None, idx=None, sem=None, val=None):
        self.eng, self.idx, self.sem, self.val = eng, idx, sem, val

    def key(self):
        return self.eng if self.sem is None else id(self.sem)

    def order(self):
        return self.idx if self.sem is None else self.val


def _merge(d, t):
    k = t.key()
    o = d.get(k)
    if o is None or o.order() < t.order():
        d[k] = t


class Res:
    def __init__(self):
        self.w = {}
        self.r = {}


class Prog:
    def __init__(self, dry):
        self.dry = dry
        self.q = {e: [] for e in ENGS}
        self.pending_dma = {}

    def op(self, eng, fn, r=(), w=(), pw=(), fw=(), dsem=None, dinc=16, extra=()):
        if self.dry:
            return None
        deps = {}
        for t in extra:
            if t is not None:
                _merge(deps, t)
        for x in r:
            for t in x.w.values():
                _merge(deps, t)
        for x in list(w) + list(pw):
            for t in x.w.values():
                _merge(deps, t)
            for t in x.r.values():
                _merge(deps, t)
        if dsem is not None:
            dsem.count += dinc
            tok = Tok(sem=dsem, val=dsem.count)
            self.pending_dma[id(dsem)] = tok
        else:
            tok = Tok(eng=eng, idx=len(self.q[eng]))
        for x in r:
            _merge(x.r, tok)
        for x in w:
            x.w = {tok.key(): tok}
            x.r = {}
        for x in pw:
            x.w = {}
            x.r = {}
        for x in fw:
            x.w = {tok.key(): tok}
            x.r = {}
        self.q[eng].append((fn, list(deps.values()), tok, dinc))
        return tok


def _mk_program(stage):
    nc = bass.Bass("TRN2", target_bir_lowering=False)
    dr = {}

    def din(name, shape):
        dr[name] = nc.dram_tensor(name, list(shape), F32, kind="ExternalInput").ap()
        return dr[name]

    xT_in = din("xT", [NDC, 128, TL])
    pp_d = din("pp", [128, NPP])
    bc_d = din("bc", [128, NBC])
    cm_d = din("cmat", [128, 128 * 3 + 2 * 256 + 16 * 128])
    poolw_d = din("poolw", [128, L * 4 * 128])
    sguw_d = din("sguwT", [128, L * 4 * 128])
    ada_d = din("adaw", [L, 72, 128, 2 * 2048])
    w13_d = din("w13", [L * 2, NFC, 128, 4096])
    w2_d = din("w2", [L * 2, 16, 128, DFF])
    winf_d = din("winf", [L, 18, 128, 2 * 2048])
    wint_d = din("wint", [L, 2, 2, 128, 8 * 512])
    wout_d = din("wout", [L, 8, 128, 2 * 2048])
    yT_out = nc.dram_tensor("yT", [NDC, 128, TL], F32, kind="ExternalOutput").ap()
    xs = nc.dram_tensor("xs", [NDC, 128, TL], F32).ap()
    gk_i = nc.dram_tensor("gk_i", [512, 2048], BF16).ap()
    gk_o = nc.dram_tensor("gk_o", [1024, 2048], BF16).ap()
    gv_i = nc.dram_tensor("gv_i", [512, 2048], BF16).ap()
    gv_o = nc.dram_tensor("gv_o", [1024, 2048], BF16).ap()
    gh_i = nc.dram_tensor("gh_i", [128, 2048], F32).ap()
    gh_o = nc.dram_tensor("gh_o", [256, 2048], F32).ap()
    wsc = nc.dram_tensor("wsc", [L, 22, 128, 4096], BF16).ap()

    es = contextlib.ExitStack()
    with es:
        def sb(name, shape, dt):
            return es.enter_context(nc.sbuf_tensor("sb_" + name, list(shape), dt))

        def newsem(name):
            return Sem(es.enter_context(nc.semaphore(name)))

        pp = sb("pp", [128, NPP], F32)
        bc = sb("bc", [128, NBC], F32)
        modpp = sb("modpp", [128, L * 144], F32)
        der = sb("der", [128, L * 3 * 16 * 3], F32)
        misc = sb("misc", [128, 64], F32)
        ident_f = sb("ident_f", [128, 128], F32)
        trilT = sb("trilT", [128, 128], F32)
        ident_b = sb("ident_b", [128, 128], BF16)
        causal_b = sb("causal_b", [128, 2, 256], BF16)
        onehot_b = sb("onehot_b", [128, 16, 128], BF16)
        ones_b = sb("ones_b", [128, 4, 128], BF16)
        poolw_b = sb("poolw_b", [128, L * 4, 128], BF16)
        sguw_b = sb("sguw_b", [128, L * 4, 128], BF16)
        siluc_b = sb("siluc_b", [128, 16], BF16)
        wbuf = sb("wbuf", [128, NW, WSLOT], BF16)
        psum = [es.enter_context(nc.psum_tensor(f"ps{k}", [128, 512], F32)) for k in range(8)]

        PH = int(nc.sbuf_bytes_remaining) // 4 - 16
        arena = sb("arena", [128, PH], F32)
        cstage = arena[:, 0:2560]
        sguw_f = arena[:, 2560:2560 + L * 4 * 128].rearrange("p (a b) -> p a b", a=L * 4)
        sem_eng = {e: newsem(f"s_{e}") for e in ENGS}
        wsem = [newsem(f"w{k}") for k in range(NW)]
        dsems = {}

        def dsem(name):
            if name not in dsems:
                dsems[name] = newsem("d_" + name)
            return dsems[name]

        def run(dry, wseq):
            P = Prog(dry)
            R = {}

            def res(name):
                if name not in R:
                    R[name] = Res()
                return R[name]

            bankres = [Res() for _ in range(8)]
            bank_ptr = [0]

            reserved = set()

            def take_banks(n):
                out = []
                while len(out) < n:
                    if bank_ptr[0] not in reserved:
                        out.append(bank_ptr[0])
                    bank_ptr[0] = (bank_ptr[0] + 1) % 8
                return out

            xres = {}

            def xr(buf, dc, slab):
                k = (buf, dc, slab)
                if k not in xres:
                    xres[k] = Res()
                return xres[k]

            wrec = []
            wstate = {'next_issue': 0, 'next_get': 0}
            wslot_res = [Res() for _ in range(NW)]

            def w_issue_upto(n):
                while wstate['next_issue'] < min(n, len(wseq)):
                    i = wstate['next_issue']
                    src, nel, rkey = wseq[i]
                    k = i % NW
                    P.op('pool', lambda e, src=src, k=k, nel=nel: e.dma_start(out=wbuf[:, k, 0:nel], in_=src),
                         r=[res(rkey)] if rkey else [], w=[wslot_res[k]], dsem=wsem[k])
                    wstate['next_issue'] += 1

            def wget(src, nel, rkey=None):
                i = wstate['next_get']
                wstate['next_get'] += 1
                wrec.append((src, nel, rkey))
                wstate['last_k'] = i % NW
                if dry:
                    return wbuf[:, 0, 0:nel], wslot_res[0]
                w_issue_upto(i + NW)
                k = i % NW
                return wbuf[:, k, 0:nel], wslot_res[k]

            def af(off, n):
                return arena[:, off:off + n]

            def ab(off, n):
                return arena[:, off:off + n].bitcast(BF16)

            r_pp, r_bc, r_c = res('pp'), res('bc'), res('consts')
            P.op('sp', lambda e: e.dma_start(out=pp[:], in_=pp_d), w=[r_pp], dsem=dsem('pp'))
            P.op('sp', lambda e: e.dma_start(out=bc[:], in_=bc_d), w=[r_bc], dsem=dsem('bc'))
            r_cst = res('cstage')
            P.op('sp', lambda e: e.dma_start(out=ident_f[:], in_=cm_d[:, 0:128]), w=[res('ident_f')], dsem=dsem('c1'))
            P.op('sp', lambda e: e.dma_start(out=trilT[:], in_=cm_d[:, 128:256]), w=[res('trilT')], dsem=dsem('c2'))
            P.op('sp', lambda e: e.dma_start(out=cstage[:, 0:512], in_=cm_d[:, 384:896]), w=[r_cst], dsem=dsem('c3'))
            P.op('sp', lambda e: e.dma_start(out=cstage[0:16, 512:2560], in_=cm_d[0:16, 896:2944]), w=[res('cst2')], dsem=dsem('c4'))
            P.op('sp', lambda e: e.dma_start(out=arena[:, 2560:2560 + L * 4 * 128], in_=sguw_d), w=[res('sguw_f')], dsem=dsem('c5'))
            P.op('pool', lambda e: e.dma_start(out=poolw_b[:].rearrange("p a b -> p (a b)"), in_=poolw_d), w=[res('poolw_b')], dsem=dsem('c6'))
            P.op('dve', lambda e: e.tensor_copy(out=ident_b[:], in_=ident_f[:]), r=[res('ident_f')], w=[res('ident_b')])
            P.op('dve', lambda e: e.tensor_copy(out=causal_b[:].rearrange("p a b -> p (a b)"), in_=cstage[:, 0:512]), r=[r_cst], w=[res('causal_b')])
            P.op('dve', lambda e: e.memset(onehot_b[:].rearrange("p a b -> p (a b)"), 0.0), w=[res('onehot_b')])
            P.op('dve', lambda e: e.tensor_copy(out=onehot_b[0:16].rearrange("p a b -> p (a b)"), in_=cstage[0:16, 512:2560]), r=[res('cst2'), res('onehot_b')], w=[res('onehot_b')])
            for k, v in enumerate((1.0 / 2048, 1.0 / 128, 1.0 / 512, 1.0)):
                P.op('dve', lambda e, k=k, v=v: e.memset(ones_b[:, k, :], v), w=[res(f'ones{k}')])
            P.op('dve', lambda e: e.tensor_tensor(out=sguw_b[:], in0=sguw_f, in1=trilT[:].unsqueeze(1).to_broadcast([128, L * 4, 128]), op=ALU.mult),
                 r=[res('sguw_f'), res('trilT')], w=[res('sguw_b')])
            P.op('act', lambda e: e.activation(out=siluc_b[:], in_=pp[:, PP_C:PP_C + 16], func=AF.Silu), r=[r_pp], w=[res('siluc')])
            P.op('dve', lambda e: e.tensor_scalar(out=misc[:, 0:L], in0=pp[:, PP_QG:PP_QG + L], scalar1=float(128 ** -0.5), scalar2=None, op0=ALU.mult),
                 r=[r_pp], w=[res('qgs')])

            def ones(k):
                return ones_b[:, k, :], res(f'ones{k}')

            P.op('dve', lambda e: e.memset(misc[:, 8:9], EPS), w=[res('eps')])

            def rsqrt(out_ap, in_ap, in_res, out_res, prescale=1.0):
                P.op('act', lambda e: e.activation(out=out_ap, in_=in_ap, func=AF.Sqrt, bias=misc[0:out_ap.shape[0], 8:9], scale=prescale),
                     r=list(in_res) + [res('eps')], w=[out_res])
                P.op('dve', lambda e: e.reciprocal(out=out_ap, in_=out_ap), r=[out_res], w=[out_res])

            def ada_group(l, grp):
                wt, wr = wget(ada_d[l, grp], 4096)
                wv = wt.rearrange("p (c k n) -> p c k n", c=2, k=16)
                bk = take_banks(1)[0]
                for c2 in range(2):
                    for dc in range(16):
                        P.op('pe', lambda e, dc=dc, wv=wv, c2=c2, bk=bk: e.matmul(psum[bk][:, c2:c2 + 1], lhsT=wv[:, c2, dc, :], rhs=siluc_b[:, dc:dc + 1], start=(dc == 0), stop=(dc == 15)),
                             r=[wr, res('siluc')], pw=[bankres[bk]] if (c2 == 0 and dc == 0) else [], fw=[bankres[bk]] if (c2 == 1 and dc == 15) else [])
                o = l * 144 + grp * 2
                P.op('dve', lambda e, o=o, bk=bk: e.tensor_tensor(out=modpp[:, o:o + 2], in0=psum[bk][:, 0:2], in1=pp[:, PP_ADAB + o:PP_ADAB + o + 2], op=ALU.add),
                     r=[bankres[bk], r_pp], w=[res(f'mod{l}_{grp}')])

            def ada_derive(l, i):
                mo = l * 144
                modres = [res(f'mod{l}_{g}') for g in range(i * 24, (i + 1) * 24)]
                o = (l * 3 + i) * 48
                sh = mo + (i * 3 + 0) * 16
                sc = mo + (i * 3 + 1) * 16
                gt = mo + (i * 3 + 2) * 16
                ng = PP_NORMG + (l * 3 + i) * 16
                P.op('dve', lambda e: e.scalar_tensor_tensor(out=der[:, o:o + 16], in0=modpp[:, sc:sc + 16], scalar=1.0, in1=pp[:, ng:ng + 16], op0=ALU.add, op1=ALU.mult),
                     r=modres + [r_pp], w=[res(f'derA{l}{i}')])
                P.op('dve', lambda e: e.tensor_copy(out=der[:, o + 16:o + 32], in_=modpp[:, sh:sh + 16]),
                     r=modres, w=[res(f'derB{l}{i}')])
                gm = 1.0 if i == 1 else 0.5
                P.op('dve', lambda e: e.tensor_scalar(out=der[:, o + 32:o + 48], in0=modpp[:, gt:gt + 16], scalar1=gm, scalar2=None, op0=ALU.mult),
                     r=modres, w=[res(f'derG{l}{i}')])

            def derA(l, i, dc):
                o = (l * 3 + i) * 48
                return der[:, o + dc:o + dc + 1]

            def derB(l, i, dc):
                o = (l * 3 + i) * 48 + 16
                return der[:, o + dc:o + dc + 1]

            def derG(l, i, dc):
                o = (l * 3 + i) * 48 + 32
                return der[:, o + dc:o + dc + 1]

            def der_res(l, i):
                return [res(f'derA{l}{i}'), res(f'derB{l}{i}'), res(f'derG{l}{i}')]

            def ffn(l, i, widx, src, srcname, dst, dstname, bg=()):
                bg = list(bg)
                XN = 0
                HH = 8192
                XC = HH + 22528
                RS = XC + 3072
                SQ = RS + 1024
                ST = SQ + 1024
                TMP = ST + 1024
                assert TMP + 1024 <= PH, (TMP + 1024, PH)
                xn = ab(XN, 8192).rearrange("p (c t) -> p c t", c=16)
                hh = ab(HH, 22528).rearrange("p (c t) -> p c t", c=NFC)
                xc = [af(XC + k * 1024, 1024) for k in range(3)]
                xc_res = [res(f'xc{k}') for k in range(3)]
                xc_sem = [dsem(f'xc{k}') for k in range(3)]
                rstd = af(RS, 1024)
                sq = [ab(SQ + k * 512, 512) for k in range(2)]
                stt = [af(ST + k * 512, 512) for k in range(2)]
                tmp = [af(TMP + k * 512, 512) for k in range(2)]
                xci = [0]
                dres = der_res(l, i)
                o1, o1r = ones(0)

                def xload(dc, t0):
                    k = xci[0] % 3
                    xci[0] += 1
                    slabs = [xr(srcname, dc, (t0 // TM) + s) for s in range(TF // TM)]
                    P.op('sp', lambda e, k=k, dc=dc, t0=t0: e.dma_start(out=xc[k], in_=src[dc, :, t0:t0 + TF]),
                         r=slabs, w=[xc_res[k]], dsem=xc_sem[k])
                    return k

                def mn_p1_load(dc, t0, srcbuf=None):
                    if srcbuf is None:
                        k = xload(dc, t0)
                        xa, xar = xc[k], xc_res[k]
                    else:
                        xa, xar = srcbuf
                    s_ = dc % 2
                    P.op('act', lambda e, xa=xa, s_=s_: e.activation(out=sq[s_], in_=xa, func=AF.Square),
                         r=[xar], w=[res(f'sq{s_}')])

                def mn_p1_mm(dc, sb_):
                    s_ = dc % 2
                    for hf in range(2):
                        P.op('pe', lambda e, s_=s_, hf=hf, dc=dc, b=sb_[hf]: e.matmul(psum[b][:, :], lhsT=o1, rhs=sq[s_][:, hf * 512:(hf + 1) * 512], start=(dc == 0), stop=(dc == 15)),
                             r=[res(f'sq{s_}'), o1r], pw=[bankres[sb_[hf]]] if dc == 0 else [], fw=[bankres[sb_[hf]]] if dc == 15 else [])

                def mn_rsqrt(sb_):
                    for hf in range(2):
                        rsqrt(rstd[:, hf * 512:(hf + 1) * 512], psum[sb_[hf]][:, :], [bankres[sb_[hf]]], res(f'rstd{hf}'))

                def mn_pass2_step(dc, t0, srcbuf=None):
                    if srcbuf is None:
                        k = xload(dc, t0)
                        xa, xar = xc[k], xc_res[k]
                    else:
                        xa, xar = srcbuf
                    for hf in range(2):
                        P.op('dve', lambda e, xa=xa, hf=hf, dc=dc: e.scalar_tensor_tensor(out=tmp[hf], in0=xa[:, hf * 512:(hf + 1) * 512], scalar=derA(l, i, dc), in1=rstd[:, hf * 512:(hf + 1) * 512], op0=ALU.mult, op1=ALU.mult),
                             r=[xar, res(f'rstd{hf}')] + dres, w=[res(f'tmp{hf}')])
                        P.op('act', lambda e, hf=hf, dc=dc: e.activation(out=xn[:, dc, hf * 512:(hf + 1) * 512], in_=tmp[hf], func=AF.Identity, bias=derB(l, i, dc), scale=1.0),
                             r=[res(f'tmp{hf}')] + dres, w=[res(f'xn{dc}')] if hf == 1 else [], pw=[res(f'xn{dc}')] if hf == 0 else [])

                NT = TL // TF
                for tt in range(NT):
                    t0 = tt * TF
                    if tt == 0:
                        xst = af(HH, 16384).rearrange("p (c t) -> p c t", c=16)
                        hh_all = [res(f'hh{fc}_{hf}') for fc in range(NFC) for hf in range(2)]
                        for dc in range(16):
                            slabs0 = [xr(srcname, dc, s4) for s4 in range(TF // TM)]
                            P.op('sp', lambda e, dc=dc: e.dma_start(out=xst[:, dc, :], in_=src[dc, :, 0:TF]),
                                 r=slabs0, w=[res(f'xst{dc}')] + (hh_all if dc == 0 else []), dsem=dsem(f'xst{dc}'))
                        sb_ = take_banks(2)
                        for dc in range(16):
                            mn_p1_load(dc, t0, (xst[:, dc, :], res(f'xst{dc}')))
                            mn_p1_mm(dc, sb_)
                        mn_rsqrt(sb_)
                        for dc in range(16):
                            mn_pass2_step(dc, t0, (xst[:, dc, :], res(f'xst{dc}')))
                    for fc in range(NFC):
                        wt, wr = wget(w13_d[widx, fc], 4096)
                        wv = wt.rearrange("p (a k n) -> p a k n", a=2, k=16)
                        bk = take_banks(4)
                        for dc in range(16):
                            for a_ in range(2):
                                for hf in range(2):
                                    b = bk[a_ * 2 + hf]
                                    P.op('pe', lambda e, a_=a_, hf=hf, dc=dc, b=b, wv=wv: e.matmul(psum[b][:, :], lhsT=wv[:, a_, dc, :], rhs=xn[:, dc, hf * 512:(hf + 1) * 512], start=(dc == 0), stop=(dc == 15)),
                                         r=[wr, res(f'xn{dc}')], pw=[bankres[b]] if dc == 0 else [], fw=[bankres[b]] if dc == 15 else [])
                        for hf in range(2):
                            P.op('act', lambda e, hf=hf, b=bk[hf]: e.activation(out=stt[hf], in_=psum[b][:, :], func=AF.Silu),
                                 r=[bankres[bk[hf]]], w=[res(f'stt{hf}')])
                            P.op('dve', lambda e, hf=hf, b=bk[2 + hf], fc=fc: e.tensor_tensor(out=hh[:, fc, hf * 512:(hf + 1) * 512], in0=psum[b][:, :], in1=stt[hf], op=ALU.mult),
                                 r=[bankres[bk[2 + hf]], res(f'stt{hf}')], w=[res(f'hh{fc}_{hf}')])
                        if bg:
                            bg.pop(0)()
                    nxt = tt + 1 < NT
                    if nxt:
                        sbn = take_banks(2)
                        reserved.update(sbn)
                    for dt in range(16):
                        wt, wr = wget(w2_d[widx, dt], DFF)
                        wv = wt.rearrange("p (f n) -> p f n", f=NFC)
                        bk = take_banks(2)
                        k = xload(dt, t0)
                        for fc in range(NFC):
                            for hf in range(2):
                                b = bk[hf]
                                P.op('pe', lambda e, hf=hf, fc=fc, b=b, wv=wv: e.matmul(psum[b][:, :], lhsT=wv[:, fc, :], rhs=hh[:, fc, hf * 512:(hf + 1) * 512], start=(fc == 0), stop=(fc == NFC - 1)),
                                     r=[wr, res(f'hh{fc}_{hf}')], pw=[bankres[b]] if fc == 0 else [], fw=[bankres[b]] if fc == NFC - 1 else [])
                        if nxt and 1 <= dt <= 8:
                            mn_p1_mm(2 * (dt - 1), sbn)
                            mn_p1_mm(2 * (dt - 1) + 1, sbn)
                            if dt == 8:
                                mn_rsqrt(sbn)
                                reserved.difference_update(sbn)
                        for hf in range(2):
                            P.op('dve', lambda e, hf=hf, b=bk[hf], k=k, dt=dt: e.scalar_tensor_tensor(out=xc[k][:, hf * 512:(hf + 1) * 512], in0=psum[b][:, :], scalar=derG(l, i, dt), in1=xc[k][:, hf * 512:(hf + 1) * 512], op0=ALU.mult, op1=ALU.add),
                                 r=[bankres[bk[hf]], xc_res[k]] + dres, fw=[xc_res[k]] if hf == 1 else [])
                        slabs = [xr(dstname, dt, (t0 // TM) + s) for s in range(TF // TM)]
                        P.op('sp', lambda e, k=k, dt=dt, t0=t0: e.dma_start(out=dst[dt, :, t0:t0 + TF], in_=xc[k]),
                             r=[xc_res[k]], w=slabs, dsem=xc_sem[k])
                        if nxt:
                            if dt < 8:
                                mn_p1_load(2 * dt, t0 + TF)
                                mn_p1_load(2 * dt + 1, t0 + TF)
                            else:
                                mn_pass2_step(2 * (dt - 8), t0 + TF)
                                mn_pass2_step(2 * (dt - 8) + 1, t0 + TF)
                while bg:
                    bg.pop(0)()

            def barrier():
                if dry:
                    return
                toks = []
                for e in ENGS:
                    for ent in reversed(P.q[e]):
                        if ent[0] is not None:
                            if ent[2].sem is None:
                                toks.append(ent[2])
                            break
                toks += list(P.pending_dma.values())
                for e in ENGS:
                    P.op(e, None, extra=toks)

            def mixer(l, src, srcname, dst, dstname):
                KL, KP, VL, VP = 0, 4096, 8192, 12288
                XT = 16384
                XN = XT + 4096
                YC = XN + 2048
                RS = YC + 4096
                TP = RS + 1024
                T2 = TP + 768
                O_TQ, O_TRS = T2 + 0, T2 + 128
                O_SM, O_M8, O_SEL, O_SS4, O_JUNK, O_T16, O_TG, O_HST, O_HPV = T2 + 384, T2 + 448, T2 + 480, T2 + 544, T2 + 552, T2 + 680, T2 + 696, T2 + 712, T2 + 840
                O_P0B = T2 + 968
                O_ZB = O_P0B + 1088
                O_G = O_ZB + 1040
                O_QT = O_G + 2176
                O_PT = O_QT + 512
                O_QF = O_PT + 512
                O_MSK = O_QF + 1024
                O_DD = O_MSK + 512
                KM = O_DD + 512
                HL = KM + 64
                END = HL + 128 + 1024 + 256
                assert END <= PH, (END, PH)
                Kl = ab(KL, 4096).rearrange("p (h t) -> p h t", h=4)
                Kp = ab(KP, 4096).rearrange("p (h t) -> p h t", h=4)
                Vl = ab(VL, 4096).rearrange("p (c n) -> p c n", c=16)
                Vp = ab(VP, 4096).rearrange("p (c n) -> p c n", c=16)
                xt = af(XT, 4096).rearrange("p (c t) -> p c t", c=16)
                xn = ab(XN, 2048).rearrange("p (c t) -> p c t", c=16)
                sqy = ab(YC, 2048).rearrange("p (c t) -> p c t", c=16)
                yc = af(YC, 4096).rearrange("p (c t) -> p c t", c=16)
                rs4 = af(RS, 1024).rearrange("p (c t) -> p c t", c=4)
                kmT = af(KM, 64).rearrange("p (h n) -> p h n", h=4)
                p0h = af(HL, 64).rearrange("p (g t) -> p g t", g=4)
                zh = af(HL + 64, 64).rearrange("p (g t) -> p g t", g=4)
                t_rstd = af(TP, 256)
                t_tmp = [af(TP + 256 + k * 256, 256) for k in range(2)]
                tq = ab(O_TQ, 128)
                trs = af(O_TRS, 256)
                sm = af(O_SM, 64).rearrange("p (h n) -> p h n", h=4)
                m8 = af(O_M8, 32).rearrange("p (h n) -> p h n", h=4)
                sel = af(O_SEL, 64).rearrange("p (h n) -> p h n", h=4)
                ss4 = af(O_SS4, 4)
                junk = af(O_JUNK, 128)
                t16 = af(O_T16, 16)
                tg = af(O_TG, 16)
                hst = af(O_HST, 128).rearrange("p (a g t) -> p a g t", a=2, g=4)
                hprev = af(O_HPV, 128)
                p0b = af(O_P0B, 1088).rearrange("p (g t) -> p g t", g=4)
                zb = af(O_ZB, 1040).rearrange("p (g t) -> p g t", g=4)
                G = [af(O_G + k * 544, 544) for k in range(4)]
                rG = [res(f'G{k}') for k in range(4)]
                qT = ab(O_QT, 512).rearrange("p (h t) -> p h t", h=4)
                pT = [ab(O_PT + k * 256, 256).rearrange("p (a t) -> p a t", a=2) for k in range(2)] + [ab(HL + 128 + 1024, 256).rearrange("p (a t) -> p a t", a=2)]
                qf = af(O_QF, 1024).rearrange("p (h t) -> p h t", h=4)
                mskT = ab(O_MSK, 512).rearrange("p (h t) -> p h t", h=4)
                for h in range(4):
                    P.op('dve', lambda e, h=h: e.memset(mskT[:, h, :], 0.0), w=[res(f'mskT{h}')])
                dres = der_res(l, 1)
                o2048, o2048r = ones(0)
                o128, o128r = ones(1)
                o512, o512r = ones(2)
                o1, o1r = ones(3)
                r_xt = res('m_xt')
                rxn = [res(f'm_xn{dc}') for dc in range(16)]
                ryc = [res(f'yc{c}') for c in range(16)]
                xt_sem = dsem('m_xt')

                def load_xt(j):
                    slabs = [xr(srcname, dc, j) for dc in range(16)]
                    P.op('sp', lambda e, j=j: e.dma_start(out=xt, in_=src[:, :, j * TM:(j + 1) * TM].rearrange("c p t -> p c t")),
                         r=slabs, w=[r_xt], dsem=xt_sem)

                def modnorm_A():
                    bk = take_banks(1)[0]
                    P.op('act', lambda e: e.activation(out=sqy, in_=xt, func=AF.Square), r=[r_xt], w=ryc)
                    for dc in range(16):
                        P.op('pe', lambda e, dc=dc: e.matmul(psum[bk][:, 0:256], lhsT=o2048, rhs=sqy[:, dc, :], start=(dc == 0), stop=(dc == 15)),
                             r=ryc + [o2048r], pw=[bankres[bk]] if dc == 0 else [], fw=[bankres[bk]] if dc == 15 else [])
                    rsqrt(t_rstd, psum[bk][:, 0:256], [bankres[bk]], res('m_rstd'))
                    for q4 in range(4):
                        P.op('dve', lambda e, q4=q4: e.tensor_tensor(out=xt[:, q4 * 4:(q4 + 1) * 4, :], in0=xt[:, q4 * 4:(q4 + 1) * 4, :], in1=t_rstd.unsqueeze(1).to_broadcast([128, 4, 256]), op=ALU.mult),
                             r=[r_xt, res('m_rstd')], w=[res(f'm_xs{q4}')])

                def modnorm_B():
                    for dc in range(16):
                        q4 = dc // 4
                        if True:
                            P.op('act', lambda e, dc=dc: e.activation(out=xn[:, dc, :], in_=xt[:, dc, :], func=AF.Identity, bias=derB(l, 1, dc), scale=derA(l, 1, dc)),
                                 r=[res(f'm_xs{q4}'), r_xt] + dres, w=[rxn[dc]])
                        else:
                            P.op('dve', lambda e, dc=dc: e.tensor_scalar(out=xn[:, dc, :], in0=xt[:, dc, :], scalar1=derA(l, 1, dc), scalar2=derB(l, 1, dc), op0=ALU.mult, op1=ALU.add),
                                 r=[res(f'm_xs{q4}'), r_xt] + dres, w=[rxn[dc]])

                def proj_fm(chunks, N=256, c0=0, wg=None):
                    wg = wg or wget
                    out = []
                    for gi in range(0, len(chunks), 2):
                        cc = chunks[gi]
                        assert cc % 2 == 0 and chunks[gi + 1] == cc + 1
                        wt, wr = wg(winf_d[l, cc // 2], 4096)
                        wv = wt.rearrange("p (c k n) -> p c k n", c=2, k=16)
                        bk = take_banks(1)[0]
                        for c2 in range(2):
                            for dc in range(16):
                                P.op('pe', lambda e, c2=c2, dc=dc, wv=wv, bk=bk: e.matmul(psum[bk][:, c2 * 256:c2 * 256 + N], lhsT=wv[:, c2, dc, :], rhs=xn[:, dc, c0:c0 + N], start=(dc == 0), stop=(dc == 15)),
                                     r=[wr, rxn[dc]], pw=[bankres[bk]] if (c2 == 0 and dc == 0) else [], fw=[bankres[bk]] if (c2 == 1 and dc == 15) else [])
                            out.append((bk, c2 * 256))
                    return out

                def proj_tm(which, wg=None):
                    wg = wg or wget
                    bks = take_banks(2)
                    for half in range(2):
                        wt, wr = wg(wint_d[l, which, half], 4096)
                        wv = wt.rearrange("p (k n) -> p k n", k=8)
                        for s_ in range(2):
                            for k8 in range(8):
                                dc = half * 8 + k8
                                P.op('pe', lambda e, s_=s_, k8=k8, dc=dc, wv=wv, b=bks[s_]: e.matmul(psum[b][:, :], lhsT=xn[:, dc, s_ * 128:(s_ + 1) * 128], rhs=wv[:, k8, :], start=(dc == 0), stop=(dc == 15)),
                                     r=[wr, rxn[dc]], pw=[bankres[bks[s_]]] if dc == 0 else [], fw=[bankres[bks[s_]]] if dc == 15 else [])
                    return bks

                def headnorm(bk, off, gain_ap, gain_res, dst_b, dst_res, dst_f, dst_f_res):
                    sb_ = take_banks(1)[0]
                    P.op('act', lambda e: e.activation(out=tq, in_=psum[bk][:, off:off + 256], func=AF.Square),
                         r=[bankres[bk]], w=[res('hn_sq')])
                    P.op('pe', lambda e: e.matmul(psum[sb_][:, 0:256], lhsT=o128, rhs=tq, start=True, stop=True),
                         r=[res('hn_sq'), o128r], w=[bankres[sb_]])
                    rsqrt(trs, psum[sb_][:, 0:256], [bankres[sb_]], res('hn_rs'))
                    P.op('dve', lambda e: e.scalar_tensor_tensor(out=dst_f, in0=psum[bk][:, off:off + 256], scalar=gain_ap, in1=trs, op0=ALU.mult, op1=ALU.mult),
                         r=[bankres[bk], res('hn_rs')] + gain_res, w=[dst_f_res])
                    P.op('act', lambda e: e.activation(out=dst_b, in_=dst_f, func=AF.Copy), r=[dst_f_res], w=[dst_res])

                rqf = [res(f'qf{h}') for h in range(4)]

                def headnorm4(outs, gain_ap, gain_res, dst_b, dst_res):
                    tqs = [pT[h // 2][:, h % 2, 0:256] for h in range(4)]
                    tqr = [res(f'pT{h // 2}') for h in range(4)]
                    sbs = []
                    for h in range(4):
                        bk, off = outs[h]
                        P.op('act', lambda e, h=h, bk=bk, off=off: e.activation(out=tqs[h], in_=psum[bk][:, off:off + 256], func=AF.Square),
                             r=[bankres[bk]], w=[res(f'tq{h}')], pw=[tqr[h]] if h % 2 == 0 else [])
                    for h in range(4):
                        sb_ = take_banks(1)[0]
                        sbs.append(sb_)
                        P.op('pe', lambda e, h=h, sb_=sb_: e.matmul(psum[sb_][:, 0:256], lhsT=o128, rhs=tqs[h], start=True, stop=True),
                             r=[res(f'tq{h}'), o128r], w=[bankres[sb_]])
                    for h in range(4):
                        rsqrt(qf[:, h, :], psum[sbs[h]][:, 0:256], [bankres[sbs[h]]], rqf[h])
                    for h in range(4):
                        bk, off = outs[h]
                        P.op('dve', lambda e, h=h, bk=bk, off=off: e.scalar_tensor_tensor(out=qf[:, h, :], in0=psum[bk][:, off:off + 256], scalar=gain_ap, in1=qf[:, h, :], op0=ALU.mult, op1=ALU.mult),
                             r=[bankres[bk], rqf[h]] + gain_res, w=[rqf[h]])
                    for h in range(4):
                        P.op('act', lambda e, h=h: e.activation(out=dst_b[h], in_=qf[:, h, :], func=AF.Copy), r=[rqf[h]], w=[dst_res[h]])

                vst = [G[2][:, 0:512], G[3][:, 0:512]]
                gv_iv = gv_i.rearrange("r c -> (r c)").rearrange("(t n) -> t n", n=512)
                load_xt(0)
                modnorm_A()
                for j in range(NTM):
                    modnorm_B()
                    load_xt(j + 1 if j + 1 < NTM else 0)
                    outs = proj_fm([16, 17, 18, 19])
                    modnorm_A()
                    headnorm4(outs, pp[:, PP_KG + l:PP_KG + l + 1], [r_pp], [Kl[:, h, j * TM:(j + 1) * TM] for h in range(4)], [res(f'm_Kl_{h}_{j}') for h in range(4)])
                    P.op('sp', lambda e, j=j: e.dma_start(out=gk_i[:, j * TM:(j + 1) * TM].rearrange("(h p) t -> p h t", p=128), in_=Kl[:, :, j * TM:(j + 1) * TM]),
                         r=[res(f'm_Kl_{h}_{j}') for h in range(4)], w=[res(f'gin_k{j}')], dsem=dsem('m_kst'))
                    bks = proj_tm(1)
                    for s_ in range(2):
                        P.op('act', lambda e, s_=s_, j=j, b=bks[s_]: e.activation(out=Vl[:, j * 2 + s_, :], in_=psum[b][:, :], func=AF.Copy),
                             r=[bankres[bks[s_]]], w=[res(f'm_Vl_{j}_{s_}')])
                    P.op('sp', lambda e, j=j: e.dma_start(out=gv_iv[j * TM:(j + 1) * TM, :].rearrange("(s p) n -> p s n", p=128), in_=Vl[:, 2 * j:2 * j + 2, :]),
                         r=[res(f'm_Vl_{j}_0'), res(f'm_Vl_{j}_1')], w=[res(f'gin_v{j}')], dsem=dsem('m_vst'))
                    if j == NTM - 1:
                        o_p0 = proj_fm([0, 1, 2, 3], N=16, c0=240)
                        o_gc = proj_fm([28, 29, 30, 31], N=16, c0=240)
                        o_h = proj_fm([32, 33, 34, 35], N=16, c0=240)
                        for g in range(4):
                            bk, off = o_p0[g]
                            P.op('act', lambda e, g=g, bk=bk, off=off: e.activation(out=hst[:, 0, g, :], in_=psum[bk][:, off:off + 16], func=AF.Copy),
                                 r=[bankres[bk]], w=[res('hst')])
                            bk, off = o_gc[g]
                            P.op('act', lambda e, bk=bk, off=off: e.activation(out=tg, in_=psum[bk][:, off:off + 16], func=AF.Copy),
                                 r=[bankres[bk]], w=[res('tg')])
                            bk2, off2 = o_h[g]
                            P.op('dve', lambda e, g=g, bk2=bk2, off2=off2: e.tensor_tensor(out=hst[:, 1, g, :], in0=psum[bk2][:, off2:off2 + 16], in1=tg, op=ALU.mult),
                                 r=[bankres[bk2], res('tg'), res('hst')], w=[res('hst')])
                        P.op('sp', lambda e: e.dma_start(out=gh_i[0:8, :].rearrange("r c -> (r c)").rearrange("(p x) -> p x", p=128), in_=af(O_HST, 128)),
                             r=[res('hst')], w=[res('gin_h')], dsem=dsem('m_hst'))
                for h in range(4):
                    P.op('dve', lambda e, h=h: e.tensor_reduce(out=kmT[:, h, 8:16], in_=Kl[:, h, :].rearrange("p (n t) -> p n t", t=256), axis=AX.X, op=ALU.add),
                         r=[res(f'm_Kl_{h}_{j}') for j in range(NTM)], w=[res(f'kmL{h}')])
                RG = [[0, 1], [2, 3], [4, 5], [6, 7]]

                def gather(name, i_ap, o_ap, rlist):
                    P.op('pool', lambda e: e.collective_compute("AllGather", ALU.bypass, replica_groups=RG, ins=[i_ap.opt()], outs=[o_ap.opt()]),
                         r=rlist, w=[res('go_' + name)], dsem=dsem('cc_' + name), dinc=1)

                gather('k', gk_i, gk_o, [res(f'gin_k{j}') for j in range(NTM)])
                gather('v', gv_i, gv_o, [res(f'gin_v{j}') for j in range(NTM)])
                gather('h', gh_i, gh_o, [res('gin_h')])
                P.op('pool', lambda e: e.dma_start(out=Kp, in_=gk_o[0:512, :].rearrange("(h p) t -> p h t", p=128)),
                     r=[res('go_k')], w=[res('m_Kp0'), res('m_Kp1')], dsem=dsem('m_kp'))
                gvv = gv_o[0:512, :].rearrange("r c -> (r c)").rearrange("(t n) -> t n", n=512)
                P.op('pool', lambda e: e.dma_start(out=Vp, in_=gvv.rearrange("(c p) n -> p c n", p=128)),
                     r=[res('go_v')], w=[res('m_Vp0'), res('m_Vp1')], dsem=dsem('m_vp'))
                P.op('sp', lambda e: e.dma_start(out=hprev, in_=gh_o[0:8, :].rearrange("r c -> (r c)").rearrange("(p x) -> p x", p=128)),
                     r=[res('go_h')], w=[res('hprev')], dsem=dsem('m_hp'))
                P.op('dve', lambda e: e.tensor_scalar(out=af(HL, 128), in0=hprev, scalar1=pp[:, PP_HALF:PP_HALF + 1], scalar2=None, op0=ALU.mult),
                     r=[res('hprev'), r_pp], w=[res('p0h'), res('zh')])
                for h in range(4):
                    P.op('dve', lambda e, h=h: e.tensor_reduce(out=kmT[:, h, 0:8], in_=Kp[:, h, :].rearrange("p (n t) -> p n t", t=256), axis=AX.X, op=ALU.add),
                         r=[res(f'm_Kp{h // 2}')], w=[res(f'kmP{h}')])
                P.op('dve', lambda e: e.tensor_scalar(out=af(KM, 64), in0=af(KM, 64), scalar1=1.0 / 256, scalar2=None, op0=ALU.mult),
                     r=[res(f'kmP{h}') for h in range(4)] + [res(f'kmL{h}') for h in range(4)], w=[res('kmT')])

                def gelu_ip(buf, n, bres):
                    t_ = G[1][:, 0:n]
                    P.op('dve', lambda e: e.tensor_tensor(out=t_, in0=buf, in1=buf, op=ALU.mult), r=[bres], w=[rG[1]])
                    P.op('dve', lambda e: e.tensor_scalar(out=t_, in0=t_, scalar1=0.044715, scalar2=1.0, op0=ALU.mult, op1=ALU.add), r=[rG[1]], w=[rG[1]])
                    P.op('dve', lambda e: e.tensor_tensor(out=t_, in0=t_, in1=buf, op=ALU.mult), r=[rG[1], bres], w=[rG[1]])
                    P.op('act', lambda e: e.activation(out=t_, in_=t_, func=AF.Sigmoid, scale=1.5957691216057308), r=[rG[1]], w=[rG[1]])
                    P.op('dve', lambda e: e.tensor_tensor(out=buf, in0=t_, in1=buf, op=ALU.mult), r=[rG[1], bres], w=[bres])

                rp0 = [res(f'p0b{g}') for g in range(4)]
                rzb = [res(f'zb{g}') for g in range(4)]
                rdd = [res(f'dd{g}') for g in range(4)]
                ddb = ab(O_DD, 512).rearrange("p (g t) -> p g t", g=4)
                vr = [G[2][:, 0:512], G[3][:, 0:512]]
                vh = G[0][:, 0:512].bitcast(BF16).rearrange("p (s n) -> p s n", s=2)
                acc = G[1][:, 0:256]
                xcb = [af(HL + 128 + k * 512, 512).rearrange("p (c t) -> p c t", c=2) for k in range(2)]
                rxcb = [res(f'xcb{k}') for k in range(2)]
                xcb_sem = [dsem(f'xcb{k}') for k in range(2)]
                for j in range(NTM):
                    modnorm_B()
                    if j + 1 < NTM:
                        load_xt(j + 1)
                    wi = [0]

                    def wget2(src_, nel, j=j):
                        idx = wi[0]
                        wi[0] += 1
                        if j == 0:
                            wt_, wr_ = wget(src_, nel)
                            k_ = wstate['last_k']
                            P.op('sp', lambda e, idx=idx, wt_=wt_: e.dma_start(out=wsc[l, idx], in_=wt_),
                                 r=[wr_], w=[res(f'wsc{l}_{idx}')], dsem=dsem(f'wst{k_}'))
                            return wt_, wr_
                        return wget(wsc[l, idx], nel, rkey=f'wsc{l}_{idx}')

                    tb = PP_TABS + j * 48
                    rankb = pp[:, tb:tb + 16].unsqueeze(1).to_broadcast([128, 4, 16])
                    notown = pp[:, tb + 16:tb + 32].unsqueeze(1).to_broadcast([128, 4, 16])
                    finalb = pp[:, tb + 32:tb + 48].unsqueeze(1).to_broadcast([128, 4, 16])
                    o_q = proj_fm([12, 13, 14, 15], wg=wget2)
                    headnorm4(o_q, misc[:, l:l + 1], [res('qgs')], [qT[:, h, :] for h in range(4)], [res(f'qT{h}') for h in range(4)])
                    mbk = take_banks(2)
                    sel_toks = []
                    sbks = take_banks(2)
                    for s_ in range(2):
                        sbk = sbks[s_]
                        for h in range(4):
                            P.op('pe', lambda e, h=h, s_=s_, sbk=sbk: e.matmul(psum[sbk][:, h * 16:(h + 1) * 16], lhsT=qf[:, h, s_ * 128:(s_ + 1) * 128], rhs=kmT[:, h, :], start=True, stop=True),
                                 r=[rqf[h], res('kmT')], pw=[bankres[sbk]] if h == 0 else [], fw=[bankres[sbk]] if h == 3 else [])
                    o_p0 = proj_fm([0, 1, 2, 3], wg=wget2)
                    P.op('pool', lambda e: e.tensor_copy(out=p0b[:, :, 0:16], in_=p0h), r=[res('p0h')], w=[res('p0halo')])
                    for g in range(4):
                        bk, off = o_p0[g]
                        P.op('act', lambda e, g=g, bk=bk, off=off: e.activation(out=p0b[:, g, 16:272], in_=psum[bk][:, off:off + 256], func=AF.Copy),
                             r=[bankres[bk]], w=[rp0[g]])
                    P.op('pool', lambda e: e.tensor_copy(out=p0h, in_=p0b[:, :, 256:272]), r=rp0, w=[res('p0h')])
                    o_u = proj_fm([4, 5, 6, 7], wg=wget2)
                    for h in range(4):
                        bk, off = o_u[h]
                        P.op('act', lambda e, h=h, bk=bk, off=off: e.activation(out=yc[:, 4 + h, :], in_=psum[bk][:, off:off + 256], func=AF.Copy),
                             r=[bankres[bk]], w=[ryc[4 + h]])
                    bks = proj_tm(0, wg=wget2)
                    for s_ in range(2):
                        P.op('act', lambda e, s_=s_, b=bks[s_]: e.activation(out=vr[s_], in_=psum[b][:, :], func=AF.Copy),
                             r=[bankres[bks[s_]]], w=[rG[2 + s_]])
                    for s_ in range(2):
                        sbk = sbks[s_]
                        P.op('dve', lambda e, sbk=sbk, rankb=rankb: e.tensor_tensor(out=sm, in0=psum[sbk][:, 0:64].rearrange("p (h n) -> p h n", h=4), in1=rankb, op=ALU.add),
                             r=[bankres[sbk], r_pp], w=[res('sm')])
                        for h in range(4):
                            P.op('dve', lambda e, h=h: e.max(out=m8[:, h, :], in_=sm[:, h, :]), r=[res('sm')], w=[res('m8')])
                        for h in range(4):
                            P.op('dve', lambda e, h=h: e.tensor_scalar(out=sel[:, h, :], in0=sm[:, h, :], scalar1=m8[:, h, 2:3], scalar2=None, op0=ALU.is_ge),
                                 r=[res('sm'), res('m8')], w=[res('sel')])
                        P.op('dve', lambda e: e.tensor_scalar(out=sel, in0=sel, scalar1=-1.0, scalar2=BIG, op0=ALU.add, op1=ALU.mult), r=[res('sel')], w=[res('sel')])
                        P.op('dve', lambda e, notown=notown: e.tensor_tensor(out=sel, in0=sel, in1=notown, op=ALU.mult), r=[res('sel'), r_pp], w=[res('sel')])
                        P.op('dve', lambda e, finalb=finalb: e.tensor_tensor(out=sel, in0=sel, in1=finalb, op=ALU.add), r=[res('sel'), r_pp], w=[res('sel')])
                        for h in range(4):
                            b = mbk[h // 2]
                            P.op('pe', lambda e, h=h, s_=s_, b=b: e.transpose(psum[b][0:16, (h % 2) * 256 + s_ * 128:(h % 2) * 256 + (s_ + 1) * 128], sel[:, h, :], ident_f[:]),
                                 r=[res('sel'), res('ident_f')], pw=[bankres[b]] if (s_ == 0 and h % 2 == 0) else [], fw=[bankres[b]] if (s_ == 1 and h % 2 == 1) else [])
                    for h in range(4):
                        b = mbk[h // 2]
                        P.op('act', lambda e, h=h, b=b: e.activation(out=mskT[0:16, h, :], in_=psum[b][0:16, (h % 2) * 256:(h % 2 + 1) * 256], func=AF.Copy),
                             r=[bankres[b]], w=[res(f'mskT{h}')])
                    o_gc = proj_fm([28, 29, 30, 31], wg=wget2)
                    P.op('pool', lambda e: e.tensor_copy(out=zb[:, :, 0:2], in_=zh[:, :, 14:16]), r=[res('zh')], w=[res('zhalo')])
                    for g in range(4):
                        bk, off = o_gc[g]
                        P.op('act', lambda e, g=g, bk=bk, off=off: e.activation(out=zb[:, g, 2:258], in_=psum[bk][:, off:off + 256], func=AF.Copy), r=[bankres[bk]], w=[rzb[g]])
                    o_hh = proj_fm([32, 33, 34, 35], wg=wget2)
                    for g in range(4):
                        bk2, off2 = o_hh[g]
                        P.op('dve', lambda e, g=g, bk2=bk2, off2=off2: e.tensor_tensor(out=zb[:, g, 2:258], in0=psum[bk2][:, off2:off2 + 256], in1=zb[:, g, 2:258], op=ALU.mult),
                             r=[bankres[bk2], rzb[g]], w=[rzb[g]])
                    P.op('pool', lambda e: e.tensor_copy(out=zh[:, :, 14:16], in_=zb[:, :, 256:258]), r=rzb, w=[res('zh')])
                    o_gb = proj_fm([24, 25, 26, 27], wg=wget2)
                    for g in range(4):
                        bk3, off3 = o_gb[g]
                        P.op('act', lambda e, g=g, bk3=bk3, off3=off3: e.activation(out=yc[:, 12 + g, :], in_=psum[bk3][:, off3:off3 + 256], func=AF.Copy), r=[bankres[bk3]], w=[ryc[12 + g]])

                    def pool_unit(g):
                        wn = 2 << g
                        srcb = p0b[:, g, :]
                        cur_in, cur_res = srcb, [rp0[g], res('p0halo')]
                        wkb = G[0].rearrange("p (a t) -> p a t", a=2)
                        for st_ in range(g + 1):
                            sh = 1 << st_
                            o_ = wkb[:, st_ % 2, :]
                            P.op('pool', lambda e, o_=o_, cur_in=cur_in, sh=sh: e.tensor_tensor(out=o_[:, sh:272], in0=cur_in[:, sh:272], in1=cur_in[:, 0:272 - sh], op=ALU.add),
                                 r=cur_res, w=[rG[0]])
                            cur_in, cur_res = o_, [rG[0]]
                        dd = ddb[:, g, :]
                        P.op('dve', lambda e, cur_in=cur_in, srcb=srcb, wn=wn, dd=dd: e.scalar_tensor_tensor(out=dd, in0=cur_in[:, 16:272], scalar=1.0 / wn, in1=srcb[:, 16:272], op0=ALU.mult, op1=ALU.subtract),
                             r=cur_res + [rp0[g]], w=[rdd[g]])
                        if j == 0:
                            P.op('dve', lambda e, cur_in=cur_in, g=g: e.tensor_tensor(out=t16, in0=cur_in[:, 16:32], in1=pp[:, PP_INVC + g * 16:PP_INVC + (g + 1) * 16], op=ALU.mult),
                                 r=cur_res + [r_pp], w=[res('t16')])
                            P.op('dve', lambda e, srcb=srcb, dd=dd: e.tensor_tensor(out=dd[:, 0:16], in0=t16, in1=srcb[:, 16:32], op=ALU.subtract),
                                 r=[res('t16'), rp0[g], rdd[g]], w=[rdd[g]])

                    def sgu_v_unit(s_):
                        gelu_ip(vr[s_], 512, rG[2 + s_])
                        for h in range(4):
                            P.op('act', lambda e, h=h, s_=s_: e.activation(out=junk, in_=vr[s_][:, h * 128:(h + 1) * 128], func=AF.Square, accum_out=ss4[:, h:h + 1]),
                                 r=[rG[2 + s_]], w=[res('ss4'), res('junk')])
                        rsqrt(ss4, ss4, [res('ss4')], res('ss4'), prescale=1.0 / 128)
                        for h in range(4):
                            P.op('dve', lambda e, h=h, s_=s_: e.scalar_tensor_tensor(out=vh[:, s_, h * 128:(h + 1) * 128], in0=vr[s_][:, h * 128:(h + 1) * 128], scalar=ss4[:, h:h + 1],
                                                                                     in1=bc[:, BC_SGUG + l * 512 + h * 128:BC_SGUG + l * 512 + (h + 1) * 128], op0=ALU.mult, op1=ALU.mult),
                                 r=[rG[2 + s_], res('ss4'), r_bc], w=[res(f'vh{s_}')], pw=[rG[0]] if h == 0 else [])

                    def conv_unit(g):
                        zr = [rzb[g], res('zhalo')]
                        cw = PP_CONVW + l * 12
                        P.op('dve', lambda e, g=g, cw=cw: e.tensor_scalar(out=acc, in0=zb[:, g, 0:256], scalar1=pp[:, cw + g:cw + g + 1], scalar2=None, op0=ALU.mult),
                             r=zr + [r_pp], w=[rG[1]])
                        for jj in (1, 2):
                            P.op('dve', lambda e, g=g, cw=cw, jj=jj: e.scalar_tensor_tensor(out=acc, in0=zb[:, g, jj:jj + 256], scalar=pp[:, cw + jj * 4 + g:cw + jj * 4 + g + 1], in1=acc, op0=ALU.mult, op1=ALU.add),
                                 r=zr + [r_pp, rG[1]], w=[rG[1]])
                        P.op('dve', lambda e, g=g: e.tensor_tensor(out=yc[:, 12 + g, :], in0=yc[:, 12 + g, :], in1=acc, op=ALU.mult),
                             r=[ryc[12 + g], rG[1]], w=[ryc[12 + g]])

                    units = [lambda g=g: pool_unit(g) for g in range(4)] + [lambda s_=s_: sgu_v_unit(s_) for s_ in range(2)] \
                        + [lambda h=h: gelu_ip(yc[:, 4 + h, :], 256, ryc[4 + h]) for h in range(4)] + [lambda g=g: conv_unit(g) for g in range(4)]
                    upos = [0]

                    def run_units(n):
                        for _ in range(n):
                            if upos[0] < len(units):
                                units[upos[0]]()
                                upos[0] += 1

                    npair = 8 + (j + 1)

                    def emit_S(h, pr, gi):
                        sbk = 4 + (gi % 3)
                        for i2 in range(2):
                            kt = pr * 2 + i2
                            if kt < 16:
                                Ksrc, Kres, blk, own = Kp[:, h, kt * 128:(kt + 1) * 128], [res(f'm_Kp{h // 2}')], kt // 2, False
                            else:
                                kl = kt - 16
                                Ksrc, Kres, blk, own = Kl[:, h, kl * 128:(kl + 1) * 128], [res(f'm_Kl_{h}_{kl // 2}')], 8 + kl // 2, (kl // 2 == j)
                            oc = i2 * 256
                            P.op('pe', lambda e, Ksrc=Ksrc, oc=oc, h=h, sbk=sbk: e.matmul(psum[sbk][:, oc:oc + 256], lhsT=Ksrc, rhs=qT[:, h, :], start=True, stop=False),
                                 r=Kres + [res(f'qT{h}')], pw=[bankres[sbk]] if i2 == 0 else [])
                            P.op('pe', lambda e, blk=blk, oc=oc, h=h, sbk=sbk, own=own: e.matmul(psum[sbk][:, oc:oc + 256], lhsT=onehot_b[:, blk, :], rhs=mskT[:, h, :], start=False, stop=(not own)),
                                 r=[res('onehot_b'), res(f'mskT{h}')], fw=[bankres[sbk]] if (i2 == 1 and not own) else [])
                            if own:
                                P.op('pe', lambda e, oc=oc, sbk=sbk, i2=i2: e.matmul(psum[sbk][:, oc:oc + 256], lhsT=ident_b[:], rhs=causal_b[:, i2, :], start=False, stop=True),
                                     r=[res('ident_b'), res('causal_b')], fw=[bankres[sbk]] if i2 == 1 else [])
                        pk = gi % 3
                        P.op('act', lambda e, sbk=sbk, pk=pk: e.activation(out=pT[pk].rearrange("p a t -> p (a t)"), in_=psum[sbk][:, :], func=AF.Exp),
                             r=[bankres[sbk]], w=[res(f'pT{pk}')])

                    def emit_PV(h, pr, gi):
                        ob, db = (0, 1) if h % 2 == 0 else (2, 3)
                        pk = gi % 3
                        for i2 in range(2):
                            kt = pr * 2 + i2
                            if kt < 16:
                                Vsrc, Vres = Vp[:, kt, h * 128:(h + 1) * 128], [res(f'm_Vp{kt // 8}')]
                            else:
                                kl = kt - 16
                                Vsrc, Vres = Vl[:, kl, h * 128:(h + 1) * 128], [res(f'm_Vl_{kl // 2}_{kl % 2}')]
                            first = (pr == 0 and i2 == 0)
                            last = (pr == npair - 1 and i2 == 1)
                            P.op('pe', lambda e, Vsrc=Vsrc, pk=pk, i2=i2, ob=ob, first=first, last=last: e.matmul(psum[ob][:, 0:256], lhsT=Vsrc, rhs=pT[pk][:, i2, :], start=first, stop=last),
                                 r=Vres + [res(f'pT{pk}')], pw=[bankres[ob]] if first else [], fw=[bankres[ob]] if last else [])
                            P.op('pe', lambda e, pk=pk, i2=i2, db=db, first=first, last=last: e.matmul(psum[db][:, 0:256], lhsT=o1, rhs=pT[pk][:, i2, :], start=first, stop=last),
                                 r=[o1r, res(f'pT{pk}')], pw=[bankres[db]] if first else [], fw=[bankres[db]] if last else [])

                    seq = [(h, pr) for h in range(4) for pr in range(npair)]
                    emit_S(seq[0][0], seq[0][1], 0)
                    emit_S(seq[1][0], seq[1][1], 1)
                    for gi, (h, pr) in enumerate(seq):
                        if gi + 2 < len(seq):
                            emit_S(seq[gi + 2][0], seq[gi + 2][1], gi + 2)
                        emit_PV(h, pr, gi)
                        if pr == npair - 1:
                            ob, db = (0, 1) if h % 2 == 0 else (2, 3)
                            run_units((4, 2, 4, 4)[h])
                            rD = trs
                            P.op('dve', lambda e, db=db, rD=rD: e.reciprocal(out=rD, in_=psum[db][:, 0:256]), r=[bankres[db]], w=[res('hn_rs')])
                            P.op('dve', lambda e, ob=ob, rD=rD, h=h: e.tensor_tensor(out=yc[:, 8 + h, :], in0=psum[ob][:, 0:256], in1=rD, op=ALU.mult),
                                 r=[bankres[ob], res('hn_rs')], w=[ryc[8 + h]])
                    run_units(len(units))

                    for g in range(4):
                        bk = take_banks(1)[0]
                        P.op('pe', lambda e, g=g, bk=bk: e.matmul(psum[bk][:, 0:256], lhsT=poolw_b[:, l * 4 + g, :], rhs=ddb[:, g, :], start=True, stop=True),
                             r=[rdd[g], res('poolw_b')], w=[bankres[bk]])
                        P.op('act', lambda e, g=g, bk=bk: e.activation(out=yc[:, g, :], in_=psum[bk][:, 0:256], func=AF.Copy, scale=pp[:, PP_POOLS + l * 4 + g:PP_POOLS + l * 4 + g + 1]),
                             r=[bankres[bk], r_pp], w=[ryc[g]])
                    mb_ = take_banks(2)
                    for h in range(4):
                        b = mb_[h // 2]
                        for s_ in range(2):
                            P.op('pe', lambda e, h=h, s_=s_, b=b: e.matmul(psum[b][:, (h % 2) * 256 + s_ * 128:(h % 2) * 256 + (s_ + 1) * 128], lhsT=vh[:, s_, h * 128:(h + 1) * 128], rhs=sguw_b[:, l * 4 + h, :], start=True, stop=True),
                                 r=[res(f'vh{s_}'), rG[0], res('sguw_b')], pw=[bankres[b]] if (h % 2 == 0 and s_ == 0) else [], fw=[bankres[b]] if (h % 2 == 1 and s_ == 1) else [])
                    for h in range(4):
                        b = mb_[h // 2]
                        tmpm = G[1][:, (h % 2) * 256:(h % 2) * 256 + 256]
                        sgb = bc[:, BC_SGUB + (l * 4 + h) * 128:BC_SGUB + (l * 4 + h + 1) * 128]
                        P.op('dve', lambda e, h=h, b=b, tmpm=tmpm, sgb=sgb: e.tensor_tensor(out=tmpm.rearrange("p (s t) -> p s t", s=2), in0=psum[b][:, (h % 2) * 256:(h % 2 + 1) * 256].rearrange("p (s t) -> p s t", s=2),
                                                                                          in1=sgb.unsqueeze(1).to_broadcast([128, 2, 128]), op=ALU.add),
                             r=[bankres[b], r_bc], w=[rG[1]])
                        P.op('dve', lambda e, h=h, tmpm=tmpm: e.tensor_tensor(out=yc[:, 4 + h, :], in0=yc[:, 4 + h, :], in1=tmpm, op=ALU.mult),
                             r=[rG[1], ryc[4 + h]], w=[ryc[4 + h]])

                    if DBG_YCAT:
                        P.op('dve', lambda e: e.tensor_copy(out=xt, in_=yc), r=ryc + [r_xt], w=[r_xt])
                        slabs = [xr(dstname, dc, j) for dc in range(16)]
                        P.op('sp', lambda e, j=j: e.dma_start(out=dst[:, :, j * TM:(j + 1) * TM].rearrange("c p t -> p c t"), in_=xt),
                             r=[r_xt], w=slabs + [r_xt], dsem=xt_sem)
                        continue
                    gb_ = take_banks(2)
                    for m in range(4):
                        b = gb_[m // 2]
                        sqg = G[m][:, 0:512].bitcast(BF16).rearrange("p (c t) -> p c t", c=4)
                        P.op('act', lambda e, m=m, sqg=sqg: e.activation(out=sqg, in_=yc[:, m * 4:(m + 1) * 4, :], func=AF.Square), r=ryc[m * 4:(m + 1) * 4], w=[rG[m]])
                        for c4 in range(4):
                            P.op('pe', lambda e, m=m, c4=c4, b=b, sqg=sqg: e.matmul(psum[b][:, (m % 2) * 256:(m % 2 + 1) * 256], lhsT=o512, rhs=sqg[:, c4, :], start=(c4 == 0), stop=(c4 == 3)),
                                 r=[rG[m], o512r], pw=[bankres[b]] if (m % 2 == 0 and c4 == 0) else [], fw=[bankres[b]] if (m % 2 == 1 and c4 == 3) else [])
                    for m2 in range(2):
                        rsqrt(rs4[:, m2 * 2:m2 * 2 + 2, :].rearrange("p a t -> p (a t)"), psum[gb_[m2]][:, :], [bankres[gb_[m2]]], res(f'rs4_{m2}'))
                    for dc in range(16):
                        eng = 'dve'
                        og = PP_OUTG + l * 16 + dc
                        P.op(eng, lambda e, dc=dc, og=og: e.scalar_tensor_tensor(out=xn[:, dc, :], in0=yc[:, dc, :], scalar=pp[:, og:og + 1], in1=rs4[:, dc // 4, :], op0=ALU.mult, op1=ALU.mult),
                             r=[ryc[dc], res(f'rs4_{dc // 8}'), r_pp], w=[rxn[dc]])
                    if j + 1 < NTM and not DBG_YCAT:
                        modnorm_A()
                    for grp in range(8):
                        kb = grp % 2
                        P.op('sp', lambda e, grp=grp, kb=kb, j=j: e.dma_start(out=xcb[kb], in_=src[2 * grp:2 * grp + 2, :, j * TM:(j + 1) * TM].rearrange("c p t -> p c t")),
                             r=[xr(srcname, 2 * grp + c_, j) for c_ in range(2)], w=[rxcb[kb]], dsem=xcb_sem[kb])
                        wt, wr = wget2(wout_d[l, grp], 4096)
                        wv = wt.rearrange("p (c k n) -> p c k n", c=2, k=16)
                        bk = take_banks(1)[0]
                        for c2 in range(2):
                            for dc in range(16):
                                P.op('pe', lambda e, c2=c2, dc=dc, wv=wv, bk=bk: e.matmul(psum[bk][:, c2 * 256:(c2 + 1) * 256], lhsT=wv[:, c2, dc, :], rhs=xn[:, dc, :], start=(dc == 0), stop=(dc == 15)),
                                     r=[wr, rxn[dc]], pw=[bankres[bk]] if (c2 == 0 and dc == 0) else [], fw=[bankres[bk]] if (c2 == 1 and dc == 15) else [])
                        for c2 in range(2):
                            dt = grp * 2 + c2
                            P.op('dve', lambda e, c2=c2, dt=dt, bk=bk, kb=kb: e.scalar_tensor_tensor(out=xcb[kb][:, c2, :], in0=psum[bk][:, c2 * 256:(c2 + 1) * 256], scalar=derG(l, 1, dt), in1=xcb[kb][:, c2, :], op0=ALU.mult, op1=ALU.add),
                                 r=[bankres[bk], rxcb[kb]] + dres, fw=[rxcb[kb]] if c2 == 1 else [])
                        P.op('sp', lambda e, grp=grp, kb=kb, j=j: e.dma_start(out=dst[2 * grp:2 * grp + 2, :, j * TM:(j + 1) * TM].rearrange("c p t -> p c t"), in_=xcb[kb]),
                             r=[rxcb[kb]], w=[xr(dstname, 2 * grp + c_, j) for c_ in range(2)], dsem=xcb_sem[kb])

            nsub = min(stage, 3 * L)
            barrier()
            k = 0
            cur_src, cur_name = xT_in, 'in'
            for grp in range(24):
                ada_group(0, grp)
            ada_derive(0, 0)
            for l in range(L):
                for i in range(3):
                    if k >= nsub:
                        break
                    last = (k == nsub - 1) and not DBG_YCAT
                    dst, dname = (yT_out, 'out') if last else (xs, 'xs')
                    if i == 1:
                        barrier()
                        mixer(l, cur_src, cur_name, dst, dname)
                        barrier()
                    else:
                        bg = []
                        if l == 0 and i == 0:
                            bg = [lambda grp=grp: ada_group(0, grp) for grp in range(24, 72)] + [lambda ii=ii: ada_derive(0, ii) for ii in (1, 2)]
                        elif i == 2 and l + 1 < L:
                            bg = [lambda grp=grp, l=l: ada_group(l + 1, grp) for grp in range(72)] + [lambda ii=ii, l=l: ada_derive(l + 1, ii) for ii in range(3)]
                        ffn(l, i, l * 2 + (0 if i == 0 else 1), cur_src, cur_name, dst, dname, bg=bg)
                    cur_src, cur_name = dst, dname
                    k += 1
            if DBG_YCAT:
                barrier()
                for dc in range(16):
                    P.op('sp', lambda e, dc=dc: e.dma_start(out=yT_out[dc], in_=xs[dc]), r=[xr('xs', dc, sl) for sl in range(8)], w=[res(f'fin{dc}')], dsem=dsem('fin'))
            barrier()
            return P, wrec

        _, wseq = run(True, [])
        P, _ = run(False, wseq)

        targets = {e: set() for e in ENGS}
        for e in ENGS:
            for (fn, deps, tok, dinc) in P.q[e]:
                for d in deps:
                    if d.sem is None:
                        targets[d.eng].add(d.idx)
        cnt = {}
        for e in ENGS:
            c = 0
            arr = []
            for idx in range(len(P.q[e])):
                if idx in targets[e]:
                    c += 1
                arr.append(c)
            cnt[e] = arr
        for e in ENGS:
            for idx, (fn, deps, tok, dinc) in enumerate(P.q[e]):
                assert not (fn is None and idx in targets[e])

        engobj = {'pe': nc.tensor, 'act': nc.scalar, 'dve': nc.vector, 'pool': nc.gpsimd, 'sp': nc.sync}

        def replay(e, eng):
            seen = {}
            for idx, (fn, deps, tok, dinc) in enumerate(P.q[e]):
                for d in deps:
                    if d.sem is None:
                        if d.eng == e and d.idx >= idx:
                            continue
                        s, v = sem_eng[d.eng], cnt[d.eng][d.idx]
                    else:
                        s, v = d.sem, d.val
                    if seen.get(id(s), 0) >= v:
                        continue
                    seen[id(s)] = v
                    eng.wait_ge(s.h, v)
                if fn is None:
                    continue
                ins = fn(eng)
                if tok.sem is not None:
                    ins.then_inc(tok.sem.h, dinc)
                elif idx in targets[e]:
                    ins.then_inc(sem_eng[e].h, 1)

        with nc.Block() as block:
            @block.tensor
            def _(eng):
                replay('pe', eng)

            @block.scalar
            def _(eng):
                replay('act', eng)

            @block.vector
            def _(eng):
                replay('dve', eng)

            @block.gpsimd
            def _(eng):
                replay('pool', eng)

            @block.sync
            def _(eng):
                replay('sp', eng)
    return nc


def _host_prep(inputs):
    f = np.float32
    g = {k: np.asarray(v, dtype=f) for k, v in inputs.items()}
    sh = {}
    ada = g['ada_w'].reshape(L, 16, 128, 144, 128).transpose(0, 3, 2, 1, 4)
    sh['adaw'] = np.ascontiguousarray(ada.reshape(L, 72, 2, 128, 16, 128).transpose(0, 1, 3, 2, 4, 5)).reshape(L, 72, 128, 4096)
    w13 = []
    w2 = []
    for l in range(L):
        for nm13, nm2 in (('ffn1_w13', 'ffn1_w2'), ('ffn2_w13', 'ffn2_w2')):
            a = g[nm13][l].reshape(16, 128, 2, NFC, 128).transpose(3, 1, 2, 0, 4)
            w13.append(np.ascontiguousarray(a).reshape(NFC, 128, 4096))
            b = g[nm2][l].reshape(NFC, 128, 16, 128).transpose(2, 1, 0, 3)
            w2.append(np.ascontiguousarray(b).reshape(16, 128, DFF))
    sh['w13'] = np.stack(w13)
    sh['w2'] = np.stack(w2)
    wi = g['w_in'].reshape(L, 16, 128, 36, 128).transpose(0, 3, 2, 1, 4)
    sh['winf'] = np.ascontiguousarray(wi.reshape(L, 18, 2, 128, 16, 128).transpose(0, 1, 3, 2, 4, 5)).reshape(L, 18, 128, 4096)
    wt = np.stack([g['w_in'][:, :, 1024:1536], g['w_in'][:, :, 2560:3072]], axis=1)
    wt = wt.reshape(L, 2, 2, 8, 128, 512).transpose(0, 1, 2, 4, 3, 5)
    sh['wint'] = np.ascontiguousarray(wt).reshape(L, 2, 2, 128, 4096)
    wo = g['w_out'].reshape(L, 16, 128, 16, 128).transpose(0, 3, 2, 1, 4)
    sh['wout'] = np.ascontiguousarray(wo.reshape(L, 8, 2, 128, 16, 128).transpose(0, 1, 3, 2, 4, 5)).reshape(L, 8, 128, 4096)
    sh['poolw'] = np.ascontiguousarray(g['pool_w'].transpose(2, 0, 1, 3)).reshape(128, L * 4 * 128)
    sh['sguwT'] = np.ascontiguousarray(g['sgu_w'].transpose(3, 0, 1, 2)).reshape(128, L * 4 * 128)
    cm = np.zeros((128, 128 * 3 + 512 + 2048), f)
    cm[:, 0:128] = np.eye(128, dtype=f)
    ss, tt = np.meshgrid(np.arange(128), np.arange(128), indexing='ij')
    cm[:, 128:256] = (ss <= tt).astype(f)
    kk, qq = np.meshgrid(np.arange(128), np.arange(256), indexing='ij')
    for r in range(2):
        cm[:, 384 + r * 256:384 + (r + 1) * 256] = np.where(r * 128 + kk <= qq, 0.0, -BIG).astype(f)
    oh = np.zeros((16, 16, 128), f)
    for b_ in range(16):
        oh[b_, b_, :] = 1.0
    cm[0:16, 896:2944] = oh.reshape(16, 2048)
    sh['cmat'] = cm
    bcv = np.concatenate([g['sgu_b'].reshape(-1), g['sgu_norm_g'].reshape(-1)])
    sh['bc'] = np.ascontiguousarray(np.broadcast_to(bcv[None, :], (128, NBC)))

    def ppl(v):
        v = np.asarray(v, f)
        lead = v.shape[:-1]
        n = v.shape[-1] // 128
        return v.reshape(*lead, n, 128).transpose(len(lead) + 1, *range(len(lead)), len(lead)).reshape(128, -1)

    in_maps = []
    for c in range(8):
        b, half = c // 2, c % 2
        pp = np.zeros((128, NPP), f)
        pp[:, PP_C:PP_C + 16] = ppl(g['c'][b])
        pp[:, PP_NORMG:PP_NORMG + L * 48] = ppl(g['norm_g'])
        pp[:, PP_POOLS:PP_POOLS + L * 4] = ppl(g['pool_scale'])
        pp[:, PP_QG:PP_QG + L] = g['q_norm_g'].T
        pp[:, PP_KG:PP_KG + L] = g['k_norm_g'].T
        pp[:, PP_CONVW:PP_CONVW + L * 12] = ppl(g['conv_w'])
        pp[:, PP_OUTG:PP_OUTG + L * 16] = ppl(g['out_norm_g'])
        pp[:, PP_HALF] = float(half)
        pp[:, PP_ADAB:PP_ADAB + L * 144] = ppl(g['ada_b'])
        tabs = np.zeros((8, 3, 16), f)
        prevb = 0.0 if half == 1 else -BIG
        for o in range(8):
            tabs[o, 0, 0:8] = prevb
            tabs[o, 0, 8:8 + o] = 0.0
            tabs[o, 0, 8 + o:] = -BIG
            tabs[o, 1, :] = 1.0
            tabs[o, 1, 8 + o] = 0.0
            tabs[o, 2, 0:8] = prevb
            tabs[o, 2, 8:8 + o + 1] = 0.0
            tabs[o, 2, 8 + o + 1:] = -BIG
        pp[:, PP_TABS:PP_TABS + 384] = tabs.reshape(1, -1)
        inv = np.zeros((4, 16), f)
        for gi in range(4):
            wn = 2 << gi
            for t in range(16):
                inv[gi, t] = 1.0 / (wn if half == 1 else min(t + 1, wn))
        pp[:, PP_INVC:PP_INVC + 64] = inv.reshape(1, -1)
        xT = np.ascontiguousarray(g['x'][b, half * TL:(half + 1) * TL, :].T).reshape(NDC, 128, TL)
        m = dict(sh)
        m['pp'] = pp
        m['xT'] = xT
        in_maps.append(m)
    return in_maps


_NC_CACHE = {}


def _run(inputs, stage=6):
    in_maps = _host_prep(inputs)
    if stage not in _NC_CACHE:
        _NC_CACHE[stage] = _mk_program(stage)
    nc = _NC_CACHE[stage]
    res = run_bass_kernel_spmd(nc, in_maps, core_ids=list(range(8)))
    out = np.empty((4, 4096, D), np.float32)
    for c in range(8):
        b, half = c // 2, c % 2
        yT = np.asarray(res.results[c]["yT"]).reshape(D, TL)
        out[b, half * TL:(half + 1) * TL, :] = yT.T
    return out


def kernel(**inputs):
    return _run(inputs, stage=6)
```

```python
import contextlib
import numpy as np
import concourse.bass as bass
import concourse.mybir as mybir
from concourse.bass_utils import run_bass_kernel_spmd

F32 = mybir.dt.float32
BF16 = mybir.dt.bfloat16
ALU = mybir.AluOpType
AF = mybir.ActivationFunctionType
AX = mybir.AxisListType

L = 2
D = 2048
NDC = 16
DFF = 5632
NFC = 44
TL = 2048
TF = 1024
TM = 256
NTM = TL // TM
EPS = 1e-6
BIG = 1.0e30
WSLOT = 5632
NW = 3
ENGS = ('pe', 'act', 'dve', 'pool', 'sp')
DBG_YCAT = False

PP_C = 0
PP_NORMG = PP_C + 16
PP_POOLS = PP_NORMG + L * 3 * 16
PP_QG = PP_POOLS + L * 4
PP_KG = PP_QG + L
PP_CONVW = PP_KG + L
PP_OUTG = PP_CONVW + L * 12
PP_HALF = PP_OUTG + L * 16
PP_ADAB = PP_HALF + 1
PP_TABS = PP_ADAB + L * 144
PP_INVC = PP_TABS + 8 * 3 * 16
NPP = PP_INVC + 64
BC_SGUB = 0
BC_SGUG = BC_SGUB + L * 4 * 128
NBC = BC_SGUG + L * 512


class Sem:
    def __init__(self, h):
        self.h = h
        self.count = 0


class Tok:
    __slots__ = ('eng', 'idx', 'sem', 'val')

    def __init__(self, eng=None, idx=None, sem=None, val=None):
        self.eng, self.idx, self.sem, self.val = eng, idx, sem, val

    def key(self):
        return self.eng if self.sem is None else id(self.sem)

    def order(self):
        return self.idx if self.sem is None else self.val


def _merge(d, t):
    k = t.key()
    o = d.get(k)
    if o is None or o.order() < t.order():
        d[k] = t


class Res:
    def __init__(self):
        self.w = {}
        self.r = {}


class Prog:
    def __init__(self, dry):
        self.dry = dry
        self.q = {e: [] for e in ENGS}
        self.pending_dma = {}

    def op(self, eng, fn, r=(), w=(), pw=(), fw=(), dsem=None, dinc=16, extra=()):
        if self.dry:
            return None
        deps = {}
        for t in extra:
            if t is not None:
                _merge(deps, t)
        for x in r:
            for t in x.w.values():
                _merge(deps, t)
        for x in list(w) + list(pw):
            for t in x.w.values():
                _merge(deps, t)
            for t in x.r.values():
                _merge(deps, t)
        if dsem is not None:
            dsem.count += dinc
            tok = Tok(sem=dsem, val=dsem.count)
            self.pending_dma[id(dsem)] = tok
        else:
            tok = Tok(eng=eng, idx=len(self.q[eng]))
        for x in r:
            _merge(x.r, tok)
        for x in w:
            x.w = {tok.key(): tok}
            x.r = {}
        for x in pw:
            x.w = {}
            x.r = {}
        for x in fw:
            x.w = {tok.key(): tok}
            x.r = {}
        self.q[eng].append((fn, list(deps.values()), tok, dinc))
        return tok


def _mk_program(stage):
    nc = bass.Bass("TRN2", target_bir_lowering=False)
    dr = {}

    def din(name, shape):
        dr[name] = nc.dram_tensor(name, list(shape), F32, kind="ExternalInput").ap()
        return dr[name]

    xT_in = din("xT", [NDC, 128, TL])
    pp_d = din("pp", [128, NPP])
    bc_d = din("bc", [128, NBC])
    cm_d = din("cmat", [128, 128 * 3 + 2 * 256 + 16 * 128])
    poolw_d = din("poolw", [128, L * 4 * 128])
    sguw_d = din("sguwT", [128, L * 4 * 128])
    ada_d = din("adaw", [L, 72, 128, 2 * 2048])
    w13_d = din("w13", [L * 2, NFC, 128, 4096])
    w2_d = din("w2", [L * 2, 16, 128, DFF])
    winf_d = din("winf", [L, 18, 128, 2 * 2048])
    wint_d = din("wint", [L, 2, 2, 128, 8 * 512])
    wout_d = din("wout", [L, 8, 128, 2 * 2048])
    yT_out = nc.dram_tensor("yT", [NDC, 128, TL], F32, kind="ExternalOutput").ap()
    xs = nc.dram_tensor("xs", [NDC, 128, TL], F32).ap()
    gk_i = nc.dram_tensor("gk_i", [512, 2048], BF16).ap()
    gk_o = nc.dram_tensor("gk_o", [1024, 2048], BF16).ap()
    gv_i = nc.dram_tensor("gv_i", [512, 2048], BF16).ap()
    gv_o = nc.dram_tensor("gv_o", [1024, 2048], BF16).ap()
    gh_i = nc.dram_tensor("gh_i", [128, 2048], F32).ap()
    gh_o = nc.dram_tensor("gh_o", [256, 2048], F32).ap()
    wsc = nc.dram_tensor("wsc", [L, 22, 128, 4096], BF16).ap()

    es = contextlib.ExitStack()
    with es:
        def sb(name, shape, dt):
            return es.enter_context(nc.sbuf_tensor("sb_" + name, list(shape), dt))

        def newsem(name):
            return Sem(es.enter_context(nc.semaphore(name)))

        pp = sb("pp", [128, NPP], F32)
        bc = sb("bc", [128, NBC], F32)
        modpp = sb("modpp", [128, L * 144], F32)
        der = sb("der", [128, L * 3 * 16 * 3], F32)
        misc = sb("misc", [128, 64], F32)
        ident_f = sb("ident_f", [128, 128], F32)
        trilT = sb("trilT", [128, 128], F32)
        ident_b = sb("ident_b", [128, 128], BF16)
        causal_b = sb("causal_b", [128, 2, 256], BF16)
        onehot_b = sb("onehot_b", [128, 16, 128], BF16)
        ones_b = sb("ones_b", [128, 4, 128], BF16)
        poolw_b = sb("poolw_b", [128, L * 4, 128], BF16)
        sguw_b = sb("sguw_b", [128, L * 4, 128], BF16)
        siluc_b = sb("siluc_b", [128, 16], BF16)
        wbuf = sb("wbuf", [128, NW, WSLOT], BF16)
        psum = [es.enter_context(nc.psum_tensor(f"ps{k}", [128, 512], F32)) for k in range(8)]

        PH = int(nc.sbuf_bytes_remaining) // 4 - 16
        arena = sb("arena", [128, PH], F32)
        cstage = arena[:, 0:2560]
        sguw_f = arena[:, 2560:2560 + L * 4 * 128].rearrange("p (a b) -> p a b", a=L * 4)
        sem_eng = {e: newsem(f"s_{e}") for e in ENGS}
        wsem = [newsem(f"w{k}") for k in range(NW)]
        dsems = {}

        def dsem(name):
            if name not in dsems:
                dsems[name] = newsem("d_" + name)
            return dsems[name]

        def run(dry, wseq):
            P = Prog(dry)
            R = {}

            def res(name):
                if name not in R:
                    R[name] = Res()
                return R[name]

            bankres = [Res() for _ in range(8)]
            bank_ptr = [0]

            reserved = set()

            def take_banks(n):
                out = []
                while len(out) < n:
                    if bank_ptr[0] not in reserved:
                        out.append(bank_ptr[0])
                    bank_ptr[0] = (bank_ptr[0] + 1) % 8
                return out

            xres = {}

            def xr(buf, dc, slab):
                k = (buf, dc, slab)
                if k not in xres:
                    xres[k] = Res()
                return xres[k]

            wrec = []
            wstate = {'next_issue': 0, 'next_get': 0}
            wslot_res = [Res() for _ in range(NW)]

            def w_issue_upto(n):
                while wstate['next_issue'] < min(n, len(wseq)):
                    i = wstate['next_issue']
                    src, nel, rkey = wseq[i]
                    k = i % NW
                    P.op('pool', lambda e, src=src, k=k, nel=nel: e.dma_start(out=wbuf[:, k, 0:nel], in_=src),
                         r=[res(rkey)] if rkey else [], w=[wslot_res[k]], dsem=wsem[k])
                    wstate['next_issue'] += 1

            def wget(src, nel, rkey=None):
                i = wstate['next_get']
                wstate['next_get'] += 1
                wrec.append((src, nel, rkey))
                wstate['last_k'] = i % NW
                if dry:
                    return wbuf[:, 0, 0:nel], wslot_res[0]
                w_issue_upto(i + NW)
                k = i % NW
                return wbuf[:, k, 0:nel], wslot_res[k]

            def af(off, n):
                return arena[:, off:off + n]

            def ab(off, n):
                return arena[:, off:off + n].bitcast(BF16)

            r_pp, r_bc, r_c = res('pp'), res('bc'), res('consts')
            P.op('sp', lambda e: e.dma_start(out=pp[:], in_=pp_d), w=[r_pp], dsem=dsem('pp'))
            P.op('sp', lambda e: e.dma_start(out=bc[:], in_=bc_d), w=[r_bc], dsem=dsem('bc'))
            r_cst = res('cstage')
            P.op('sp', lambda e: e.dma_start(out=ident_f[:], in_=cm_d[:, 0:128]), w=[res('ident_f')], dsem=dsem('c1'))
            P.op('sp', lambda e: e.dma_start(out=trilT[:], in_=cm_d[:, 128:256]), w=[res('trilT')], dsem=dsem('c2'))
            P.op('sp', lambda e: e.dma_start(out=cstage[:, 0:512], in_=cm_d[:, 384:896]), w=[r_cst], dsem=dsem('c3'))
            P.op('sp', lambda e: e.dma_start(out=cstage[0:16, 512:2560], in_=cm_d[0:16, 896:2944]), w=[res('cst2')], dsem=dsem('c4'))
            P.op('sp', lambda e: e.dma_start(out=arena[:, 2560:2560 + L * 4 * 128], in_=sguw_d), w=[res('sguw_f')], dsem=dsem('c5'))
            P.op('pool', lambda e: e.dma_start(out=poolw_b[:].rearrange("p a b -> p (a b)"), in_=poolw_d), w=[res('poolw_b')], dsem=dsem('c6'))
            P.op('dve', lambda e: e.tensor_copy(out=ident_b[:], in_=ident_f[:]), r=[res('ident_f')], w=[res('ident_b')])
            P.op('dve', lambda e: e.tensor_copy(out=causal_b[:].rearrange("p a b -> p (a b)"), in_=cstage[:, 0:512]), r=[r_cst], w=[res('causal_b')])
            P.op('dve', lambda e: e.memset(onehot_b[:].rearrange("p a b -> p (a b)"), 0.0), w=[res('onehot_b')])
            P.op('dve', lambda e: e.tensor_copy(out=onehot_b[0:16].rearrange("p a b -> p (a b)"), in_=cstage[0:16, 512:2560]), r=[res('cst2'), res('onehot_b')], w=[res('onehot_b')])
            for k, v in enumerate((1.0 / 2048, 1.0 / 128, 1.0 / 512, 1.0)):
                P.op('dve', lambda e, k=k, v=v: e.memset(ones_b[:, k, :], v), w=[res(f'ones{k}')])
            P.op('dve', lambda e: e.tensor_tensor(out=sguw_b[:], in0=sguw_f, in1=trilT[:].unsqueeze(1).to_broadcast([128, L * 4, 128]), op=ALU.mult),
                 r=[res('sguw_f'), res('trilT')], w=[res('sguw_b')])
            P.op('act', lambda e: e.activation(out=siluc_b[:], in_=pp[:, PP_C:PP_C + 16], func=AF.Silu), r=[r_pp], w=[res('siluc')])
            P.op('dve', lambda e: e.tensor_scalar(out=misc[:, 0:L], in0=pp[:, PP_QG:PP_QG + L], scalar1=float(128 ** -0.5), scalar2=None, op0=ALU.mult),
                 r=[r_pp], w=[res('qgs')])

            def ones(k):
                return ones_b[:, k, :], res(f'ones{k}')

            P.op('dve', lambda e: e.memset(misc[:, 8:9], EPS), w=[res('eps')])

            def rsqrt(out_ap, in_ap, in_res, out_res, prescale=1.0):
                P.op('act', lambda e: e.activation(out=out_ap, in_=in_ap, func=AF.Sqrt, bias=misc[0:out_ap.shape[0], 8:9], scale=prescale),
                     r=list(in_res) + [res('eps')], w=[out_res])
                P.op('dve', lambda e: e.reciprocal(out=out_ap, in_=out_ap), r=[out_res], w=[out_res])

            def ada_group(l, grp):
                wt, wr = wget(ada_d[l, grp], 4096)
                wv = wt.rearrange("p (c k n) -> p c k n", c=2, k=16)
                bk = take_banks(1)[0]
                for c2 in range(2):
                    for dc in range(16):
                        P.op('pe', lambda e, dc=dc, wv=wv, c2=c2, bk=bk: e.matmul(psum[bk][:, c2:c2 + 1], lhsT=wv[:, c2, dc, :], rhs=siluc_b[:, dc:dc + 1], start=(dc == 0), stop=(dc == 15)),
                             r=[wr, res('siluc')], pw=[bankres[bk]] if (c2 == 0 and dc == 0) else [], fw=[bankres[bk]] if (c2 == 1 and dc == 15) else [])
                o = l * 144 + grp * 2
                P.op('dve', lambda e, o=o, bk=bk: e.tensor_tensor(out=modpp[:, o:o + 2], in0=psum[bk][:, 0:2], in1=pp[:, PP_ADAB + o:PP_ADAB + o + 2], op=ALU.add),
                     r=[bankres[bk], r_pp], w=[res(f'mod{l}_{grp}')])

            def ada_derive(l, i):
                mo = l * 144
                modres = [res(f'mod{l}_{g}') for g in range(i * 24, (i + 1) * 24)]
                o = (l * 3 + i) * 48
                sh = mo + (i * 3 + 0) * 16
                sc = mo + (i * 3 + 1) * 16
                gt = mo + (i * 3 + 2) * 16
                ng = PP_NORMG + (l * 3 + i) * 16
                P.op('dve', lambda e: e.scalar_tensor_tensor(out=der[:, o:o + 16], in0=modpp[:, sc:sc + 16], scalar=1.0, in1=pp[:, ng:ng + 16], op0=ALU.add, op1=ALU.mult),
                     r=modres + [r_pp], w=[res(f'derA{l}{i}')])
                P.op('dve', lambda e: e.tensor_copy(out=der[:, o + 16:o + 32], in_=modpp[:, sh:sh + 16]),
                     r=modres, w=[res(f'derB{l}{i}')])
                gm = 1.0 if i == 1 else 0.5
                P.op('dve', lambda e: e.tensor_scalar(out=der[:, o + 32:o + 48], in0=modpp[:, gt:gt + 16], scalar1=gm, scalar2=None, op0=ALU.mult),
                     r=modres, w=[res(f'derG{l}{i}')])

            def derA(l, i, dc):
                o = (l * 3 + i) * 48
                return der[:, o + dc:o + dc + 1]

            def derB(l, i, dc):
                o = (l * 3 + i) * 48 + 16
                return der[:, o + dc:o + dc + 1]

            def derG(l, i, dc):
                o = (l * 3 + i) * 48 + 32
                return der[:, o + dc:o + dc + 1]

            def der_res(l, i):
                return [res(f'derA{l}{i}'), res(f'derB{l}{i}'), res(f'derG{l}{i}')]

            def ffn(l, i, widx, src, srcname, dst, dstname, bg=()):
                bg = list(bg)
                XN = 0
                HH = 8192
                XC = HH + 22528
                RS = XC + 3072
                SQ = RS + 1024
                ST = SQ + 1024
                TMP = ST + 1024
                assert TMP + 1024 <= PH, (TMP + 1024, PH)
                xn = ab(XN, 8192).rearrange("p (c t) -> p c t", c=16)
                hh = ab(HH, 22528).rearrange("p (c t) -> p c t", c=NFC)
                xc = [af(XC + k * 1024, 1024) for k in range(3)]
                xc_res = [res(f'xc{k}') for k in range(3)]
                xc_sem = [dsem(f'xc{k}') for k in range(3)]
                rstd = af(RS, 1024)
                sq = [ab(SQ + k * 512, 512) for k in range(2)]
                stt = [af(ST + k * 512, 512) for k in range(2)]
                tmp = [af(TMP + k * 512, 512) for k in range(2)]
                xci = [0]
                dres = der_res(l, i)
                o1, o1r = ones(0)

                def xload(dc, t0):
                    k = xci[0] % 3
                    xci[0] += 1
                    slabs = [xr(srcname, dc, (t0 // TM) + s) for s in range(TF // TM)]
                    P.op('sp', lambda e, k=k, dc=dc, t0=t0: e.dma_start(out=xc[k], in_=src[dc, :, t0:t0 + TF]),
                         r=slabs, w=[xc_res[k]], dsem=xc_sem[k])
                    return k

                def mn_p1_load(dc, t0, srcbuf=None):
                    if srcbuf is None:
                        k = xload(dc, t0)
                        xa, xar = xc[k], xc_res[k]
                    else:
                        xa, xar = srcbuf
                    s_ = dc % 2
                    P.op('act', lambda e, xa=xa, s_=s_: e.activation(out=sq[s_], in_=xa, func=AF.Square),
                         r=[xar], w=[res(f'sq{s_}')])

                def mn_p1_mm(dc, sb_):
                    s_ = dc % 2
                    for hf in range(2):
                        P.op('pe', lambda e, s_=s_, hf=hf, dc=dc, b=sb_[hf]: e.matmul(psum[b][:, :], lhsT=o1, rhs=sq[s_][:, hf * 512:(hf + 1) * 512], start=(dc == 0), stop=(dc == 15)),
                             r=[res(f'sq{s_}'), o1r], pw=[bankres[sb_[hf]]] if dc == 0 else [], fw=[bankres[sb_[hf]]] if dc == 15 else [])

                def mn_rsqrt(sb_):
                    for hf in range(2):
                        rsqrt(rstd[:, hf * 512:(hf + 1) * 512], psum[sb_[hf]][:, :], [bankres[sb_[hf]]], res(f'rstd{hf}'))

                def mn_pass2_step(dc, t0, srcbuf=None):
                    if srcbuf is None:
                        k = xload(dc, t0)
                        xa, xar = xc[k], xc_res[k]
                    else:
                        xa, xar = srcbuf
                    for hf in range(2):
                        P.op('dve', lambda e, xa=xa, hf=hf, dc=dc: e.scalar_tensor_tensor(out=tmp[hf], in0=xa[:, hf * 512:(hf + 1) * 512], scalar=derA(l, i, dc), in1=rstd[:, hf * 512:(hf + 1) * 512], op0=ALU.mult, op1=ALU.mult),
                             r=[xar, res(f'rstd{hf}')] + dres, w=[res(f'tmp{hf}')])
                        P.op('act', lambda e, hf=hf, dc=dc: e.activation(out=xn[:, dc, hf * 512:(hf + 1) * 512], in_=tmp[hf], func=AF.Identity, bias=derB(l, i, dc), scale=1.0),
                             r=[res(f'tmp{hf}')] + dres, w=[res(f'xn{dc}')] if hf == 1 else [], pw=[res(f'xn{dc}')] if hf == 0 else [])

                NT = TL // TF
                for tt in range(NT):
                    t0 = tt * TF
                    if tt == 0:
                        xst = af(HH, 16384).rearrange("p (c t) -> p c t", c=16)
                        hh_all = [res(f'hh{fc}_{hf}') for fc in range(NFC) for hf in range(2)]
                        for dc in range(16):
                            slabs0 = [xr(srcname, dc, s4) for s4 in range(TF // TM)]
                            P.op('sp', lambda e, dc=dc: e.dma_start(out=xst[:, dc, :], in_=src[dc, :, 0:TF]),
                                 r=slabs0, w=[res(f'xst{dc}')] + (hh_all if dc == 0 else []), dsem=dsem(f'xst{dc}'))
                        sb_ = take_banks(2)
                        for dc in range(16):
                            mn_p1_load(dc, t0, (xst[:, dc, :], res(f'xst{dc}')))
                            mn_p1_mm(dc, sb_)
                        mn_rsqrt(sb_)
                        for dc in range(16):
                            mn_pass2_step(dc, t0, (xst[:, dc, :], res(f'xst{dc}')))
                    for fc in range(NFC):
                        wt, wr = wget(w13_d[widx, fc], 4096)
                        wv = wt.rearrange("p (a k n) -> p a k n", a=2, k=16)
                        bk = take_banks(4)
                        for dc in range(16):
                            for a_ in range(2):
                                for hf in range(2):
                                    b = bk[a_ * 2 + hf]
                                    P.op('pe', lambda e, a_=a_, hf=hf, dc=dc, b=b, wv=wv: e.matmul(psum[b][:, :], lhsT=wv[:, a_, dc, :], rhs=xn[:, dc, hf * 512:(hf + 1) * 512], start=(dc == 0), stop=(dc == 15)),
                                         r=[wr, res(f'xn{dc}')], pw=[bankres[b]] if dc == 0 else [], fw=[bankres[b]] if dc == 15 else [])
                        for hf in range(2):
                            P.op('act', lambda e, hf=hf, b=bk[hf]: e.activation(out=stt[hf], in_=psum[b][:, :], func=AF.Silu),
                                 r=[bankres[bk[hf]]], w=[res(f'stt{hf}')])
                            P.op('dve', lambda e, hf=hf, b=bk[2 + hf], fc=fc: e.tensor_tensor(out=hh[:, fc, hf * 512:(hf + 1) * 512], in0=psum[b][:, :], in1=stt[hf], op=ALU.mult),
                                 r=[bankres[bk[2 + hf]], res(f'stt{hf}')], w=[res(f'hh{fc}_{hf}')])
                        if bg:
                            bg.pop(0)()
                    nxt = tt + 1 < NT
                    if nxt:
                        sbn = take_banks(2)
                        reserved.update(sbn)
                    for dt in range(16):
                        wt, wr = wget(w2_d[widx, dt], DFF)
                        wv = wt.rearrange("p (f n) -> p f n", f=NFC)
                        bk = take_banks(2)
                        k = xload(dt, t0)
                        for fc in range(NFC):
                            for hf in range(2):
                                b = bk[hf]
                                P.op('pe', lambda e, hf=hf, fc=fc, b=b, wv=wv: e.matmul(psum[b][:, :], lhsT=wv[:, fc, :], rhs=hh[:, fc, hf * 512:(hf + 1) * 512], start=(fc == 0), stop=(fc == NFC - 1)),
                                     r=[wr, res(f'hh{fc}_{hf}')], pw=[bankres[b]] if fc == 0 else [], fw=[bankres[b]] if fc == NFC - 1 else [])
                        if nxt and 1 <= dt <= 8:
                            mn_p1_mm(2 * (dt - 1), sbn)
                            mn_p1_mm(2 * (dt - 1) + 1, sbn)
                            if dt == 8:
                                mn_rsqrt(sbn)
                                reserved.difference_update(sbn)
                        for hf in range(2):
                            P.op('dve', lambda e, hf=hf, b=bk[hf], k=k, dt=dt: e.scalar_tensor_tensor(out=xc[k][:, hf * 512:(hf + 1) * 512], in0=psum[b][:, :], scalar=derG(l, i, dt), in1=xc[k][:, hf * 512:(hf + 1) * 512], op0=ALU.mult, op1=ALU.add),
                                 r=[bankres[bk[hf]], xc_res[k]] + dres, fw=[xc_res[k]] if hf == 1 else [])
                        slabs = [xr(dstname, dt, (t0 // TM) + s) for s in range(TF // TM)]
                        P.op('sp', lambda e, k=k, dt=dt, t0=t0: e.dma_start(out=dst[dt, :, t0:t0 + TF], in_=xc[k]),
                             r=[xc_res[k]], w=slabs, dsem=xc_sem[k])
                        if nxt:
                            if dt < 8:
                                mn_p1_load(2 * dt, t0 + TF)
                                mn_p1_load(2 * dt + 1, t0 + TF)
                            else:
                                mn_pass2_step(2 * (dt - 8), t0 + TF)
                                mn_pass2_step(2 * (dt - 8) + 1, t0 + TF)
                while bg:
                    bg.pop(0)()

            def barrier():
                if dry:
                    return
                toks = []
                for e in ENGS:
                    for ent in reversed(P.q[e]):
                        if ent[0] is not None:
                            if ent[2].sem is None:
                                toks.append(ent[2])
                            break
                toks += list(P.pending_dma.values())
                for e in ENGS:
                    P.op(e, None, extra=toks)

            def mixer(l, src, srcname, dst, dstname):
                KL, KP, VL, VP = 0, 4096, 8192, 12288
                XT = 16384
                XN = XT + 4096
                YC = XN + 2048
                RS = YC + 4096
                TP = RS + 1024
                T2 = TP + 768
                O_TQ, O_TRS = T2 + 0, T2 + 128
                O_SM, O_M8, O_SEL, O_SS4, O_JUNK, O_T16, O_TG, O_HST, O_HPV = T2 + 384, T2 + 448, T2 + 480, T2 + 544, T2 + 552, T2 + 680, T2 + 696, T2 + 712, T2 + 840
                O_P0B = T2 + 968
                O_ZB = O_P0B + 1088
                O_G = O_ZB + 1040
                O_QT = O_G + 2176
                O_PT = O_QT + 512
                O_QF = O_PT + 512
                O_MSK = O_QF + 1024
                O_DD = O_MSK + 512
                KM = O_DD + 512
                HL = KM + 64
                END = HL + 128 + 1024 + 256
                assert END <= PH, (END, PH)
                Kl = ab(KL, 4096).rearrange("p (h t) -> p h t", h=4)
                Kp = ab(KP, 4096).rearrange("p (h t) -> p h t", h=4)
                Vl = ab(VL, 4096).rearrange("p (c n) -> p c n", c=16)
                Vp = ab(VP, 4096).rearrange("p (c n) -> p c n", c=16)
                xt = af(XT, 4096).rearrange("p (c t) -> p c t", c=16)
                xn = ab(XN, 2048).rearrange("p (c t) -> p c t", c=16)
                sqy = ab(YC, 2048).rearrange("p (c t) -> p c t", c=16)
                yc = af(YC, 4096).rearrange("p (c t) -> p c t", c=16)
                rs4 = af(RS, 1024).rearrange("p (c t) -> p c t", c=4)
                kmT = af(KM, 64).rearrange("p (h n) -> p h n", h=4)
                p0h = af(HL, 64).rearrange("p (g t) -> p g t", g=4)
                zh = af(HL + 64, 64).rearrange("p (g t) -> p g t", g=4)
                t_rstd = af(TP, 256)
                t_tmp = [af(TP + 256 + k * 256, 256) for k in range(2)]
                tq = ab(O_TQ, 128)
                trs = af(O_TRS, 256)
                sm = af(O_SM, 64).rearrange("p (h n) -> p h n", h=4)
                m8 = af(O_M8, 32).rearrange("p (h n) -> p h n", h=4)
                sel = af(O_SEL, 64).rearrange("p (h n) -> p h n", h=4)
                ss4 = af(O_SS4, 4)
                junk = af(O_JUNK, 128)
                t16 = af(O_T16, 16)
                tg = af(O_TG, 16)
                hst = af(O_HST, 128).rearrange("p (a g t) -> p a g t", a=2, g=4)
                hprev = af(O_HPV, 128)
                p0b = af(O_P0B, 1088).rearrange("p (g t) -> p g t", g=4)
                zb = af(O_ZB, 1040).rearrange("p (g t) -> p g t", g=4)
                G = [af(O_G + k * 544, 544) for k in range(4)]
                rG = [res(f'G{k}') for k in range(4)]
                qT = ab(O_QT, 512).rearrange("p (h t) -> p h t", h=4)
                pT = [ab(O_PT + k * 256, 256).rearrange("p (a t) -> p a t", a=2) for k in range(2)] + [ab(HL + 128 + 1024, 256).rearrange("p (a t) -> p a t", a=2)]
                qf = af(O_QF, 1024).rearrange("p (h t) -> p h t", h=4)
                mskT = ab(O_MSK, 512).rearrange("p (h t) -> p h t", h=4)
                for h in range(4):
                    P.op('dve', lambda e, h=h: e.memset(mskT[:, h, :], 0.0), w=[res(f'mskT{h}')])
                dres = der_res(l, 1)
                o2048, o2048r = ones(0)
                o128, o128r = ones(1)
                o512, o512r = ones(2)
                o1, o1r = ones(3)
                r_xt = res('m_xt')
                rxn = [res(f'm_xn{dc}') for dc in range(16)]
                ryc = [res(f'yc{c}') for c in range(16)]
                xt_sem = dsem('m_xt')

                def load_xt(j):
                    slabs = [xr(srcname, dc, j) for dc in range(16)]
                    P.op('sp', lambda e, j=j: e.dma_start(out=xt, in_=src[:, :, j * TM:(j + 1) * TM].rearrange("c p t -> p c t")),
                         r=slabs, w=[r_xt], dsem=xt_sem)

                def modnorm_A():
                    bk = take_banks(1)[0]
                    P.op('act', lambda e: e.activation(out=sqy, in_=xt, func=AF.Square), r=[r_xt], w=ryc)
                    for dc in range(16):
                        P.op('pe', lambda e, dc=dc: e.matmul(psum[bk][:, 0:256], lhsT=o2048, rhs=sqy[:, dc, :], start=(dc == 0), stop=(dc == 15)),
                             r=ryc + [o2048r], pw=[bankres[bk]] if dc == 0 else [], fw=[bankres[bk]] if dc == 15 else [])
                    rsqrt(t_rstd, psum[bk][:, 0:256], [bankres[bk]], res('m_rstd'))
                    for q4 in range(4):
                        P.op('dve', lambda e, q4=q4: e.tensor_tensor(out=xt[:, q4 * 4:(q4 + 1) * 4, :], in0=xt[:, q4 * 4:(q4 + 1) * 4, :], in1=t_rstd.unsqueeze(1).to_broadcast([128, 4, 256]), op=ALU.mult),
                             r=[r_xt, res('m_rstd')], w=[res(f'm_xs{q4}')])

                def modnorm_B():
                    for dc in range(16):
                        q4 = dc // 4
                        if True:
                            P.op('act', lambda e, dc=dc: e.activation(out=xn[:, dc, :], in_=xt[:, dc, :], func=AF.Identity, bias=derB(l, 1, dc), scale=derA(l, 1, dc)),
                                 r=[res(f'm_xs{q4}'), r_xt] + dres, w=[rxn[dc]])
                        else:
                            P.op('dve', lambda e, dc=dc: e.tensor_scalar(out=xn[:, dc, :], in0=xt[:, dc, :], scalar1=derA(l, 1, dc), scalar2=derB(l, 1, dc), op0=ALU.mult, op1=ALU.add),
                                 r=[res(f'm_xs{q4}'), r_xt] + dres, w=[rxn[dc]])

                def proj_fm(chunks, N=256, c0=0, wg=None):
                    wg = wg or wget
                    out = []
                    for gi in range(0, len(chunks), 2):
                        cc = chunks[gi]
                        assert cc % 2 == 0 and chunks[gi + 1] == cc + 1
                        wt, wr = wg(winf_d[l, cc // 2], 4096)
                        wv = wt.rearrange("p (c k n) -> p c k n", c=2, k=16)
                        bk = take_banks(1)[0]
                        for c2 in range(2):
                            for dc in range(16):
                                P.op('pe', lambda e, c2=c2, dc=dc, wv=wv, bk=bk: e.matmul(psum[bk][:, c2 * 256:c2 * 256 + N], lhsT=wv[:, c2, dc, :], rhs=xn[:, dc, c0:c0 + N], start=(dc == 0), stop=(dc == 15)),
                                     r=[wr, rxn[dc]], pw=[bankres[bk]] if (c2 == 0 and dc == 0) else [], fw=[bankres[bk]] if (c2 == 1 and dc == 15) else [])
                            out.append((bk, c2 * 256))
                    return out

                def proj_tm(which, wg=None):
                    wg = wg or wget
                    bks = take_banks(2)
                    for half in range(2):
                        wt, wr = wg(wint_d[l, which, half], 4096)
                        wv = wt.rearrange("p (k n) -> p k n", k=8)
                        for s_ in range(2):
                            for k8 in range(8):
                                dc = half * 8 + k8
                                P.op('pe', lambda e, s_=s_, k8=k8, dc=dc, wv=wv, b=bks[s_]: e.matmul(psum[b][:, :], lhsT=xn[:, dc, s_ * 128:(s_ + 1) * 128], rhs=wv[:, k8, :], start=(dc == 0), stop=(dc == 15)),
                                     r=[wr, rxn[dc]], pw=[bankres[bks[s_]]] if dc == 0 else [], fw=[bankres[bks[s_]]] if dc == 15 else [])
                    return bks

                def headnorm(bk, off, gain_ap, gain_res, dst_b, dst_res, dst_f, dst_f_res):
                    sb_ = take_banks(1)[0]
                    P.op('act', lambda e: e.activation(out=tq, in_=psum[bk][:, off:off + 256], func=AF.Square),
                         r=[bankres[bk]], w=[res('hn_sq')])
                    P.op('pe', lambda e: e.matmul(psum[sb_][:, 0:256], lhsT=o128, rhs=tq, start=True, stop=True),
                         r=[res('hn_sq'), o128r], w=[bankres[sb_]])
                    rsqrt(trs, psum[sb_][:, 0:256], [bankres[sb_]], res('hn_rs'))
                    P.op('dve', lambda e: e.scalar_tensor_tensor(out=dst_f, in0=psum[bk][:, off:off + 256], scalar=gain_ap, in1=trs, op0=ALU.mult, op1=ALU.mult),
                         r=[bankres[bk], res('hn_rs')] + gain_res, w=[dst_f_res])
                    P.op('act', lambda e: e.activation(out=dst_b, in_=dst_f, func=AF.Copy), r=[dst_f_res], w=[dst_res])

                rqf = [res(f'qf{h}') for h in range(4)]

                def headnorm4(outs, gain_ap, gain_res, dst_b, dst_res):
                    tqs = [pT[h // 2][:, h % 2, 0:256] for h in range(4)]
                    tqr = [res(f'pT{h // 2}') for h in range(4)]
                    sbs = []
                    for h in range(4):
                        bk, off = outs[h]
                        P.op('act', lambda e, h=h, bk=bk, off=off: e.activation(out=tqs[h], in_=psum[bk][:, off:off + 256], func=AF.Square),
                             r=[bankres[bk]], w=[res(f'tq{h}')], pw=[tqr[h]] if h % 2 == 0 else [])
                    for h in range(4):
                        sb_ = take_banks(1)[0]
                        sbs.append(sb_)
                        P.op('pe', lambda e, h=h, sb_=sb_: e.matmul(psum[sb_][:, 0:256], lhsT=o128, rhs=tqs[h], start=True, stop=True),
                             r=[res(f'tq{h}'), o128r], w=[bankres[sb_]])
                    for h in range(4):
                        rsqrt(qf[:, h, :], psum[sbs[h]][:, 0:256], [bankres[sbs[h]]], rqf[h])
                    for h in range(4):
                        bk, off = outs[h]
                        P.op('dve', lambda e, h=h, bk=bk, off=off: e.scalar_tensor_tensor(out=qf[:, h, :], in0=psum[bk][:, off:off + 256], scalar=gain_ap, in1=qf[:, h, :], op0=ALU.mult, op1=ALU.mult),
                             r=[bankres[bk], rqf[h]] + gain_res, w=[rqf[h]])
                    for h in range(4):
                        P.op('act', lambda e, h=h: e.activation(out=dst_b[h], in_=qf[:, h, :], func=AF.Copy), r=[rqf[h]], w=[dst_res[h]])

                vst = [G[2][:, 0:512], G[3][:, 0:512]]
                gv_iv = gv_i.rearrange("r c -> (r c)").rearrange("(t n) -> t n", n=512)
                load_xt(0)
                modnorm_A()
                for j in range(NTM):
                    modnorm_B()
                    load_xt(j + 1 if j + 1 < NTM else 0)
                    outs = proj_fm([16, 17, 18, 19])
                    modnorm_A()
                    headnorm4(outs, pp[:, PP_KG + l:PP_KG + l + 1], [r_pp], [Kl[:, h, j * TM:(j + 1) * TM] for h in range(4)], [res(f'm_Kl_{h}_{j}') for h in range(4)])
                    P.op('sp', lambda e, j=j: e.dma_start(out=gk_i[:, j * TM:(j + 1) * TM].rearrange("(h p) t -> p h t", p=128), in_=Kl[:, :, j * TM:(j + 1) * TM]),
                         r=[res(f'm_Kl_{h}_{j}') for h in range(4)], w=[res(f'gin_k{j}')], dsem=dsem('m_kst'))
                    bks = proj_tm(1)
                    for s_ in range(2):
                        P.op('act', lambda e, s_=s_, j=j, b=bks[s_]: e.activation(out=Vl[:, j * 2 + s_, :], in_=psum[b][:, :], func=AF.Copy),
                             r=[bankres[bks[s_]]], w=[res(f'm_Vl_{j}_{s_}')])
                    P.op('sp', lambda e, j=j: e.dma_start(out=gv_iv[j * TM:(j + 1) * TM, :].rearrange("(s p) n -> p s n", p=128), in_=Vl[:, 2 * j:2 * j + 2, :]),
                         r=[res(f'm_Vl_{j}_0'), res(f'm_Vl_{j}_1')], w=[res(f'gin_v{j}')], dsem=dsem('m_vst'))
                    if j == NTM - 1:
                        o_p0 = proj_fm([0, 1, 2, 3], N=16, c0=240)
                        o_gc = proj_fm([28, 29, 30, 31], N=16, c0=240)
                        o_h = proj_fm([32, 33, 34, 35], N=16, c0=240)
                        for g in range(4):
                            bk, off = o_p0[g]
                            P.op('act', lambda e, g=g, bk=bk, off=off: e.activation(out=hst[:, 0, g, :], in_=psum[bk][:, off:off + 16], func=AF.Copy),
                                 r=[bankres[bk]], w=[res('hst')])
                            bk, off = o_gc[g]
                            P.op('act', lambda e, bk=bk, off=off: e.activation(out=tg, in_=psum[bk][:, off:off + 16], func=AF.Copy),
                                 r=[bankres[bk]], w=[res('tg')])
                            bk2, off2 = o_h[g]
                            P.op('dve', lambda e, g=g, bk2=bk2, off2=off2: e.tensor_tensor(out=hst[:, 1, g, :], in0=psum[bk2][:, off2:off2 + 16], in1=tg, op=ALU.mult),
                                 r=[bankres[bk2], res('tg'), res('hst')], w=[res('hst')])
                        P.op('sp', lambda e: e.dma_start(out=gh_i[0:8, :].rearrange("r c -> (r c)").rearrange("(p x) -> p x", p=128), in_=af(O_HST, 128)),
                             r=[res('hst')], w=[res('gin_h')], dsem=dsem('m_hst'))
                for h in range(4):
                    P.op('dve', lambda e, h=h: e.tensor_reduce(out=kmT[:, h, 8:16], in_=Kl[:, h, :].rearrange("p (n t) -> p n t", t=256), axis=AX.X, op=ALU.add),
                         r=[res(f'm_Kl_{h}_{j}') for j in range(NTM)], w=[res(f'kmL{h}')])
                RG = [[0, 1], [2, 3], [4, 5], [6, 7]]

                def gather(name, i_ap, o_ap, rlist):
                    P.op('pool', lambda e: e.collective_compute("AllGather", ALU.bypass, replica_groups=RG, ins=[i_ap.opt()], outs=[o_ap.opt()]),
                         r=rlist, w=[res('go_' + name)], dsem=dsem('cc_' + name), dinc=1)

                gather('k', gk_i, gk_o, [res(f'gin_k{j}') for j in range(NTM)])
                gather('v', gv_i, gv_o, [res(f'gin_v{j}') for j in range(NTM)])
                gather('h', gh_i, gh_o, [res('gin_h')])
                P.op('pool', lambda e: e.dma_start(out=Kp, in_=gk_o[0:512, :].rearrange("(h p) t -> p h t", p=128)),
                     r=[res('go_k')], w=[res('m_Kp0'), res('m_Kp1')], dsem=dsem('m_kp'))
                gvv = gv_o[0:512, :].rearrange("r c -> (r c)").rearrange("(t n) -> t n", n=512)
                P.op('pool', lambda e: e.dma_start(out=Vp, in_=gvv.rearrange("(c p) n -> p c n", p=128)),
                     r=[res('go_v')], w=[res('m_Vp0'), res('m_Vp1')], dsem=dsem('m_vp'))
                P.op('sp', lambda e: e.dma_start(out=hprev, in_=gh_o[0:8, :].rearrange("r c -> (r c)").rearrange("(p x) -> p x", p=128)),
                     r=[res('go_h')], w=[res('hprev')], dsem=dsem('m_hp'))
                P.op('dve', lambda e: e.tensor_scalar(out=af(HL, 128), in0=hprev, scalar1=pp[:, PP_HALF:PP_HALF + 1], scalar2=None, op0=ALU.mult),
                     r=[res('hprev'), r_pp], w=[res('p0h'), res('zh')])
                for h in range(4):
                    P.op('dve', lambda e, h=h: e.tensor_reduce(out=kmT[:, h, 0:8], in_=Kp[:, h, :].rearrange("p (n t) -> p n t", t=256), axis=AX.X, op=ALU.add),
                         r=[res(f'm_Kp{h // 2}')], w=[res(f'kmP{h}')])
                P.op('dve', lambda e: e.tensor_scalar(out=af(KM, 64), in0=af(KM, 64), scalar1=1.0 / 256, scalar2=None, op0=ALU.mult),
                     r=[res(f'kmP{h}') for h in range(4)] + [res(f'kmL{h}') for h in range(4)], w=[res('kmT')])

                def gelu_ip(buf, n, bres):
                    t_ = G[1][:, 0:n]
                    P.op('dve', lambda e: e.tensor_tensor(out=t_, in0=buf, in1=buf, op=ALU.mult), r=[bres], w=[rG[1]])
                    P.op('dve', lambda e: e.tensor_scalar(out=t_, in0=t_, scalar1=0.044715, scalar2=1.0, op0=ALU.mult, op1=ALU.add), r=[rG[1]], w=[rG[1]])
                    P.op('dve', lambda e: e.tensor_tensor(out=t_, in0=t_, in1=buf, op=ALU.mult), r=[rG[1], bres], w=[rG[1]])
                    P.op('act', lambda e: e.activation(out=t_, in_=t_, func=AF.Sigmoid, scale=1.5957691216057308), r=[rG[1]], w=[rG[1]])
                    P.op('dve', lambda e: e.tensor_tensor(out=buf, in0=t_, in1=buf, op=ALU.mult), r=[rG[1], bres], w=[bres])

                rp0 = [res(f'p0b{g}') for g in range(4)]
                rzb = [res(f'zb{g}') for g in range(4)]
                rdd = [res(f'dd{g}') for g in range(4)]
                ddb = ab(O_DD, 512).rearrange("p (g t) -> p g t", g=4)
                vr = [G[2][:, 0:512], G[3][:, 0:512]]
                vh = G[0][:, 0:512].bitcast(BF16).rearrange("p (s n) -> p s n", s=2)
                acc = G[1][:, 0:256]
                xcb = [af(HL + 128 + k * 512, 512).rearrange("p (c t) -> p c t", c=2) for k in range(2)]
                rxcb = [res(f'xcb{k}') for k in range(2)]
                xcb_sem = [dsem(f'xcb{k}') for k in range(2)]
                for j in range(NTM):
                    modnorm_B()
                    if j + 1 < NTM:
                        load_xt(j + 1)
                    wi = [0]

                    def wget2(src_, nel, j=j):
                        idx = wi[0]
                        wi[0] += 1
                        if j == 0:
                            wt_, wr_ = wget(src_, nel)
                            k_ = wstate['last_k']
                            P.op('sp', lambda e, idx=idx, wt_=wt_: e.dma_start(out=wsc[l, idx], in_=wt_),
                                 r=[wr_], w=[res(f'wsc{l}_{idx}')], dsem=dsem(f'wst{k_}'))
                            return wt_, wr_
                        return wget(wsc[l, idx], nel, rkey=f'wsc{l}_{idx}')

                    tb = PP_TABS + j * 48
                    rankb = pp[:, tb:tb + 16].unsqueeze(1).to_broadcast([128, 4, 16])
                    notown = pp[:, tb + 16:tb + 32].unsqueeze(1).to_broadcast([128, 4, 16])
                    finalb = pp[:, tb + 32:tb + 48].unsqueeze(1).to_broadcast([128, 4, 16])
                    o_q = proj_fm([12, 13, 14, 15], wg=wget2)
                    headnorm4(o_q, misc[:, l:l + 1], [res('qgs')], [qT[:, h, :] for h in range(4)], [res(f'qT{h}') for h in range(4)])
                    mbk = take_banks(2)
                    sel_toks = []
                    sbks = take_banks(2)
                    for s_ in range(2):
                        sbk = sbks[s_]
                        for h in range(4):
                            P.op('pe', lambda e, h=h, s_=s_, sbk=sbk: e.matmul(psum[sbk][:, h * 16:(h + 1) * 16], lhsT=qf[:, h, s_ * 128:(s_ + 1) * 128], rhs=kmT[:, h, :], start=True, stop=True),
                                 r=[rqf[h], res('kmT')], pw=[bankres[sbk]] if h == 0 else [], fw=[bankres[sbk]] if h == 3 else [])
                    o_p0 = proj_fm([0, 1, 2, 3], wg=wget2)
                    P.op('pool', lambda e: e.tensor_copy(out=p0b[:, :, 0:16], in_=p0h), r=[res('p0h')], w=[res('p0halo')])
                    for g in range(4):
                        bk, off = o_p0[g]
                        P.op('act', lambda e, g=g, bk=bk, off=off: e.activation(out=p0b[:, g, 16:272], in_=psum[bk][:, off:off + 256], func=AF.Copy),
                             r=[bankres[bk]], w=[rp0[g]])
                    P.op('pool', lambda e: e.tensor_copy(out=p0h, in_=p0b[:, :, 256:272]), r=rp0, w=[res('p0h')])
                    o_u = proj_fm([4, 5, 6, 7], wg=wget2)
                    for h in range(4):
                        bk, off = o_u[h]
                        P.op('act', lambda e, h=h, bk=bk, off=off: e.activation(out=yc[:, 4 + h, :], in_=psum[bk][:, off:off + 256], func=AF.Copy),
                             r=[bankres[bk]], w=[ryc[4 + h]])
                    bks = proj_tm(0, wg=wget2)
                    for s_ in range(2):
                        P.op('act', lambda e, s_=s_, b=bks[s_]: e.activation(out=vr[s_], in_=psum[b][:, :], func=AF.Copy),
                             r=[bankres[bks[s_]]], w=[rG[2 + s_]])
                    for s_ in range(2):
                        sbk = sbks[s_]
                        P.op('dve', lambda e, sbk=sbk, rankb=rankb: e.tensor_tensor(out=sm, in0=psum[sbk][:, 0:64].rearrange("p (h n) -> p h n", h=4), in1=rankb, op=ALU.add),
                             r=[bankres[sbk], r_pp], w=[res('sm')])
                        for h in range(4):
                            P.op('dve', lambda e, h=h: e.max(out=m8[:, h, :], in_=sm[:, h, :]), r=[res('sm')], w=[res('m8')])
                        for h in range(4):
                            P.op('dve', lambda e, h=h: e.tensor_scalar(out=sel[:, h, :], in0=sm[:, h, :], scalar1=m8[:, h, 2:3], scalar2=None, op0=ALU.is_ge),
                                 r=[res('sm'), res('m8')], w=[res('sel')])
                        P.op('dve', lambda e: e.tensor_scalar(out=sel, in0=sel, scalar1=-1.0, scalar2=BIG, op0=ALU.add, op1=ALU.mult), r=[res('sel')], w=[res('sel')])
                        P.op('dve', lambda e, notown=notown: e.tensor_tensor(out=sel, in0=sel, in1=notown, op=ALU.mult), r=[res('sel'), r_pp], w=[res('sel')])
                        P.op('dve', lambda e, finalb=finalb: e.tensor_tensor(out=sel, in0=sel, in1=finalb, op=ALU.add), r=[res('sel'), r_pp], w=[res('sel')])
                        for h in range(4):
                            b = mbk[h // 2]
                            P.op('pe', lambda e, h=h, s_=s_, b=b: e.transpose(psum[b][0:16, (h % 2) * 256 + s_ * 128:(h % 2) * 256 + (s_ + 1) * 128], sel[:, h, :], ident_f[:]),
                                 r=[res('sel'), res('ident_f')], pw=[bankres[b]] if (s_ == 0 and h % 2 == 0) else [], fw=[bankres[b]] if (s_ == 1 and h % 2 == 1) else [])
                    for h in range(4):
                        b = mbk[h // 2]
                        P.op('act', lambda e, h=h, b=b: e.activation(out=mskT[0:16, h, :], in_=psum[b][0:16, (h % 2) * 256:(h % 2 + 1) * 256], func=AF.Copy),
                             r=[bankres[b]], w=[res(f'mskT{h}')])
                    o_gc = proj_fm([28, 29, 30, 31], wg=wget2)
                    P.op('pool', lambda e: e.tensor_copy(out=zb[:, :, 0:2], in_=zh[:, :, 14:16]), r=[res('zh')], w=[res('zhalo')])
                    for g in range(4):
                        bk, off = o_gc[g]
                        P.op('act', lambda e, g=g, bk=bk, off=off: e.activation(out=zb[:, g, 2:258], in_=psum[bk][:, off:off + 256], func=AF.Copy), r=[bankres[bk]], w=[rzb[g]])
                    o_hh = proj_fm([32, 33, 34, 35], wg=wget2)
                    for g in range(4):
                        bk2, off2 = o_hh[g]
                        P.op('dve', lambda e, g=g, bk2=bk2, off2=off2: e.tensor_tensor(out=zb[:, g, 2:258], in0=psum[bk2][:, off2:off2 + 256], in1=zb[:, g, 2:258], op=ALU.mult),
                             r=[bankres[bk2], rzb[g]], w=[rzb[g]])
                    P.op('pool', lambda e: e.tensor_copy(out=zh[:, :, 14:16], in_=zb[:, :, 256:258]), r=rzb, w=[res('zh')])
                    o_gb = proj_fm([24, 25, 26, 27], wg=wget2)
                    for g in range(4):
                        bk3, off3 = o_gb[g]
                        P.op('act', lambda e, g=g, bk3=bk3, off3=off3: e.activation(out=yc[:, 12 + g, :], in_=psum[bk3][:, off3:off3 + 256], func=AF.Copy), r=[bankres[bk3]], w=[ryc[12 + g]])

                    def pool_unit(g):
                        wn = 2 << g
                        srcb = p0b[:, g, :]
                        cur_in, cur_res = srcb, [rp0[g], res('p0halo')]
                        wkb = G[0].rearrange("p (a t) -> p a t", a=2)
                        for st_ in range(g + 1):
                            sh = 1 << st_
                            o_ = wkb[:, st_ % 2, :]
                            P.op('pool', lambda e, o_=o_, cur_in=cur_in, sh=sh: e.tensor_tensor(out=o_[:, sh:272], in0=cur_in[:, sh:272], in1=cur_in[:, 0:272 - sh], op=ALU.add),
                                 r=cur_res, w=[rG[0]])
                            cur_in, cur_res = o_, [rG[0]]
                        dd = ddb[:, g, :]
                        P.op('dve', lambda e, cur_in=cur_in, srcb=srcb, wn=wn, dd=dd: e.scalar_tensor_tensor(out=dd, in0=cur_in[:, 16:272], scalar=1.0 / wn, in1=srcb[:, 16:272], op0=ALU.mult, op1=ALU.subtract),
                             r=cur_res + [rp0[g]], w=[rdd[g]])
                        if j == 0:
                            P.op('dve', lambda e, cur_in=cur_in, g=g: e.tensor_tensor(out=t16, in0=cur_in[:, 16:32], in1=pp[:, PP_INVC + g * 16:PP_INVC + (g + 1) * 16], op=ALU.mult),
                                 r=cur_res + [r_pp], w=[res('t16')])
                            P.op('dve', lambda e, srcb=srcb, dd=dd: e.tensor_tensor(out=dd[:, 0:16], in0=t16, in1=srcb[:, 16:32], op=ALU.subtract),
                                 r=[res('t16'), rp0[g], rdd[g]], w=[rdd[g]])

                    def sgu_v_unit(s_):
                        gelu_ip(vr[s_], 512, rG[2 + s_])
                        for h in range(4):
                            P.op('act', lambda e, h=h, s_=s_: e.activation(out=junk, in_=vr[s_][:, h * 128:(h + 1) * 128], func=AF.Square, accum_out=ss4[:, h:h + 1]),
                                 r=[rG[2 + s_]], w=[res('ss4'), res('junk')])
                        rsqrt(ss4, ss4, [res('ss4')], res('ss4'), prescale=1.0 / 128)
                        for h in range(4):
                            P.op('dve', lambda e, h=h, s_=s_: e.scalar_tensor_tensor(out=vh[:, s_, h * 128:(h + 1) * 128], in0=vr[s_][:, h * 128:(h + 1) * 128], scalar=ss4[:, h:h + 1],
                                                                                     in1=bc[:, BC_SGUG + l * 512 + h * 128:BC_SGUG + l * 512 + (h + 1) * 128], op0=ALU.mult, op1=ALU.mult),
                                 r=[rG[2 + s_], res('ss4'), r_bc], w=[res(f'vh{s_}')], pw=[rG[0]] if h == 0 else [])

                    def conv_unit(g):
                        zr = [rzb[g], res('zhalo')]
                        cw = PP_CONVW + l * 12
                        P.op('dve', lambda e, g=g, cw=cw: e.tensor_scalar(out=acc, in0=zb[:, g, 0:256], scalar1=pp[:, cw + g:cw + g + 1], scalar2=None, op0=ALU.mult),
                             r=zr + [r_pp], w=[rG[1]])
                        for jj in (1, 2):
                            P.op('dve', lambda e, g=g, cw=cw, jj=jj: e.scalar_tensor_tensor(out=acc, in0=zb[:, g, jj:jj + 256], scalar=pp[:, cw + jj * 4 + g:cw + jj * 4 + g + 1], in1=acc, op0=ALU.mult, op1=ALU.add),
                                 r=zr + [r_pp, rG[1]], w=[rG[1]])
                        P.op('dve', lambda e, g=g: e.tensor_tensor(out=yc[:, 12 + g, :], in0=yc[:, 12 + g, :], in1=acc, op=ALU.mult),
                             r=[ryc[12 + g], rG[1]], w=[ryc[12 + g]])

                    units = [lambda g=g: pool_unit(g) for g in range(4)] + [lambda s_=s_: sgu_v_unit(s_) for s_ in range(2)] \
                        + [lambda h=h: gelu_ip(yc[:, 4 + h, :], 256, ryc[4 + h]) for h in range(4)] + [lambda g=g: conv_unit(g) for g in range(4)]
                    upos = [0]

                    def run_units(n):
                        for _ in range(n):
                            if upos[0] < len(units):
                                units[upos[0]]()
                                upos[0] += 1

                    npair = 8 + (j + 1)

                    def emit_S(h, pr, gi):
                        sbk = 4 + (gi % 3)
                        for i2 in range(2):
                            kt = pr * 2 + i2
                            if kt < 16:
                                Ksrc, Kres, blk, own = Kp[:, h, kt * 128:(kt + 1) * 128], [res(f'm_Kp{h // 2}')], kt // 2, False
                            else:
                                kl = kt - 16
                                Ksrc, Kres, blk, own = Kl[:, h, kl * 128:(kl + 1) * 128], [res(f'm_Kl_{h}_{kl // 2}')], 8 + kl // 2, (kl // 2 == j)
                            oc = i2 * 256
                            P.op('pe', lambda e, Ksrc=Ksrc, oc=oc, h=h, sbk=sbk: e.matmul(psum[sbk][:, oc:oc + 256], lhsT=Ksrc, rhs=qT[:, h, :], start=True, stop=False),
                                 r=Kres + [res(f'qT{h}')], pw=[bankres[sbk]] if i2 == 0 else [])
                            P.op('pe', lambda e, blk=blk, oc=oc, h=h, sbk=sbk, own=own: e.matmul(psum[sbk][:, oc:oc + 256], lhsT=onehot_b[:, blk, :], rhs=mskT[:, h, :], start=False, stop=(not own)),
                                 r=[res('onehot_b'), res(f'mskT{h}')], fw=[bankres[sbk]] if (i2 == 1 and not own) else [])
                            if own:
                                P.op('pe', lambda e, oc=oc, sbk=sbk, i2=i2: e.matmul(psum[sbk][:, oc:oc + 256], lhsT=ident_b[:], rhs=causal_b[:, i2, :], start=False, stop=True),
                                     r=[res('ident_b'), res('causal_b')], fw=[bankres[sbk]] if i2 == 1 else [])
                        pk = gi % 3
                        P.op('act', lambda e, sbk=sbk, pk=pk: e.activation(out=pT[pk].rearrange("p a t -> p (a t)"), in_=psum[sbk][:, :], func=AF.Exp),
                             r=[bankres[sbk]], w=[res(f'pT{pk}')])

                    def emit_PV(h, pr, gi):
                        ob, db = (0, 1) if h % 2 == 0 else (2, 3)
                        pk = gi % 3
                        for i2 in range(2):
                            kt = pr * 2 + i2
                            if kt < 16:
                                Vsrc, Vres = Vp[:, kt, h * 128:(h + 1) * 128], [res(f'm_Vp{kt // 8}')]
                            else:
                                kl = kt - 16
                                Vsrc, Vres = Vl[:, kl, h * 128:(h + 1) * 128], [res(f'm_Vl_{kl // 2}_{kl % 2}')]
                            first = (pr == 0 and i2 == 0)
                            last = (pr == npair - 1 and i2 == 1)
                            P.op('pe', lambda e, Vsrc=Vsrc, pk=pk, i2=i2, ob=ob, first=first, last=last: e.matmul(psum[ob][:, 0:256], lhsT=Vsrc, rhs=pT[pk][:, i2, :], start=first, stop=last),
                                 r=Vres + [res(f'pT{pk}')], pw=[bankres[ob]] if first else [], fw=[bankres[ob]] if last else [])
                        for i2 in range(2):
                            first = (pr == 0 and i2 == 0)
                            last = (pr == npair - 1 and i2 == 1)
                            P.op('pe', lambda e, pk=pk, i2=i2, db=db, first=first, last=last: e.matmul(psum[db][:, 0:256], lhsT=o1, rhs=pT[pk][:, i2, :], start=first, stop=last),
                                 r=[o1r, res(f'pT{pk}')], pw=[bankres[db]] if first else [], fw=[bankres[db]] if last else [])

                    seq = [(h, pr) for h in range(4) for pr in range(npair)]
                    emit_S(seq[0][0], seq[0][1], 0)
                    emit_S(seq[1][0], seq[1][1], 1)
                    for gi, (h, pr) in enumerate(seq):
                        if gi + 2 < len(seq):
                            emit_S(seq[gi + 2][0], seq[gi + 2][1], gi + 2)
                        emit_PV(h, pr, gi)
                        if pr == npair - 1:
                            ob, db = (0, 1) if h % 2 == 0 else (2, 3)
                            run_units((4, 2, 4, 4)[h])
                            rD = trs
                            P.op('dve', lambda e, db=db, rD=rD: e.reciprocal(out=rD, in_=psum[db][:, 0:256]), r=[bankres[db]], w=[res('hn_rs')])
                            P.op('dve', lambda e, ob=ob, rD=rD, h=h: e.tensor_tensor(out=yc[:, 8 + h, :], in0=psum[ob][:, 0:256], in1=rD, op=ALU.mult),
                                 r=[bankres[ob], res('hn_rs')], w=[ryc[8 + h]])
                    run_units(len(units))

                    for g in range(4):
                        bk = take_banks(1)[0]
                        P.op('pe', lambda e, g=g, bk=bk: e.matmul(psum[bk][:, 0:256], lhsT=poolw_b[:, l * 4 + g, :], rhs=ddb[:, g, :], start=True, stop=True),
                             r=[rdd[g], res('poolw_b')], w=[bankres[bk]])
                        P.op('act', lambda e, g=g, bk=bk: e.activation(out=yc[:, g, :], in_=psum[bk][:, 0:256], func=AF.Copy, scale=pp[:, PP_POOLS + l * 4 + g:PP_POOLS + l * 4 + g + 1]),
                             r=[bankres[bk], r_pp], w=[ryc[g]])
                    mb_ = take_banks(2)
                    for h in range(4):
                        b = mb_[h // 2]
                        for s_ in range(2):
                            P.op('pe', lambda e, h=h, s_=s_, b=b: e.matmul(psum[b][:, (h % 2) * 256 + s_ * 128:(h % 2) * 256 + (s_ + 1) * 128], lhsT=vh[:, s_, h * 128:(h + 1) * 128], rhs=sguw_b[:, l * 4 + h, :], start=True, stop=True),
                                 r=[res(f'vh{s_}'), rG[0], res('sguw_b')], pw=[bankres[b]] if (h % 2 == 0 and s_ == 0) else [], fw=[bankres[b]] if (h % 2 == 1 and s_ == 1) else [])
                    for h in range(4):
                        b = mb_[h // 2]
                        tmpm = G[1][:, (h % 2) * 256:(h % 2) * 256 + 256]
                        sgb = bc[:, BC_SGUB + (l * 4 + h) * 128:BC_SGUB + (l * 4 + h + 1) * 128]
                        P.op('dve', lambda e, h=h, b=b, tmpm=tmpm, sgb=sgb: e.tensor_tensor(out=tmpm.rearrange("p (s t) -> p s t", s=2), in0=psum[b][:, (h % 2) * 256:(h % 2 + 1) * 256].rearrange("p (s t) -> p s t", s=2),
                                                                                          in1=sgb.unsqueeze(1).to_broadcast([128, 2, 128]), op=ALU.add),
                             r=[bankres[b], r_bc], w=[rG[1]])
                        P.op('dve', lambda e, h=h, tmpm=tmpm: e.tensor_tensor(out=yc[:, 4 + h, :], in0=yc[:, 4 + h, :], in1=tmpm, op=ALU.mult),
                             r=[rG[1], ryc[4 + h]], w=[ryc[4 + h]])

                    if DBG_YCAT:
                        P.op('dve', lambda e: e.tensor_copy(out=xt, in_=yc), r=ryc + [r_xt], w=[r_xt])
                        slabs = [xr(dstname, dc, j) for dc in range(16)]
                        P.op('sp', lambda e, j=j: e.dma_start(out=dst[:, :, j * TM:(j + 1) * TM].rearrange("c p t -> p c t"), in_=xt),
                             r=[r_xt], w=slabs + [r_xt], dsem=xt_sem)
                        continue
                    gb_ = take_banks(2)
                    for m in range(4):
                        b = gb_[m // 2]
                        sqg = G[m][:, 0:512].bitcast(BF16).rearrange("p (c t) -> p c t", c=4)
                        P.op('act', lambda e, m=m, sqg=sqg: e.activation(out=sqg, in_=yc[:, m * 4:(m + 1) * 4, :], func=AF.Square), r=ryc[m * 4:(m + 1) * 4], w=[rG[m]])
                        for c4 in range(4):
                            P.op('pe', lambda e, m=m, c4=c4, b=b, sqg=sqg: e.matmul(psum[b][:, (m % 2) * 256:(m % 2 + 1) * 256], lhsT=o512, rhs=sqg[:, c4, :], start=(c4 == 0), stop=(c4 == 3)),
                                 r=[rG[m], o512r], pw=[bankres[b]] if (m % 2 == 0 and c4 == 0) else [], fw=[bankres[b]] if (m % 2 == 1 and c4 == 3) else [])
                    for m2 in range(2):
                        rsqrt(rs4[:, m2 * 2:m2 * 2 + 2, :].rearrange("p a t -> p (a t)"), psum[gb_[m2]][:, :], [bankres[gb_[m2]]], res(f'rs4_{m2}'))
                    for dc in range(16):
                        eng = 'dve'
                        og = PP_OUTG + l * 16 + dc
                        P.op(eng, lambda e, dc=dc, og=og: e.scalar_tensor_tensor(out=xn[:, dc, :], in0=yc[:, dc, :], scalar=pp[:, og:og + 1], in1=rs4[:, dc // 4, :], op0=ALU.mult, op1=ALU.mult),
                             r=[ryc[dc], res(f'rs4_{dc // 8}'), r_pp], w=[rxn[dc]])
                    if j + 1 < NTM and not DBG_YCAT:
                        modnorm_A()
                    for grp in range(8):
                        kb = grp % 2
                        P.op('sp', lambda e, grp=grp, kb=kb, j=j: e.dma_start(out=xcb[kb], in_=src[2 * grp:2 * grp + 2, :, j * TM:(j + 1) * TM].rearrange("c p t -> p c t")),
                             r=[xr(srcname, 2 * grp + c_, j) for c_ in range(2)], w=[rxcb[kb]], dsem=xcb_sem[kb])
                        wt, wr = wget2(wout_d[l, grp], 4096)
                        wv = wt.rearrange("p (c k n) -> p c k n", c=2, k=16)
                        bk = take_banks(1)[0]
                        for c2 in range(2):
                            for dc in range(16):
                                P.op('pe', lambda e, c2=c2, dc=dc, wv=wv, bk=bk: e.matmul(psum[bk][:, c2 * 256:(c2 + 1) * 256], lhsT=wv[:, c2, dc, :], rhs=xn[:, dc, :], start=(dc == 0), stop=(dc == 15)),
                                     r=[wr, rxn[dc]], pw=[bankres[bk]] if (c2 == 0 and dc == 0) else [], fw=[bankres[bk]] if (c2 == 1 and dc == 15) else [])
                        for c2 in range(2):
                            dt = grp * 2 + c2
                            P.op('dve', lambda e, c2=c2, dt=dt, bk=bk, kb=kb: e.scalar_tensor_tensor(out=xcb[kb][:, c2, :], in0=psum[bk][:, c2 * 256:(c2 + 1) * 256], scalar=derG(l, 1, dt), in1=xcb[kb][:, c2, :], op0=ALU.mult, op1=ALU.add),
                                 r=[bankres[bk], rxcb[kb]] + dres, fw=[rxcb[kb]] if c2 == 1 else [])
                        P.op('sp', lambda e, grp=grp, kb=kb, j=j: e.dma_start(out=dst[2 * grp:2 * grp + 2, :, j * TM:(j + 1) * TM].rearrange("c p t -> p c t"), in_=xcb[kb]),
                             r=[rxcb[kb]], w=[xr(dstname, 2 * grp + c_, j) for c_ in range(2)], dsem=xcb_sem[kb])

            nsub = min(stage, 3 * L)
            barrier()
            k = 0
            cur_src, cur_name = xT_in, 'in'
            for grp in range(24):
                ada_group(0, grp)
            ada_derive(0, 0)
            for l in range(L):
                for i in range(3):
                    if k >= nsub:
                        break
                    last = (k == nsub - 1) and not DBG_YCAT
                    dst, dname = (yT_out, 'out') if last else (xs, 'xs')
                    if i == 1:
                        barrier()
                        mixer(l, cur_src, cur_name, dst, dname)
                        barrier()
                    else:
                        bg = []
                        if l == 0 and i == 0:
                            bg = [lambda grp=grp: ada_group(0, grp) for grp in range(24, 72)] + [lambda ii=ii: ada_derive(0, ii) for ii in (1, 2)]
                        elif i == 2 and l + 1 < L:
                            bg = [lambda grp=grp, l=l: ada_group(l + 1, grp) for grp in range(72)] + [lambda ii=ii, l=l: ada_derive(l + 1, ii) for ii in range(3)]
                        ffn(l, i, l * 2 + (0 if i == 0 else 1), cur_src, cur_name, dst, dname, bg=bg)
                    cur_src, cur_name = dst, dname
                    k += 1
            if DBG_YCAT:
                barrier()
                for dc in range(16):
                    P.op('sp', lambda e, dc=dc: e.dma_start(out=yT_out[dc], in_=xs[dc]), r=[xr('xs', dc, sl) for sl in range(8)], w=[res(f'fin{dc}')], dsem=dsem('fin'))
            barrier()
            return P, wrec

        _, wseq = run(True, [])
        P, _ = run(False, wseq)

        targets = {e: set() for e in ENGS}
        for e in ENGS:
            for (fn, deps, tok, dinc) in P.q[e]:
                for d in deps:
                    if d.sem is None:
                        targets[d.eng].add(d.idx)
        cnt = {}
        for e in ENGS:
            c = 0
            arr = []
            for idx in range(len(P.q[e])):
                if idx in targets[e]:
                    c += 1
                arr.append(c)
            cnt[e] = arr
        for e in ENGS:
            for idx, (fn, deps, tok, dinc) in enumerate(P.q[e]):
                assert not (fn is None and idx in targets[e])

        engobj = {'pe': nc.tensor, 'act': nc.scalar, 'dve': nc.vector, 'pool': nc.gpsimd, 'sp': nc.sync}

        def replay(e, eng):
            seen = {}
            for idx, (fn, deps, tok, dinc) in enumerate(P.q[e]):
                for d in deps:
                    if d.sem is None:
                        if d.eng == e and d.idx >= idx:
                            continue
                        s, v = sem_eng[d.eng], cnt[d.eng][d.idx]
                    else:
                        s, v = d.sem, d.val
                    if seen.get(id(s), 0) >= v:
                        continue
                    seen[id(s)] = v
                    eng.wait_ge(s.h, v)
                if fn is None:
                    continue
                ins = fn(eng)
                if tok.sem is not None:
                    ins.then_inc(tok.sem.h, dinc)
                elif idx in targets[e]:
                    ins.then_inc(sem_eng[e].h, 1)

        with nc.Block() as block:
            @block.tensor
            def _(eng):
                replay('pe', eng)

            @block.scalar
            def _(eng):
                replay('act', eng)

            @block.vector
            def _(eng):
                replay('dve', eng)

            @block.gpsimd
            def _(eng):
                replay('pool', eng)

            @block.sync
            def _(eng):
                replay('sp', eng)
    return nc


def _host_prep(inputs):
    f = np.float32
    g = {k: np.asarray(v, dtype=f) for k, v in inputs.items()}
    sh = {}
    ada = g['ada_w'].reshape(L, 16, 128, 144, 128).transpose(0, 3, 2, 1, 4)
    sh['adaw'] = np.ascontiguousarray(ada.reshape(L, 72, 2, 128, 16, 128).transpose(0, 1, 3, 2, 4, 5)).reshape(L, 72, 128, 4096)
    w13 = []
    w2 = []
    for l in range(L):
        for nm13, nm2 in (('ffn1_w13', 'ffn1_w2'), ('ffn2_w13', 'ffn2_w2')):
            a = g[nm13][l].reshape(16, 128, 2, NFC, 128).transpose(3, 1, 2, 0, 4)
            w13.append(np.ascontiguousarray(a).reshape(NFC, 128, 4096))
            b = g[nm2][l].reshape(NFC, 128, 16, 128).transpose(2, 1, 0, 3)
            w2.append(np.ascontiguousarray(b).reshape(16, 128, DFF))
    sh['w13'] = np.stack(w13)
    sh['w2'] = np.stack(w2)
    wi = g['w_in'].reshape(L, 16, 128, 36, 128).transpose(0, 3, 2, 1, 4)
    sh['winf'] = np.ascontiguousarray(wi.reshape(L, 18, 2, 128, 16, 128).transpose(0, 1, 3, 2, 4, 5)).reshape(L, 18, 128, 4096)
    wt = np.stack([g['w_in'][:, :, 1024:1536], g['w_in'][:, :, 2560:3072]], axis=1)
    wt = wt.reshape(L, 2, 2, 8, 128, 512).transpose(0, 1, 2, 4, 3, 5)
    sh['wint'] = np.ascontiguousarray(wt).reshape(L, 2, 2, 128, 4096)
    wo = g['w_out'].reshape(L, 16, 128, 16, 128).transpose(0, 3, 2, 1, 4)
    sh['wout'] = np.ascontiguousarray(wo.reshape(L, 8, 2, 128, 16, 128).transpose(0, 1, 3, 2, 4, 5)).reshape(L, 8, 128, 4096)
    sh['poolw'] = np.ascontiguousarray(g['pool_w'].transpose(2, 0, 1, 3)).reshape(128, L * 4 * 128)
    sh['sguwT'] = np.ascontiguousarray(g['sgu_w'].transpose(3, 0, 1, 2)).reshape(128, L * 4 * 128)
    cm = np.zeros((128, 128 * 3 + 512 + 2048), f)
    cm[:, 0:128] = np.eye(128, dtype=f)
    ss, tt = np.meshgrid(np.arange(128), np.arange(128), indexing='ij')
    cm[:, 128:256] = (ss <= tt).astype(f)
    kk, qq = np.meshgrid(np.arange(128), np.arange(256), indexing='ij')
    for r in range(2):
        cm[:, 384 + r * 256:384 + (r + 1) * 256] = np.where(r * 128 + kk <= qq, 0.0, -BIG).astype(f)
    oh = np.zeros((16, 16, 128), f)
    for b_ in range(16):
        oh[b_, b_, :] = 1.0
    cm[0:16, 896:2944] = oh.reshape(16, 2048)
    sh['cmat'] = cm
    bcv = np.concatenate([g['sgu_b'].reshape(-1), g['sgu_norm_g'].reshape(-1)])
    sh['bc'] = np.ascontiguousarray(np.broadcast_to(bcv[None, :], (128, NBC)))

    def ppl(v):
        v = np.asarray(v, f)
        lead = v.shape[:-1]
        n = v.shape[-1] // 128
        return v.reshape(*lead, n, 128).transpose(len(lead) + 1, *range(len(lead)), len(lead)).reshape(128, -1)

    in_maps = []
    for c in range(8):
        b, half = c // 2, c % 2
        pp = np.zeros((128, NPP), f)
        pp[:, PP_C:PP_C + 16] = ppl(g['c'][b])
        pp[:, PP_NORMG:PP_NORMG + L * 48] = ppl(g['norm_g'])
        pp[:, PP_POOLS:PP_POOLS + L * 4] = ppl(g['pool_scale'])
        pp[:, PP_QG:PP_QG + L] = g['q_norm_g'].T
        pp[:, PP_KG:PP_KG + L] = g['k_norm_g'].T
        pp[:, PP_CONVW:PP_CONVW + L * 12] = ppl(g['conv_w'])
        pp[:, PP_OUTG:PP_OUTG + L * 16] = ppl(g['out_norm_g'])
        pp[:, PP_HALF] = float(half)
        pp[:, PP_ADAB:PP_ADAB + L * 144] = ppl(g['ada_b'])
        tabs = np.zeros((8, 3, 16), f)
        prevb = 0.0 if half == 1 else -BIG
        for o in range(8):
            tabs[o, 0, 0:8] = prevb
            tabs[o, 0, 8:8 + o] = 0.0
            tabs[o, 0, 8 + o:] = -BIG
            tabs[o, 1, :] = 1.0
            tabs[o, 1, 8 + o] = 0.0
            tabs[o, 2, 0:8] = prevb
            tabs[o, 2, 8:8 + o + 1] = 0.0
            tabs[o, 2, 8 + o + 1:] = -BIG
        pp[:, PP_TABS:PP_TABS + 384] = tabs.reshape(1, -1)
        inv = np.zeros((4, 16), f)
        for gi in range(4):
            wn = 2 << gi
            for t in range(16):
                inv[gi, t] = 1.0 / (wn if half == 1 else min(t + 1, wn))
        pp[:, PP_INVC:PP_INVC + 64] = inv.reshape(1, -1)
        xT = np.ascontiguousarray(g['x'][b, half * TL:(half + 1) * TL, :].T).reshape(NDC, 128, TL)
        m = dict(sh)
        m['pp'] = pp
        m['xT'] = xT
        in_maps.append(m)
    return in_maps


_NC_CACHE = {}


def _run(inputs, stage=6):
    in_maps = _host_prep(inputs)
    if stage not in _NC_CACHE:
        _NC_CACHE[stage] = _mk_program(stage)
    nc = _NC_CACHE[stage]
    res = run_bass_kernel_spmd(nc, in_maps, core_ids=list(range(8)))
    out = np.empty((4, 4096, D), np.float32)
    for c in range(8):
        b, half = c // 2, c % 2
        yT = np.asarray(res.results[c]["yT"]).reshape(D, TL)
        out[b, half * TL:(half + 1) * TL, :] = yT.T
    return out


def kernel(**inputs):
    return _run(inputs, stage=6)
```
